# Optimizing a Trainium2 kernel written in Bass

```python
import jax, jax.numpy as jnp
from jax import lax
import numpy as np


D_MODEL = 1024
BATCH = 2
SEQ = 8192
DEPTH = 4

N_MIXERS = 3
GRID_W = 64
D_FF = -(-8 * D_MODEL // (3 * 256)) * 256
NORM_EPS = 1e-6
NEG_INF = -1e30
QBLK = 128

A_GROUPS = ((128, 1), (512, 4), (2048, 16))
A_NG = len(A_GROUPS)
A_HEAD_DIM = 64
A_HEADS = D_MODEL // A_HEAD_DIM
A_QKV = 3 * A_NG * A_HEADS * A_HEAD_DIM

B_HEAD_DIM = 128
B_HEADS = D_MODEL // B_HEAD_DIM
B_KV_HEADS = 2
B_ROPE_THETA = 10000.0
B_QKV = (B_HEADS + 2 * B_KV_HEADS) * B_HEAD_DIM

C_HEADS = 16
C_Q_LORA = 384
C_KV_LORA = 256
C_NOPE = 64
C_ROPE = 32
C_V = 64
C_ROPE_THETA = 10000.0
C_IN = C_Q_LORA + C_KV_LORA + C_ROPE

N_A = (DEPTH + 2) // 3
N_B = (DEPTH + 1) // 3
N_C = DEPTH // 3

kernel_name = 'hybrid_dilated_axial_mla_encoder'


def rms_norm(x, gain):
    xf = x.astype(jnp.float32)
    y = xf * lax.rsqrt(jnp.mean(xf * xf, axis=-1, keepdims=True) + NORM_EPS)
    return (y * gain.astype(jnp.float32)).astype(x.dtype)


def rope(x, pos, theta):
    dim = x.shape[-1]
    half = dim // 2
    freqs = jnp.power(jnp.float32(theta), -jnp.arange(half, dtype=jnp.float32) * 2.0 / dim)
    ang = pos[:, None] * freqs[None, :]
    cos = jnp.cos(ang)[:, None, :]
    sin = jnp.sin(ang)[:, None, :]
    x1 = x[..., :half].astype(jnp.float32)
    x2 = x[..., half:].astype(jnp.float32)
    return jnp.concatenate([x1 * cos - x2 * sin, x1 * sin + x2 * cos], axis=-1).astype(x.dtype)


def axial_rope(x, row, col, theta):
    half = x.shape[-1] // 2
    return jnp.concatenate([rope(x[..., :half], row, theta), rope(x[..., half:], col, theta)], axis=-1)


def alibi_slopes(n):
    return jnp.asarray(2.0 ** (-8.0 * np.arange(1, n + 1) / n), dtype=jnp.float32)


def dense_block_attention(q, k, v):
    B, S, KVH, G, dq = q.shape
    nblk = S // QBLK
    qb = q.reshape(B, nblk, QBLK, KVH, G, dq).transpose(1, 0, 2, 3, 4, 5)
    scale = dq ** -0.5

    def attend(qi):
        s = jnp.einsum('bqkgd,bskd->bkgqs', qi, k).astype(jnp.float32) * scale
        p = jax.nn.softmax(s, axis=-1)
        return jnp.einsum('bkgqs,bske->bqkge', p.astype(v.dtype), v)

    o = lax.map(attend, qb)
    return o.transpose(1, 0, 2, 3, 4, 5).reshape(B, S, -1)


def dilated_window_attention(q, k, v, window, dilation, slopes):
    B, S, H, dh = q.shape
    d = dilation
    radius = window // (2 * dilation)
    wb = radius
    L = S // d
    nb = -(-L // wb)
    Lp = nb * wb

    def by_residue(a):
        return a.reshape(B, L, d, H, dh).transpose(0, 2, 1, 3, 4)

    qb = jnp.pad(by_residue(q), ((0, 0), (0, 0), (0, Lp - L), (0, 0), (0, 0))).reshape(B, d, nb, wb, H, dh)

    def key_blocks(a):
        ap = jnp.pad(by_residue(a), ((0, 0), (0, 0), (wb, Lp - L + wb), (0, 0), (0, 0)))
        ap = ap.reshape(B, d, nb + 2, wb, H, dh)
        return jnp.concatenate([ap[:, :, :-2], ap[:, :, 1:-1], ap[:, :, 2:]], axis=3)

    kb = key_blocks(k)
    vb = key_blocks(v)
    s = jnp.einsum('bdnqhe,bdnche->bdnhqc', qb, kb).astype(jnp.float32) * (dh ** -0.5)
    a_idx = jnp.arange(wb)
    c_idx = jnp.arange(3 * wb)
    diff = c_idx[None, :] - wb - a_idx[:, None]
    j_idx = (jnp.arange(nb) * wb)[:, None] - wb + c_idx[None, :]
    valid = (jnp.abs(diff) <= radius)[None] & ((j_idx >= 0) & (j_idx < L))[:, None, :]
    dist = (jnp.abs(diff) * d).astype(jnp.float32)
    bias = -slopes[:, None, None] * dist[None]
    s = s + bias[None, None, None]
    s = jnp.where(valid[None, None, :, None], s, NEG_INF)
    lse = jax.nn.logsumexp(s, axis=-1)
    p = jnp.exp(s - lse[..., None])
    o = jnp.einsum('bdnhqc,bdnche->bdnqhe', p.astype(vb.dtype), vb)
    o = o.reshape(B, d, Lp, H, dh)[:, :, :L].transpose(0, 2, 1, 3, 4).reshape(B, S, H, dh)
    lse = lse.transpose(0, 1, 2, 4, 3).reshape(B, d, Lp, H)[:, :, :L].transpose(0, 2, 1, 3).reshape(B, S, H)
    return o, lse


def mixer_a(x, wqkv, wo):
    B, S, _ = x.shape
    qkv = (x @ wqkv).reshape(B, S, 3, A_NG, A_HEADS, A_HEAD_DIM)
    slopes = alibi_slopes(A_NG * A_HEADS).reshape(A_NG, A_HEADS)
    outs = []
    lses = []
    for g, (window, dilation) in enumerate(A_GROUPS):
        o, l = dilated_window_attention(qkv[:, :, 0, g], qkv[:, :, 1, g], qkv[:, :, 2, g], window, dilation, slopes[g])
        outs.append(o)
        lses.append(l)
    wgt = jax.nn.softmax(jnp.stack(lses, axis=0), axis=0)
    o = jnp.sum(wgt[..., None] * jnp.stack(outs, axis=0).astype(jnp.float32), axis=0)
    return o.reshape(B, S, -1).astype(x.dtype) @ wo


def mixer_b(x, wqkv, qnorm, knorm, wo):
    B, S, _ = x.shape
    qkv = x @ wqkv
    nq = B_HEADS * B_HEAD_DIM
    nk = B_KV_HEADS * B_HEAD_DIM
    q = qkv[..., :nq].reshape(B, S, B_HEADS, B_HEAD_DIM)
    k = qkv[..., nq:nq + nk].reshape(B, S, B_KV_HEADS, B_HEAD_DIM)
    v = qkv[..., nq + nk:].reshape(B, S, B_KV_HEADS, B_HEAD_DIM)
    q = rms_norm(q, qnorm)
    k = rms_norm(k, knorm)
    rows = S // GRID_W
    row = jnp.repeat(jnp.arange(rows, dtype=jnp.float32), GRID_W)
    col = jnp.tile(jnp.arange(GRID_W, dtype=jnp.float32), rows)
    q = axial_rope(q, row, col, B_ROPE_THETA)
    k = axial_rope(k, row, col, B_ROPE_THETA)
    q = q.reshape(B, S, B_KV_HEADS, B_HEADS // B_KV_HEADS, B_HEAD_DIM)
    return dense_block_attention(q, k, v) @ wo


def mixer_c(x, win, qnorm, kvnorm, wuq, wukv, wo):
    B, S, _ = x.shape
    c = x @ win
    cq = c[..., :C_Q_LORA]
    ckv = c[..., C_Q_LORA:C_Q_LORA + C_KV_LORA]
    kr = c[..., C_Q_LORA + C_KV_LORA:]
    q = (rms_norm(cq, qnorm) @ wuq).reshape(B, S, C_HEADS, C_NOPE + C_ROPE)
    kv = (rms_norm(ckv, kvnorm) @ wukv).reshape(B, S, C_HEADS, C_NOPE + C_V)
    pos = jnp.arange(S, dtype=jnp.float32)
    q = jnp.concatenate([q[..., :C_NOPE], rope(q[..., C_NOPE:], pos, C_ROPE_THETA)], axis=-1)
    kr = rope(kr[:, :, None, :], pos, C_ROPE_THETA)
    k = jnp.concatenate([kv[..., :C_NOPE], jnp.broadcast_to(kr, (B, S, C_HEADS, C_ROPE))], axis=-1)
    v = kv[..., C_NOPE:]
    return dense_block_attention(q[:, :, :, None, :], k, v) @ wo


def swiglu(x, wg, wu, wd):
    return (jax.nn.silu(x @ wg) * (x @ wu)) @ wd


def setup_inputs(seed: int = 0) -> dict:
    key = jax.random.key(seed)
    ks = jax.random.split(key, 20)
    f32 = jnp.float32

    def w(k, shape, fan_in):
        return jax.random.normal(k, shape, f32) * (fan_in ** -0.5)

    def g(k, shape):
        return 1.0 + 0.05 * jax.random.normal(k, shape, f32)

    return {
        'x': jax.random.normal(ks[0], (BATCH, SEQ, D_MODEL), f32),
        'norm_mix_pre': g(ks[1], (DEPTH, D_MODEL)),
        'norm_mix_post': g(ks[2], (DEPTH, D_MODEL)),
        'norm_ffn_pre': g(ks[3], (DEPTH, D_MODEL)),
        'norm_ffn_post': g(ks[4], (DEPTH, D_MODEL)),
        'ffn_wg': w(ks[5], (DEPTH, D_MODEL, D_FF), D_MODEL),
        'ffn_wu': w(ks[6], (DEPTH, D_MODEL, D_FF), D_MODEL),
        'ffn_wd': w(ks[7], (DEPTH, D_FF, D_MODEL), D_FF),
        'a_wqkv': w(ks[8], (N_A, D_MODEL, A_QKV), D_MODEL),
        'a_wo': w(ks[9], (N_A, A_HEADS * A_HEAD_DIM, D_MODEL), A_HEADS * A_HEAD_DIM),
        'b_wqkv': w(ks[10], (N_B, D_MODEL, B_QKV), D_MODEL),
        'b_qnorm': g(ks[11], (N_B, B_HEAD_DIM)),
        'b_knorm': g(ks[12], (N_B, B_HEAD_DIM)),
        'b_wo': w(ks[13], (N_B, B_HEADS * B_HEAD_DIM, D_MODEL), B_HEADS * B_HEAD_DIM),
        'c_win': w(ks[14], (N_C, D_MODEL, C_IN), D_MODEL),
        'c_qnorm': g(ks[15], (N_C, C_Q_LORA)),
        'c_kvnorm': g(ks[16], (N_C, C_KV_LORA)),
        'c_wuq': w(ks[17], (N_C, C_Q_LORA, C_HEADS * (C_NOPE + C_ROPE)), C_Q_LORA),
        'c_wukv': w(ks[18], (N_C, C_KV_LORA, C_HEADS * (C_NOPE + C_V)), C_KV_LORA),
        'c_wo': w(ks[19], (N_C, C_HEADS * C_V, D_MODEL), C_HEADS * C_V),
    }


def reference(x, norm_mix_pre, norm_mix_post, norm_ffn_pre, norm_ffn_post, ffn_wg, ffn_wu, ffn_wd,
              a_wqkv, a_wo, b_wqkv, b_qnorm, b_knorm, b_wo,
              c_win, c_qnorm, c_kvnorm, c_wuq, c_wukv, c_wo):
    h = x
    for i in range(DEPTH):
        kind = i % N_MIXERS
        j = i // N_MIXERS
        y = rms_norm(h, norm_mix_pre[i])
        if kind == 0:
            y = mixer_a(y, a_wqkv[j], a_wo[j])
        elif kind == 1:
            y = mixer_b(y, b_wqkv[j], b_qnorm[j], b_knorm[j], b_wo[j])
        else:
            y = mixer_c(y, c_win[j], c_qnorm[j], c_kvnorm[j], c_wuq[j], c_wukv[j], c_wo[j])
        h = h + rms_norm(y, norm_mix_post[i])
        y = swiglu(rms_norm(h, norm_ffn_pre[i]), ffn_wg[i], ffn_wu[i], ffn_wd[i])
        h = h + rms_norm(y, norm_ffn_post[i])
    return h
```

```python
import contextlib
import os
KSTOP = os.environ.get('KSTOP', '')
import numpy as np
import concourse.bass as bass
import concourse.mybir as mybir
from concourse.bass_utils import run_bass_kernel_spmd

F32 = mybir.dt.float32
BF16 = mybir.dt.bfloat16
AF = mybir.ActivationFunctionType
ALU = mybir.AluOpType

NCORES = 8
D = 1024
KC = 8
T = 2048
S = 8192
NB = 4
BW = 512
DFF = 2816
JC = 22
EPS = 1e-6
DEPTH = 4
SB_BASE = 18560
SB_END = 229376
A_GROUPS = ((128, 1), (512, 4), (2048, 16))


class Prog:
    ENG = ("pe", "act", "dve", "pool", "sp")
    MAXC = 20000

    def __init__(self, nc):
        self.nc = nc
        self.q = {e: [] for e in self.ENG}
        self.tick = {}
        self.nsem = 0
        self.lastw = {}
        self.readers = {}
        self.seen = {e: {} for e in self.ENG}
        self.dma_slots = {}
        self.dma_rr = {}
        self.last_ticket = {}
        self.sems = {}

    def _newsem(self, name):
        self.nsem += 1
        h = self.nc.alloc_semaphore(f"{name}_{self.nsem}")
        self.sems[id(h)] = h
        return h

    def _compute_ticket(self, eng):
        st = self.tick.get(eng)
        if st is None or st[1] >= self.MAXC:
            st = [self._newsem("t" + eng), 0]
            self.tick[eng] = st
        st[1] += 1
        return (id(st[0]), st[1], eng), (st[0], 1)

    def _dma_ticket(self, eng, nslots=12):
        slots = self.dma_slots.setdefault(eng, [])
        if len(slots) < nslots:
            slots.append([self._newsem("d" + eng), 0])
            i = len(slots) - 1
        else:
            i = self.dma_rr.get(eng, 0) % nslots
            self.dma_rr[eng] = i + 1
        st = slots[i]
        prev = (id(st[0]), st[1], "dma") if st[1] > 0 else None
        st[1] += 16
        return (id(st[0]), st[1], "dma"), (st[0], 16), prev

    def add(self, eng, fns, reads=(), writes=(), dma=False, waits=(), cc=False):
        if callable(fns):
            fns = [fns]
        tks = set(waits)
        for k in reads:
            t = self.lastw.get(k)
            if t is not None:
                tks.add(t)
        for k in writes:
            t = self.lastw.get(k)
            if t is not None:
                tks.add(t)
            for t in self.readers.get(k, ()):
                tks.add(t)
        if dma:
            ticket, inc, prev = self._dma_ticket(eng)
            if prev is not None:
                tks.add(prev)
        elif cc:
            st = [self._newsem("cc"), 1]
            ticket, inc = (id(st[0]), st[1], "dma"), (st[0], None)
        else:
            ticket, inc = self._compute_ticket(eng)
        need = {}
        for (sid, val, prod) in tks:
            if prod == "pe" and eng == "pe":
                continue
            if self.seen[eng].get(sid, 0) >= val:
                continue
            if need.get(sid, 0) < val:
                need[sid] = val
        for sid, val in need.items():
            self.seen[eng][sid] = val
        if not (dma or cc):
            pass
        self.q[eng].append(([(self.sems[sid], val) for sid, val in need.items()], fns, inc))
        for k in reads:
            self.readers.setdefault(k, []).append(ticket)
        for k in writes:
            self.lastw[k] = ticket
            self.readers[k] = []
        self.last_ticket[eng if not (dma or cc) else ("dma", ticket[0])] = ticket
        return ticket

    def barrier(self):
        tks = list(self.last_ticket.values())
        for eng in self.ENG:
            need = {}
            for (sid, val, prod) in tks:
                if self.seen[eng].get(sid, 0) >= val:
                    continue
                if need.get(sid, 0) < val:
                    need[sid] = val
            for sid, val in need.items():
                self.seen[eng][sid] = val
            if need:
                self.q[eng].append(([(self.sems[sid], val) for sid, val in need.items()], [], None))
        self.lastw = {}
        self.readers = {}

    def replay(self, block):
        def run(eng):
            def body(e):
                for waits, fns, inc in self.q[eng]:
                    for sem, val in waits:
                        e.wait_ge(sem, val)
                    ins = None
                    for fn in fns:
                        ins = fn(e)
                    if inc is not None and ins is not None:
                        if inc[1] is None:
                            ins.then_inc(inc[0])
                        else:
                            ins.then_inc(inc[0], inc[1])
            return body
        block.tensor(run("pe"))
        block.scalar(run("act"))
        block.vector(run("dve"))
        block.gpsimd(run("pool"))
        block.sync(run("sp"))


class SB:
    def __init__(self, nc):
        self.nc = nc
        self.n = 0

    def at(self, off, shape, dtype):
        self.n += 1
        esz = 4 if dtype == F32 else 2
        nbytes = esz * int(np.prod(shape[1:]))
        assert off % 32 == 0, off
        assert SB_BASE <= off and off + nbytes <= SB_END, (off, nbytes, shape)
        return self.nc.alloc_sbuf_tensor_at(f"sb{self.n}", list(shape), dtype, offset=off)


def alibi_slope(g, h):
    return float(2.0 ** (-8.0 * (g * 16 + h + 1) / 48.0))


def build_program(layers=(0, 1, 2, 3), skip_mixer=False, skip_ffn=False):
    nc = bass.Bass("TRN2", target_bir_lowering=False)
    P = Prog(nc)
    sb = SB(nc)

    used_inputs = []

    class _Lazy:
        def __init__(self, name, shape):
            self.name, self.shape, self.ap_ = name, shape, None

        def get(self):
            if self.ap_ is None:
                self.ap_ = nc.dram_tensor(self.name, list(self.shape), F32, kind="ExternalInput").ap()
                used_inputs.append(self.name)
            return self.ap_

        def __getitem__(self, idx):
            return self.get()[idx]

        def rearrange(self, *a, **k):
            return self.get().rearrange(*a, **k)

    def din(name, shape, dt=F32):
        return _Lazy(name, shape)

    xT = din("xT", [D, T])
    gains_d = din("gains", [128, 128])
    consts_d = din("consts", [128, 1024])
    ropeB_d = din("ropeB", [128, 2 * T])
    ropeC_d = din("ropeC", [128, 4 * T])
    validA_d = din("validA", [128, 72])
    bgains_d = din("bgains", [128, 8])
    cgains_d = din("cgains", [128, 8])
    wg_d = din("wg", [DEPTH * JC * 128, 1024])
    wu_d = din("wu", [DEPTH * JC * 128, 1024])
    wd_d = din("wd", [DEPTH * 2 * 8 * 128, 11 * 128])
    awqkv_d = din("a_wqkv", [2 * 3 * 3 * 8 * 128, 1024])
    awo_d = din("a_wo", [2 * D, D])
    bwqkv_d = din("b_wqkv", [D, 1536])
    bwo_d = din("b_wo", [D, D])
    cwin_d = din("c_win", [D, 672])
    cwuq_d = din("c_wuq", [384, 1536])
    cwukv_d = din("c_wukv", [256, 2048])
    cwo_d = din("c_wo", [D, D])
    out_d = nc.dram_tensor("out", [D, T], F32, kind="ExternalOutput").ap()
    hbuf = nc.dram_tensor("hbuf", [D, T], F32).ap()
    payA4 = [nc.dram_tensor(f"payA{i}", [D, BW], BF16) for i in range(4)]
    gA4 = [nc.dram_tensor(f"gA{i}", [4 * D, BW], BF16) for i in range(4)]
    payBK = nc.dram_tensor("payBK", [256, T], BF16)
    gBK = nc.dram_tensor("gBK", [4 * 256, T], BF16)
    payBV = nc.dram_tensor("payBV", [256, T], BF16)
    gBV = nc.dram_tensor("gBV", [4 * 256, T], BF16)
    RG = [[0, 1, 2, 3], [4, 5, 6, 7]]
    sv = {}

    o = SB_BASE
    R_Y = o; o += 32768
    R_AO = o; o += 32768
    R_BIG = o; o += 114688
    R_W = o; o += 17408
    R_M = o
    assert SB_END - R_M >= 13000, SB_END - R_M

    yT = sb.at(R_Y, [128, KC, T], BF16)
    aoT = sb.at(R_AO, [128, KC, T], BF16)
    m = R_M
    gains = sb.at(m, [128, 128], F32); m += 512
    ones_bf = sb.at(m, [128, 128], BF16); m += 256
    ones_f = sb.at(m, [128, 64], F32); m += 256
    rstd = [sb.at(m + i * 2048, [128, BW], F32) for i in range(2)]; m += 4096
    sdt = [sb.at(m + i * 2048, [128, BW], F32) for i in range(2)]; m += 4096
    bgains = sb.at(m, [128, 8], F32); m += 32
    cgains = sb.at(m, [128, 8], F32); m += 32
    validA = sb.at(m, [128, 72], F32); m += 288
    rh_ = sb.at(m, [128, BW], BF16); m += 1024
    rl_ = sb.at(m, [128, BW], BF16); m += 1024
    assert m <= SB_END, m

    def bcast64_a(row_ap, row_key):
        P.add("dve", lambda e: e.tensor_copy(out=rh_[64:65, :], in_=row_ap), reads=[row_key], writes=["rh"])
        P.add("dve", lambda e: e.tensor_tensor(out=rl_[64:65, :], in0=row_ap, in1=rh_[64:65, :], op=ALU.subtract),
              reads=[row_key, "rh"], writes=["rl"])

    def bcast64_b(zb):
        mm_group(zb, ps[zb][0:64, :], [(ones_bf[64:65, 0:64], rh_[64:65, :]), (ones_bf[64:65, 0:64], rl_[64:65, :])],
                 reads=["rh", "rl", "ones_bf"])

    ps = [nc.alloc_psum_tensor(f"ps{i}", [128, BW], F32) for i in range(8)]
    bank_rr = [0]

    def nbank(lo=0, hi=8):
        b = lo + bank_rr[0] % (hi - lo)
        bank_rr[0] += 1
        return b

    def mm_group(bank, out_ap, pairs, reads, extra_writes=()):
        n = len(pairs)
        fns = []
        for i, (l, r) in enumerate(pairs):
            fns.append(lambda e, l=l, r=r, i=i: e.matmul(out_ap, l, r, start=(i == 0), stop=(i == n - 1)))
        return P.add("pe", fns, reads=reads, writes=[("ps", bank)] + list(extra_writes))

    def gcol(kind, layer, k):
        c = (kind * 4 + layer) * 8 + k
        return gains[:, c:c + 1]

    P.add("sp", lambda e: e.dma_start(out=gains[:], in_=gains_d[:, :]), writes=["gains"], dma=True)
    P.add("sp", lambda e: e.dma_start(out=bgains[:], in_=bgains_d[:, :]), writes=["bgains"], dma=True)
    P.add("sp", lambda e: e.dma_start(out=cgains[:], in_=cgains_d[:, :]), writes=["cgains"], dma=True)
    P.add("sp", lambda e: e.dma_start(out=validA[:], in_=validA_d[:, :]), writes=["validA"], dma=True)
    P.add("dve", lambda e: e.memset(ones_bf[:], 1.0), writes=["ones_bf"])
    P.add("dve", lambda e: e.memset(ones_f[:], 1.0), writes=["ones_f"])

    xT3 = xT.rearrange("(k p) t -> p k t", p=128)
    hb3 = hbuf.rearrange("(k p) t -> p k t", p=128)
    out3 = out_d.rearrange("(k p) t -> p k t", p=128)

    stat_rr = [0]

    def rms_rstd(sq_aps, sq_keys, nfeat, npart=128):
        slot = stat_rr[0] % 2
        stat_rr[0] += 1
        bank = nbank()
        mm_group(bank, ps[bank][0:npart, :], [(ones_bf[:, 0:npart], a) for a in sq_aps],
                 reads=list(sq_keys) + ["ones_bf"])
        P.add("act", lambda e: e.activation(out=sdt[slot][0:npart, :], in_=ps[bank][0:npart, :], func=AF.Ln,
                                            bias=EPS, scale=1.0 / nfeat),
              reads=[("ps", bank)], writes=[("sd", slot)])
        P.add("act", lambda e: e.activation(out=rstd[slot][0:npart, :], in_=sdt[slot][0:npart, :], func=AF.Exp, scale=-0.5),
              reads=[("sd", slot)], writes=[("rstd", slot)])
        return slot

    def prologue_only(blk, hblk, sqb, layer, kind_pre, hsrc3, first_store):
        bs = slice(blk * BW, (blk + 1) * BW)
        P.add("sp", lambda e: e.dma_start(out=hblk[:], in_=hsrc3[:, :, bs]), reads=[("hb", blk)],
              writes=[("hblk", id(hblk)), ("hblkA", id(hblk)), ("hblkB", id(hblk))], dma=True)
        if first_store:
            P.add("sp", lambda e: e.dma_start(out=hb3[:, :, bs], in_=hblk[:]), reads=[("hblk", id(hblk)), ("hblkA", id(hblk)), ("hblkB", id(hblk))],
                  writes=[("hb", blk)], dma=True)
        pre_norm(blk, hblk, sqb, layer, kind_pre)

    def pre_norm(blk, hblk, sqb, layer, kind_pre):
        bs = slice(blk * BW, (blk + 1) * BW)
        P.add("act", lambda e: e.activation(out=sqb[:], in_=hblk[:], func=AF.Square),
              reads=[("hblk", id(hblk)), ("hblkA", id(hblk)), ("hblkB", id(hblk))], writes=[("sq", id(sqb))])
        slot = rms_rstd([sqb[:, k, :] for k in range(KC)], [("sq", id(sqb))], D)
        for k in range(KC):
            P.add("dve", lambda e, k=k: e.scalar_tensor_tensor(
                out=yT[:, k, bs], in0=hblk[:, k, :], scalar=gcol(kind_pre, layer, k), in1=rstd[slot][:],
                op0=ALU.mult, op1=ALU.mult),
                reads=[("hblk", id(hblk)), ("hblkA", id(hblk)), ("hblkB", id(hblk)), ("rstd", slot), "gains"], writes=[("yT", blk, k)])
        if kind_pre == 0 and layer % 3 == 0:
            P.add("sp", lambda e: e.dma_start(out=payA4[blk].ap().rearrange("(k p) t -> p k t", p=128), in_=yT[:, :, bs]),
                  reads=[("yT", blk, k) for k in range(KC)], writes=[("payA", blk)], dma=True)
            P.add("pool", lambda e: e.collective_compute("AllGather", ALU.bypass, replica_groups=RG,
                                                         ins=[payA4[blk].ap().opt()], outs=[gA4[blk].ap().opt()]),
                  reads=[("payA", blk)], writes=[("gA", blk)], cc=True)

    def epilogue(blk, osb, sqo, hblk, sqb, layer, kind_post, next_pre, final):
        bs = slice(blk * BW, (blk + 1) * BW)
        P.add("sp", lambda e: e.dma_start(out=hblk[:], in_=hb3[:, :, bs]), reads=[("hb", blk)],
              writes=[("hblk", id(hblk)), ("hblkA", id(hblk)), ("hblkB", id(hblk))], dma=True)
        slot = rms_rstd([sqo[:, k, :] for k in range(KC)], [("sqo", id(sqo))], D)
        for k in range(KC):
            P.add("dve", lambda e, k=k: e.scalar_tensor_tensor(
                out=osb[:, k, :], in0=osb[:, k, :], scalar=gcol(kind_post, layer, k), in1=rstd[slot][:],
                op0=ALU.mult, op1=ALU.mult),
                reads=[("rstd", slot), "gains"], writes=[("osb", id(osb))])
        P.add("pool", lambda e: e.tensor_tensor(out=hblk[:, 0:2, :], in0=hblk[:, 0:2, :], in1=osb[:, 0:2, :], op=ALU.add),
              reads=[("osb", id(osb)), ("hblk", id(hblk))], writes=[("hblkA", id(hblk))])
        P.add("dve", lambda e: e.tensor_tensor(out=hblk[:, 2:8, :], in0=hblk[:, 2:8, :], in1=osb[:, 2:8, :], op=ALU.add),
              reads=[("osb", id(osb)), ("hblk", id(hblk))], writes=[("hblkB", id(hblk))])
        dst = out3 if final else hb3
        P.add("sp", lambda e: e.dma_start(out=dst[:, :, bs], in_=hblk[:]), reads=[("hblk", id(hblk)), ("hblkA", id(hblk)), ("hblkB", id(hblk))],
              writes=[("hb", blk)] if not final else [("outd", blk)], dma=True)
        if next_pre is not None:
            pre_norm(blk, hblk, sqb, next_pre[0], next_pre[1])

    def out_proj_epilogue(w_sb, layer, kind_post, next_pre, final, scr):
        osbs, sqos, hblks, sqbs = scr

        def wo_blk(blk):
            bs = slice(blk * BW, (blk + 1) * BW)
            osb, sqo = osbs[blk % 2], sqos[blk % 2]
            for n in range(KC):
                bank = nbank()
                mm_group(bank, ps[bank][:], [(w_sb[:, k, n * 128:(n + 1) * 128], aoT[:, k, bs]) for k in range(KC)],
                         reads=[("ao", blk, k) for k in range(KC)] + ["wo"])
                P.add("act", lambda e, n=n, bank=bank, osb=osb: e.activation(out=osb[:, n, :], in_=ps[bank][:], func=AF.Copy),
                      reads=[("ps", bank)], writes=[("osb", id(osb))])
                P.add("act", lambda e, n=n, bank=bank, sqo=sqo: e.activation(out=sqo[:, n, :], in_=ps[bank][:], func=AF.Square),
                      reads=[("ps", bank)], writes=[("sqo", id(sqo))])

        wo_blk(0)
        for blk in range(NB):
            if blk + 1 < NB:
                wo_blk(blk + 1)
            epilogue(blk, osbs[blk % 2], sqos[blk % 2], hblks[blk % 2], sqbs[blk % 2], layer, kind_post, next_pre, final)

    def epi_scratch(base):
        o = base
        osbs = [sb.at(o + i * 16384, [128, KC, BW], F32) for i in range(2)]; o += 32768
        hblks = [sb.at(o + i * 16384, [128, KC, BW], F32) for i in range(2)]; o += 32768
        sqos = [sb.at(o + i * 8192, [128, KC, BW], BF16) for i in range(2)]; o += 16384
        sqbs = [sb.at(o + i * 8192, [128, KC, BW], BF16) for i in range(2)]; o += 16384
        return (osbs, sqos, hblks, sqbs), o

    def ffn(layer, next_pre, final):
        P.barrier()
        o = R_BIG
        act = sb.at(o, [128, 11, T], BF16); o += 45056
        oall = sb.at(o, [128, KC, T], F32); o += 65536
        assert o <= R_BIG + 114688
        sg = [sb.at(R_AO + i * 2048, [128, BW], F32) for i in range(2)]
        wgu = [sb.at(R_W + i * 4096, [128, 2, KC, 128], BF16) for i in range(3)]
        wds = [sb.at(R_W + 12288 + i * 2816, [128, 11, 128], BF16) for i in range(1)]
        wd2 = sb.at(R_AO + 4096, [128, 11, 128], BF16)
        wdl = [wds[0], wd2]
        for grp in range(2):
            for jj in range(11):
                j = grp * 11 + jj
                wsl = wgu[j % 3]
                r0 = (layer * JC + j) * 128
                P.add("pool", lambda e, wsl=wsl, r0=r0: e.dma_start(
                    out=wsl[:, 0, :, :], in_=wg_d[r0:r0 + 128, :].rearrange("p (k n) -> p k n", k=KC)),
                    writes=[("wgu", j % 3, 0)], dma=True)
                P.add("pool", lambda e, wsl=wsl, r0=r0: e.dma_start(
                    out=wsl[:, 1, :, :], in_=wu_d[r0:r0 + 128, :].rearrange("p (k n) -> p k n", k=KC)),
                    writes=[("wgu", j % 3, 1)], dma=True)
                for blk in range(NB):
                    bs = slice(blk * BW, (blk + 1) * BW)
                    bg = nbank()
                    mm_group(bg, ps[bg][:], [(wsl[:, 0, k, :], yT[:, k, bs]) for k in range(KC)],
                             reads=[("yT", blk, k) for k in range(KC)] + [("wgu", j % 3, 0)])
                    bu = nbank()
                    mm_group(bu, ps[bu][:], [(wsl[:, 1, k, :], yT[:, k, bs]) for k in range(KC)],
                             reads=[("yT", blk, k) for k in range(KC)] + [("wgu", j % 3, 1)])
                    s = sg[(j * NB + blk) % 2]
                    P.add("act", lambda e, s=s, bg=bg: e.activation(out=s[:], in_=ps[bg][:], func=AF.Silu),
                          reads=[("ps", bg)], writes=[("sg", id(s))])
                    P.add("dve", lambda e, s=s, bu=bu, jj=jj, bs=bs: e.tensor_tensor(
                        out=act[:, jj, bs], in0=ps[bu][:], in1=s[:], op=ALU.mult),
                        reads=[("ps", bu), ("sg", id(s))], writes=[("act", jj, bs.start)])
            for n in range(KC):
                wsl = wdl[n % 2]
                r0 = ((layer * 2 + grp) * 8 + n) * 128
                P.add("pool", lambda e, wsl=wsl, r0=r0: e.dma_start(
                    out=wsl[:], in_=wd_d[r0:r0 + 128, :].rearrange("p (j n) -> p j n", j=11)),
                    writes=[("wd", n % 2)], dma=True)
                for blk in range(NB):
                    bs = slice(blk * BW, (blk + 1) * BW)
                    bank = nbank()
                    mm_group(bank, ps[bank][:], [(wsl[:, jj, :], act[:, jj, bs]) for jj in range(11)],
                             reads=[("act", jj, bs.start) for jj in range(11)] + [("wd", n % 2)])
                    if grp == 0:
                        P.add("act", lambda e, n=n, bs=bs, bank=bank: e.activation(
                            out=oall[:, n, bs], in_=ps[bank][:], func=AF.Copy),
                            reads=[("ps", bank)], writes=[("oall", n, bs.start)])
                    else:
                        P.add("dve", lambda e, n=n, bs=bs, bank=bank: e.tensor_tensor(
                            out=oall[:, n, bs], in0=oall[:, n, bs], in1=ps[bank][:], op=ALU.add),
                            reads=[("ps", bank)], writes=[("oall", n, bs.start)])
        P.barrier()
        o = R_BIG
        hblks = [sb.at(o + i * 16384, [128, KC, BW], F32) for i in range(2)]; o += 32768
        hblks.append(sb.at(R_W, [128, KC, BW], F32))
        sqos = [sb.at(R_AO + 8192 + i * 8192, [128, KC, BW], BF16) for i in range(2)]
        sqbs = [sb.at(o + i * 8192, [128, KC, BW], BF16) for i in range(1)]; o += 8192
        sqbs.append(sb.at(R_AO + 24576, [128, KC, BW], BF16))
        assert o <= R_BIG + 45056
        for blk in range(NB):
            bs = slice(blk * BW, (blk + 1) * BW)
            sqo = sqos[blk % 2]
            P.add("act", lambda e, bs=bs, sqo=sqo: e.activation(out=sqo[:], in_=oall[:, :, bs], func=AF.Square),
                  writes=[("sqo", id(sqo))])
            epilogue(blk, _View(oall, bs), sqo, hblks[blk % 3], sqbs[blk % 2], layer, 3, next_pre, final)

    class _View:
        def __init__(self, t, bs):
            self.t, self.bs = t, bs

        def __getitem__(self, idx):
            if idx == slice(None):
                return self.t[:, :, self.bs]
            a, b, c = idx
            assert c == slice(None)
            return self.t[a, b, self.bs]

    pt_rr = [0]

    def attn_core(nkt, qk_pair, pv_ops, scale, pT, kv_reads, q_reads, acc_banks, npart_s=128, hook=None, zsum=None):
        sbanks = [None] * nkt
        pts = [None] * nkt

        def qk(t):
            bk = nbank(0, 4)
            sbanks[t] = bk
            l, r = qk_pair(t)
            mm_group(bk, ps[bk][0:npart_s, :], [(l, r)], reads=list(kv_reads) + list(q_reads))
            i = pt_rr[0] % len(pT)
            pt_rr[0] += 1
            pts[t] = i
            P.add("act", lambda e: e.activation(out=pT[i][0:npart_s, :], in_=ps[bk][0:npart_s, :], func=AF.Exp,
                                                scale=scale),
                  reads=[("ps", bk)], writes=[("pT", i)])

        def pv(t):
            i = pts[t]
            fns = []
            for (bank, out_ap, lhs_fn) in pv_ops:
                fns.append(lambda e, out_ap=out_ap, lhs_fn=lhs_fn: e.matmul(
                    out_ap, lhs_fn(t), pT[i][0:npart_s, :], start=(t == 0), stop=(t == nkt - 1)))
            P.add("pe", fns, reads=[("pT", i)] + list(kv_reads), writes=[("ps", b) for b in acc_banks])
            if zsum is not None:
                zb_, abuf, zbuf = zsum
                if zq and zq[0][0] <= t - 2:
                    zemit()
                if t % 2 == 1:
                    a = abuf[(t // 2) % 2]
                    i0 = pts[t - 1]
                    P.add("dve", lambda e: e.tensor_tensor(out=a[:], in0=pT[i0][:], in1=pT[i][:], op=ALU.add),
                          reads=[("pT", i0), ("pT", i)], writes=[("zab", id(a))])
                if t % 4 == 3:
                    z = zbuf[(t // 4) % 2]
                    P.add("dve", lambda e: e.tensor_tensor(out=z[:], in0=abuf[0][:], in1=abuf[1][:], op=ALU.add),
                          reads=[("zab", id(abuf[0])), ("zab", id(abuf[1]))], writes=[("zzb", id(z))])
                    zq.append((t, z, t // 4))

        zq = []

        def zemit():
            zb_, abuf, zbuf = zsum
            t_, z, g_ = zq.pop(0)
            ng = nkt // 4
            P.add("pe", lambda e: e.matmul(ps[zb_][:], ones_bf[:], z[:], start=(g_ == 0), stop=(g_ == ng - 1)),
                  reads=[("zzb", id(z)), "ones_bf"], writes=[("ps", zb_)])

        LA = 2
        for t in range(nkt + LA):
            if t < nkt:
                qk(t)
            if t >= LA:
                pv(t - LA)
            if hook is not None:
                hook(t)
        while zq:
            zemit()

    def mixer_b(layer, next_pre, final):
        P.barrier()
        o = R_BIG
        qT = sb.at(o, [128, 8, T], BF16); o += 32768
        kTo = sb.at(o, [128, 2, T], BF16); o += 8192
        Vo = sb.at(o, [128, 2, T], BF16); o += 8192
        o2 = o
        wqkv = sb.at(o, [128, KC, 1536], BF16); o += 24576
        qf = [sb.at(o + i * 2048, [128, BW], F32) for i in range(2)]; o += 4096
        sqh = [sb.at(o + i * 1024, [128, BW], BF16) for i in range(2)]; o += 2048
        ta = [sb.at(o + i * 2048, [128, BW], F32) for i in range(2)]; o += 4096
        tb = [sb.at(o + i * 2048, [128, BW], F32) for i in range(2)]; o += 4096
        rc = [sb.at(o + i * 2048, [128, BW], F32) for i in range(2)]; o += 4096
        rs = [sb.at(o + i * 2048, [128, BW], F32) for i in range(2)]; o += 4096
        permB = sb.at(o, [128, 128], F32); o += 512
        assert o <= R_BIG + 114688, o - R_BIG
        wo_sb = sb.at(R_W, [128, KC, D], BF16)
        P.add("pool", lambda e: e.dma_start(out=wqkv[:], in_=bwqkv_d.rearrange("(k p) n -> p k n", p=128)),
              writes=["wqkv"], dma=True)
        P.add("pool", lambda e: e.dma_start(out=wo_sb[:], in_=bwo_d.rearrange("(k p) n -> p k n", p=128)),
              writes=["wo"], dma=True)
        P.add("sp", lambda e: e.dma_start(out=permB[:], in_=consts_d[:, 0:128]), writes=["permB"], dma=True)
        cnt = 0
        for blk in range(NB):
            bs = slice(blk * BW, (blk + 1) * BW)
            s1 = blk % 2
            P.add("sp", lambda e, s1=s1, bs=bs: e.dma_start(out=rc[s1][:], in_=ropeB_d[:, bs]),
                  writes=[("rc", s1)], dma=True)
            P.add("sp", lambda e, s1=s1, blk=blk: e.dma_start(out=rs[s1][:], in_=ropeB_d[:, T + blk * BW:T + (blk + 1) * BW]),
                  writes=[("rs", s1)], dma=True)
            for c in range(10):
                s2 = cnt % 2
                cnt += 1
                bank = nbank()
                mm_group(bank, ps[bank][:], [(wqkv[:, k, c * 128:(c + 1) * 128], yT[:, k, bs]) for k in range(KC)],
                         reads=[("yT", blk, k) for k in range(KC)] + ["wqkv"])
                P.add("act", lambda e, s2=s2, bank=bank: e.activation(out=qf[s2][:], in_=ps[bank][:], func=AF.Copy),
                      reads=[("ps", bank)], writes=[("qf", s2)])
                P.add("act", lambda e, s2=s2, bank=bank: e.activation(out=sqh[s2][:], in_=ps[bank][:], func=AF.Square),
                      reads=[("ps", bank)], writes=[("sqh", s2)])
                slot = rms_rstd([sqh[s2][:]], [("sqh", s2)], 128)
                b2 = nbank()
                mm_group(b2, ps[b2][:], [(permB[:], qf[s2][:])], reads=[("qf", s2), "permB"])
                gi = 0 if c < 8 else 2
                P.add("dve", lambda e, s2=s2, s1=s1, gi=gi: e.scalar_tensor_tensor(
                    out=ta[s2][:], in0=qf[s2][:], scalar=bgains[:, gi:gi + 1], in1=rc[s1][:], op0=ALU.mult, op1=ALU.mult),
                    reads=[("qf", s2), ("rc", s1), "bgains"], writes=[("ta", s2)])
                P.add("dve", lambda e, s2=s2, s1=s1, gi=gi, b2=b2: e.scalar_tensor_tensor(
                    out=tb[s2][:], in0=ps[b2][:], scalar=bgains[:, gi + 1:gi + 2], in1=rs[s1][:], op0=ALU.mult, op1=ALU.mult),
                    reads=[("ps", b2), ("rs", s1), "bgains"], writes=[("tb", s2)])
                P.add("pool", lambda e, s2=s2: e.tensor_tensor(out=ta[s2][:], in0=ta[s2][:], in1=tb[s2][:], op=ALU.add),
                      reads=[("tb", s2)], writes=[("ta", s2)])
                dest = qT[:, c, bs] if c < 8 else kTo[:, c - 8, bs]
                dkey = ("qT", c, blk) if c < 8 else ("kTo", c - 8, blk)
                P.add("dve", lambda e, s2=s2, slot=slot, dest=dest: e.tensor_tensor(
                    out=dest, in0=ta[s2][:], in1=rstd[slot][:], op=ALU.mult),
                    reads=[("ta", s2), ("rstd", slot)], writes=[dkey])
            for tt in range(4):
                ti = blk * 4 + tt
                bank = nbank()
                mm_group(bank, ps[bank][:, 0:256],
                         [(yT[:, k, ti * 128:(ti + 1) * 128], wqkv[:, k, 1280:1536]) for k in range(KC)],
                         reads=[("yT", blk, k) for k in range(KC)] + ["wqkv"])
                P.add("act", lambda e, ti=ti, bank=bank: e.activation(
                    out=Vo[:, :, ti * 128:(ti + 1) * 128], in_=ps[bank][:, 0:256].rearrange("p (k n) -> p k n", k=2),
                    func=AF.Copy), reads=[("ps", bank)], writes=[("Vo", ti)])
        if KSTOP == 'b1':
            P.barrier()
            scr, _ = epi_scratch(R_BIG)
            out_proj_epilogue(wo_sb, layer, 1, next_pre, final, scr)
            return
        P.add("sp", lambda e: e.dma_start(out=payBK.ap().rearrange("(k d) t -> d k t", k=2), in_=kTo[:]),
              reads=[("kTo", kv, blk) for kv in range(2) for blk in range(NB)], writes=["payBK"], dma=True)
        P.add("sp", lambda e: e.dma_start(out=payBV.ap().rearrange("(k p) f -> p k f", k=2), in_=Vo[:]),
              reads=[("Vo", ti) for ti in range(16)], writes=["payBV"], dma=True)
        P.add("pool", lambda e: e.collective_compute("AllGather", ALU.bypass, replica_groups=RG,
                                                     ins=[payBK.ap().opt()], outs=[gBK.ap().opt()]),
              reads=["payBK"], writes=["gBK"], cc=True)
        P.add("pool", lambda e: e.collective_compute("AllGather", ALU.bypass, replica_groups=RG,
                                                     ins=[payBV.ap().opt()], outs=[gBV.ap().opt()]),
              reads=["payBV"], writes=["gBV"], cc=True)
        o = o2
        Kk = sb.at(o, [128, 4, T], BF16); o += 16384
        Vk = sb.at(o, [128, 4, T], BF16); o += 16384
        pT = [sb.at(o + i * 1024, [128, BW], BF16) for i in range(6)]; o += 6144
        rz = [sb.at(o + i * 2048, [128, BW], F32) for i in range(2)]; o += 4096
        zab = [sb.at(o + i * 1024, [128, BW], BF16) for i in range(2)]; o += 2048
        zzb = [sb.at(o + i * 1024, [128, BW], BF16) for i in range(2)]; o += 2048
        assert o <= R_BIG + 114688
        P.barrier()
        it = 0
        for kv in range(2):
            P.add("sp", lambda e, kv=kv: e.dma_start(
                out=Kk[:], in_=gBK.ap().rearrange("(r k d) t -> d k r t", k=2, d=128)[:, kv]),
                reads=["gBK"], writes=["Kk"], dma=True)
            P.add("sp", lambda e, kv=kv: e.dma_start(
                out=Vk[:], in_=gBV.ap().rearrange("(r k p) f -> p k r f", k=2, p=128)[:, kv]),
                reads=["gBV"], writes=["Vk"], dma=True)
            for h in range(kv * 4, kv * 4 + 4):
                if KSTOP == 'b2' or (KSTOP == 'b3' and h > 0):
                    break
                for qb in range(NB if KSTOP != 'b3' else 1):
                    qs = slice(qb * BW, (qb + 1) * BW)
                    ob = 4 + (it % 2) * 2
                    zb = ob + 1
                    it += 1
                    attn_core(
                        64,
                        lambda t, h=h, qs=qs: (Kk[:, t // 16, (t % 16) * 128:(t % 16 + 1) * 128], qT[:, h, qs]),
                        [(ob, ps[ob][:], lambda t: Vk[:, t // 16, (t % 16) * 128:(t % 16 + 1) * 128])],
                        float(128 ** -0.5), pT, ["Kk", "Vk"], [("qT", h, qb)], [ob], zsum=(zb, zab, zzb))
                    rzi = rz[it % 2]
                    P.add("dve", lambda e, rzi=rzi, zb=zb: e.tensor_copy(out=rzi[:], in_=ps[zb][:]),
                          reads=[("ps", zb)], writes=[("rz", id(rzi))])
                    P.add("dve", lambda e, rzi=rzi: e.reciprocal(out=rzi[:], in_=rzi[:]),
                          reads=[("rz", id(rzi))], writes=[("rz", id(rzi))])
                    P.add("dve", lambda e, rzi=rzi, ob=ob, h=h, qs=qs: e.tensor_tensor(
                        out=aoT[:, h, qs], in0=ps[ob][:], in1=rzi[:], op=ALU.mult),
                        reads=[("ps", ob), ("rz", id(rzi))], writes=[("ao", qb, h)])
        P.barrier()
        scr, _ = epi_scratch(R_BIG)
        out_proj_epilogue(wo_sb, layer, 1, next_pre, final, scr)

    mixers[1] = mixer_b

    def mixer_c(layer, next_pre, final):
        P.barrier()
        o = R_BIG
        cqn = sb.at(o, [128, 3, T], BF16); o += 12288
        o_keep = o
        ckvo = sb.at(o, [128, 2, T], BF16); o += 8192
        kro = sb.at(o, [128, T], BF16); o += 4096
        win = sb.at(o, [128, KC, 672], BF16); o += 10752
        cf = [sb.at(o + i * 6144, [128, 3, BW], F32) for i in range(2)]; o += 12288
        sq = [sb.at(o + i * 3072, [128, 3, BW], BF16) for i in range(2)]; o += 6144
        ta = [sb.at(o + i * 2048, [128, BW], F32) for i in range(2)]; o += 4096
        tb = [sb.at(o + i * 2048, [128, BW], F32) for i in range(2)]; o += 4096
        rc = [sb.at(o + i * 2048, [128, BW], F32) for i in range(2)]; o += 4096
        rs = [sb.at(o + i * 2048, [128, BW], F32) for i in range(2)]; o += 4096
        perm32 = sb.at(o, [128, 32], F32); o += 128
        perm96 = sb.at(o, [128, 96], F32); o += 384
        assert o <= R_BIG + 114688
        wuq = sb.at(R_W, [128, 3, 1536], BF16)
        wukv = sb.at(R_W + 9216, [128, 2, 2048], BF16)
        P.add("pool", lambda e: e.dma_start(out=win[:], in_=cwin_d.rearrange("(k p) n -> p k n", p=128)),
              writes=["win"], dma=True)
        P.add("pool", lambda e: e.dma_start(out=wuq[:], in_=cwuq_d.rearrange("(k p) n -> p k n", p=128)),
              writes=["wuq"], dma=True)
        P.add("pool", lambda e: e.dma_start(out=wukv[:], in_=cwukv_d.rearrange("(k p) n -> p k n", p=128)),
              writes=["wukv"], dma=True)
        P.add("sp", lambda e: e.dma_start(out=perm32[:], in_=consts_d[:, 256:288]), writes=["perm32"], dma=True)
        P.add("sp", lambda e: e.dma_start(out=perm96[:], in_=consts_d[:, 128:224]), writes=["perm96"], dma=True)
        cnt = 0
        for blk in range(NB):
            bs = slice(blk * BW, (blk + 1) * BW)
            s1 = blk % 2
            P.add("sp", lambda e, s1=s1, blk=blk: e.dma_start(
                out=rc[s1][0:32, :], in_=ropeC_d[0:32, 2 * T + blk * BW:2 * T + (blk + 1) * BW]),
                writes=[("rc", s1)], dma=True)
            P.add("sp", lambda e, s1=s1, blk=blk: e.dma_start(
                out=rs[s1][0:32, :], in_=ropeC_d[0:32, 3 * T + blk * BW:3 * T + (blk + 1) * BW]),
                writes=[("rs", s1)], dma=True)
            for (nch, c0, g0, dst, dname) in ((3, 0, 0, cqn, "cqn"), (2, 384, 3, ckvo, "ckvo")):
                s2 = cnt % 2
                cnt += 1
                for i in range(nch):
                    bank = nbank()
                    mm_group(bank, ps[bank][:], [(win[:, k, c0 + i * 128:c0 + (i + 1) * 128], yT[:, k, bs]) for k in range(KC)],
                             reads=[("yT", blk, k) for k in range(KC)] + ["win"])
                    P.add("act", lambda e, s2=s2, i=i, bank=bank: e.activation(out=cf[s2][:, i, :], in_=ps[bank][:], func=AF.Copy),
                          reads=[("ps", bank)], writes=[("cf", s2, i)])
                    P.add("act", lambda e, s2=s2, i=i, bank=bank: e.activation(out=sq[s2][:, i, :], in_=ps[bank][:], func=AF.Square),
                          reads=[("ps", bank)], writes=[("csq", s2, i)])
                slot = rms_rstd([sq[s2][:, i, :] for i in range(nch)], [("csq", s2, i) for i in range(nch)], nch * 128)
                for i in range(nch):
                    P.add("dve", lambda e, s2=s2, i=i, slot=slot, dst=dst, g0=g0, bs=bs: e.scalar_tensor_tensor(
                        out=dst[:, i, bs], in0=cf[s2][:, i, :], scalar=cgains[:, g0 + i:g0 + i + 1], in1=rstd[slot][:],
                        op0=ALU.mult, op1=ALU.mult),
                        reads=[("cf", s2, i), ("rstd", slot), "cgains"], writes=[(dname, i, blk)])
            bank = nbank()
            mm_group(bank, ps[bank][0:32, :], [(win[:, k, 640:672], yT[:, k, bs]) for k in range(KC)],
                     reads=[("yT", blk, k) for k in range(KC)] + ["win"])
            P.add("act", lambda e, s1=s1, bank=bank: e.activation(out=ta[s1][0:32, :], in_=ps[bank][0:32, :], func=AF.Copy),
                  reads=[("ps", bank)], writes=[("ta", s1)])
            b2 = nbank()
            mm_group(b2, ps[b2][0:32, :], [(perm32[0:32, :], ta[s1][0:32, :])], reads=[("ta", s1), "perm32"])
            P.add("dve", lambda e, s1=s1, b2=b2: e.tensor_tensor(out=tb[s1][0:32, :], in0=ps[b2][0:32, :], in1=rs[s1][0:32, :], op=ALU.mult),
                  reads=[("ps", b2), ("rs", s1)], writes=[("tb", s1)])
            P.add("dve", lambda e, s1=s1: e.tensor_tensor(out=ta[s1][0:32, :], in0=ta[s1][0:32, :], in1=rc[s1][0:32, :], op=ALU.mult),
                  reads=[("rc", s1)], writes=[("ta", s1)])
            P.add("pool", lambda e, s1=s1, bs=bs: e.tensor_tensor(out=kro[0:32, bs], in0=ta[s1][0:32, :], in1=tb[s1][0:32, :], op=ALU.add),
                  reads=[("ta", s1), ("tb", s1)], writes=[("kro", blk)])
        if KSTOP == 'c1':
            P.barrier()
            wo_sb = sb.at(R_W, [128, KC, D], BF16)
            P.add("pool", lambda e: e.dma_start(out=wo_sb[:], in_=cwo_d.rearrange("(k p) n -> p k n", p=128)),
                  writes=["wo"], dma=True)
            scr, _ = epi_scratch(R_BIG)
            out_proj_epilogue(wo_sb, layer, 1, next_pre, final, scr)
            return
        P.add("sp", lambda e: e.dma_start(out=payBK.ap().rearrange("(i p) t -> p i t", p=128), in_=ckvo[:]),
              reads=[("ckvo", i, blk) for i in range(2) for blk in range(NB)], writes=["payBK"], dma=True)
        P.add("sp", lambda e: e.dma_start(out=payBV.ap().rearrange("(i p) t -> p i t", p=128)[:, 0, :], in_=kro[:]),
              reads=[("kro", blk) for blk in range(NB)], writes=["payBV"], dma=True)
        P.add("pool", lambda e: e.collective_compute("AllGather", ALU.bypass, replica_groups=RG,
                                                     ins=[payBK.ap().opt()], outs=[gBK.ap().opt()]),
              reads=["payBK"], writes=["gBK"], cc=True)
        P.add("pool", lambda e: e.collective_compute("AllGather", ALU.bypass, replica_groups=RG,
                                                     ins=[payBV.ap().opt()], outs=[gBV.ap().opt()]),
              reads=["payBV"], writes=["gBV"], cc=True)
        P.barrier()
        ckvA = sb.at(R_Y, [128, 2, S], BF16)
        o = o_keep
        Kh = [sb.at(o + i * 16384, [128, S], BF16) for i in range(2)]; o += 32768
        Vh = [sb.at(o + i * 8320, [128, 64, 65], BF16) for i in range(2)]; o += 16640
        qh = [sb.at(o + i * 4096, [128, T], BF16) for i in range(2)]; o += 8192
        pT = [sb.at(o + i * 1024, [128, BW], BF16) for i in range(6)]; o += 6144
        tq = sb.at(o, [128, 2, T], F32); o += 16384
        qf = [sb.at(o + i * 2048, [128, BW], F32) for i in range(2)]; o += 4096
        tqa = [sb.at(o + i * 2048, [128, BW], F32) for i in range(2)]; o += 4096
        tqb = [sb.at(o + i * 2048, [128, BW], F32) for i in range(2)]; o += 4096
        rzr = [sb.at(o + i * 2048, [128, BW], F32) for i in range(2)]; o += 4096
        zs = [sb.at(o, [128, BW], F32) for i in range(2)]; o += 2048
        perm96b = sb.at(o, [128, 96], F32); o += 384
        assert o <= R_BIG + 114688, o - R_BIG
        P.add("sp", lambda e: e.dma_start(out=perm96b[:], in_=consts_d[:, 128:224]), writes=["perm96"], dma=True)
        P.add("sp", lambda e: e.dma_start(out=tq[:], in_=ropeC_d[:, 0:2 * T].rearrange("p (a t) -> p a t", a=2)),
              writes=["tq"], dma=True)
        for r in range(4):
            P.add("sp", lambda e, r=r: e.dma_start(
                out=ckvA[:, :, r * T:(r + 1) * T], in_=gBK.ap()[r * 256:(r + 1) * 256, :].rearrange("(i p) t -> p i t", p=128)),
                writes=[("ckvA", r)], dma=True)
            for b in range(2):
                P.add("sp", lambda e, r=r, b=b: e.dma_start(
                    out=Kh[b][64:96, r * T:(r + 1) * T], in_=gBV.ap()[r * 256:r * 256 + 32, :]),
                    writes=[("Khr", b, r)], dma=True)
        for b in range(2):
            P.add("pool", lambda e, b=b: e.memset(Vh[b][:], 1.0), writes=[("Vh1", b), ("Vh", b)])
        ckv_reads = [("ckvA", r) for r in range(4)]
        prep_cnt = [0]

        def prep(h):
            b = h % 2
            for blk in range(NB):
                bs = slice(blk * BW, (blk + 1) * BW)
                s2 = prep_cnt[0] % 2
                prep_cnt[0] += 1
                bank = nbank(6, 8)
                mm_group(bank, ps[bank][0:96, :], [(wuq[:, i, h * 96:(h + 1) * 96], cqn[:, i, bs]) for i in range(3)],
                         reads=["wuq"])
                P.add("dve", lambda e, s2=s2, bank=bank: e.tensor_copy(out=qf[s2][0:96, :], in_=ps[bank][0:96, :]),
                      reads=[("ps", bank)], writes=[("qf", s2)])
                b2 = nbank(6, 8)
                mm_group(b2, ps[b2][0:96, :], [(perm96b[0:96, :], qf[s2][0:96, :])], reads=[("qf", s2), "perm96"])
                P.add("dve", lambda e, s2=s2, bs=bs: e.tensor_tensor(out=tqa[s2][0:96, :], in0=qf[s2][0:96, :], in1=tq[0:96, 0, bs], op=ALU.mult),
                      reads=[("qf", s2), "tq"], writes=[("tqa", s2)])
                P.add("dve", lambda e, s2=s2, bs=bs, b2=b2: e.tensor_tensor(out=tqb[s2][0:96, :], in0=ps[b2][0:96, :], in1=tq[0:96, 1, bs], op=ALU.mult),
                      reads=[("ps", b2), "tq"], writes=[("tqb", s2)])
                P.add("pool", lambda e, s2=s2, bs=bs, b=b: e.tensor_tensor(out=qh[b][0:96, bs], in0=tqa[s2][0:96, :], in1=tqb[s2][0:96, :], op=ALU.add),
                      reads=[("tqa", s2), ("tqb", s2)], writes=[("qh", b, blk)])
                yield
            for kb in range(16):
                ks = slice(kb * BW, (kb + 1) * BW)
                bank = nbank(6, 8)
                mm_group(bank, ps[bank][0:64, :], [(wukv[:, i, h * 128:h * 128 + 64], ckvA[:, i, ks]) for i in range(2)],
                         reads=["wukv"] + ckv_reads)
                P.add("dve", lambda e, b=b, ks=ks, bank=bank: e.tensor_copy(out=Kh[b][0:64, ks], in_=ps[bank][0:64, :]),
                      reads=[("ps", bank)], writes=[("Kh", b)])
                yield
            for g8 in range(8):
                bank = nbank(6, 8)
                fns = []
                for u in range(8):
                    tl = g8 * 8 + u
                    for i in range(2):
                        fns.append(lambda e, u=u, tl=tl, i=i, bank=bank: e.matmul(
                            ps[bank][:, u * 64:(u + 1) * 64], ckvA[:, i, tl * 128:(tl + 1) * 128],
                            wukv[:, i, h * 128 + 64:h * 128 + 128], start=(i == 0), stop=(i == 1)))
                P.add("pe", fns, reads=["wukv"] + ckv_reads, writes=[("ps", bank)])
                P.add("dve", lambda e, b=b, g8=g8, bank=bank: e.tensor_copy(
                    out=Vh[b][:, g8 * 8:(g8 + 1) * 8, 0:64], in_=ps[bank][:].rearrange("p (u n) -> p u n", u=8)),
                    reads=[("ps", bank)], writes=[("Vh", b)])
                yield

        it = 0

        def cpull(gen, n=1):
            if gen is None:
                return
            for _ in range(n):
                try:
                    next(gen)
                except StopIteration:
                    return

        cpend = []

        def cnorm(zi, ob, h, qs, qb):
            pb = (h % 2) * 64

            def ph_a():
                P.add("dve", lambda e: e.tensor_copy(out=rzr[zi][64:65, :], in_=ps[ob][64:65, :]),
                      reads=[("ps", ob)], writes=[("rzr", zi)])
                P.add("dve", lambda e: e.reciprocal(out=rzr[zi][64:65, :], in_=rzr[zi][64:65, :]),
                      reads=[("rzr", zi)], writes=[("rzr", zi)])
                bcast64_a(rzr[zi][64:65, :], ("rzr", zi))

            def ph_b():
                zb = nbank(6, 8)
                bcast64_b(zb)
                P.add("dve", lambda e: e.tensor_copy(out=zs[zi][0:64, :], in_=ps[zb][0:64, :]),
                      reads=[("ps", zb)], writes=[("zs", 0)])
                P.add("dve", lambda e: e.tensor_tensor(
                    out=aoT[pb:pb + 64, h // 2, qs], in0=ps[ob][0:64, :], in1=zs[zi][0:64, :], op=ALU.mult),
                    reads=[("ps", ob), ("zs", 0)], writes=[("ao", qb, h // 2, h % 2)])
            return [ph_a, ph_b]

        def crun():
            if not cpend:
                return
            th = cpend[0]
            th.pop(0)()
            if not th:
                cpend.pop(0)

        def chook(t, cgen):
            if t % 8 == 4:
                cpull(cgen, 1)
            if t in (6, 14):
                crun()

        if KSTOP != 'c2':
            cpull(prep(0), 1000)
        for h in range(16):
            if KSTOP in ('c2', 'c3'):
                break
            b = h % 2
            cgen = prep(h + 1) if h + 1 < 16 else None
            for qb in range(NB):
                qs = slice(qb * BW, (qb + 1) * BW)
                ob = 4 + (it % 2)
                it += 1
                attn_core(
                    64,
                    lambda t, b=b, qs=qs: (Kh[b][0:96, t * 128:(t + 1) * 128], qh[b][0:96, qs]),
                    [(ob, ps[ob][0:65, :], lambda t, b=b: Vh[b][:, t, :])],
                    float(96 ** -0.5), pT, [("Kh", b), ("Vh", b), ("Vh1", b)] + [("Khr", b, r) for r in range(4)],
                    [("qh", b, qb)], [ob], hook=(lambda t, cgen=cgen: chook(t, cgen)))
                cpend.append(cnorm(it % 2, ob, h, qs, qb))
                if qb == NB - 1:
                    cpull(cgen, 1000)
        while cpend:
            crun()
        P.barrier()
        wo_sb = sb.at(R_W, [128, KC, D], BF16)
        P.add("pool", lambda e: e.dma_start(out=wo_sb[:], in_=cwo_d.rearrange("(k p) n -> p k n", p=128)),
              writes=["wo"], dma=True)
        scr, _ = epi_scratch(R_BIG)
        out_proj_epilogue(wo_sb, layer, 1, next_pre, final, scr)

    mixers[2] = mixer_c

    def mixer_a(layer, next_pre, final):
        la = layer // 3
        P.barrier()
        P.barrier()
        o = R_BIG
        yw = sb.at(o, [128, KC, 4096], BF16); o += 65536
        qs_ = [sb.at(o + i * 4096, [128, T], BF16) for i in range(2)]; o += 8192
        kb_ = [sb.at(o + i * 8192, [128, 4096], BF16) for i in range(2)]; o += 16384
        vb_ = [sb.at(o + i * 8320, [128, 32, 2, 65], BF16) for i in range(2)]; o += 16640
        tmp = [sb.at(o + i * 1024, [128, 256], F32) for i in range(4)]; o += 4096
        ptl = [sb.at(o + i * 512, [128, 256], BF16) for i in range(5)]; o += 2560
        dist = sb.at(o, [128, 256], F32); o += 1024
        assert o <= R_BIG + 114688, o - R_BIG
        o = R_Y
        acc = sb.at(o, [128, 2, T], F32); o += 16384
        wsl = [sb.at(o + i * 6144, [128, 3, KC, 128], BF16) for i in range(2)]; o += 12288
        rzr = [sb.at(o + i * 2048, [128, BW], F32) for i in range(2)]; o += 4096
        assert o <= R_Y + 32768
        wo_sb = sb.at(R_W, [128, KC, D], BF16)
        P.add("pool", lambda e: e.dma_start(out=wo_sb[:], in_=awo_d[la * D:(la + 1) * D, :].rearrange("(k p) n -> p k n", p=128)),
              writes=["wo"], dma=True)
        P.add("sp", lambda e: e.dma_start(out=dist[:], in_=consts_d[:, 384:640]), writes=["dist"], dma=True)

        def wl(part, blk, o0):
            def f(e):
                if not sv:
                    pid = nc.partition_id()
                    g = pid % 4
                    sv[0] = ((g + 3) % 4) * D
                    sv[1] = g * D
                    sv[2] = ((g + 1) % 4) * D
                return e.dma_start(out=yw[:, :, o0:o0 + BW],
                                   in_=gA4[blk].ap()[bass.ds(sv[part], D), :].rearrange("(k p) t -> p k t", p=128))
            return f

        wl_list = [(1, b_, 1024 + b_ * BW) for b_ in range(4)] + [(0, 2, 0), (0, 3, BW), (2, 0, 3072), (2, 1, 3072 + BW)]
        for (part, b_, o0) in wl_list:
            P.add("pool", wl(part, b_, o0), writes=[("yw", part, b_)], dma=True)
        yw_reads = [("yw", part, b_) for (part, b_, o0) in wl_list]
        steps = [(c, gi) for c in range(8) for gi in range(3)]
        vcol0 = [0, 17, 37]
        evac_rr = [0]

        def evac(out_ap, in_ap, reads, writes, scale=None):
            use_act = (evac_rr[0] % 2 == 0)
            evac_rr[0] += 1
            if use_act:
                if scale is None:
                    P.add("act", lambda e: e.activation(out=out_ap, in_=in_ap, func=AF.Copy), reads=reads, writes=writes)
                else:
                    P.add("act", lambda e: e.activation(out=out_ap, in_=in_ap, func=AF.Copy, scale=scale), reads=reads, writes=writes)
            else:
                if scale is None:
                    P.add("dve", lambda e: e.tensor_copy(out=out_ap, in_=in_ap), reads=reads, writes=writes)
                else:
                    P.add("dve", lambda e: e.tensor_scalar(out=out_ap, in0=in_ap, scalar1=scale, scalar2=None, op0=ALU.mult),
                          reads=reads, writes=writes)

        def proj(si):
            c, gi = steps[si]
            st = si % 2
            d = A_GROUPS[gi][1]
            nqt = 16 // d
            w = wsl[st]
            for s3 in range(3):
                r0 = ((((la * 3 + s3) * 3 + gi) * 8) + c) * 128
                P.add("pool", lambda e, s3=s3, r0=r0: e.dma_start(
                    out=w[:, s3, :, :], in_=awqkv_d[r0:r0 + 128, :].rearrange("p (k n) -> p k n", k=KC)),
                    writes=[("aw", st, s3)], dma=True)
            for blk in range(NB):
                bank = nbank(6, 8)
                mm_group(bank, ps[bank][:], [(w[:, 0, k, :], yw[:, k, 1024 + blk * BW:1024 + (blk + 1) * BW]) for k in range(KC)],
                         reads=yw_reads + [("aw", st, 0)])
                evac(qs_[st][:, blk * BW:(blk + 1) * BW], ps[bank][:], [("ps", bank)], [("aq", st)], scale=0.125)
                yield
            w0 = 1024 - 64 * d
            Lk = 2048 + 128 * d
            j0 = 0
            while j0 < Lk:
                wd_ = min(BW, Lk - j0)
                bank = nbank(6, 8)
                mm_group(bank, ps[bank][:, 0:wd_], [(w[:, 1, k, :], yw[:, k, w0 + j0:w0 + j0 + wd_]) for k in range(KC)],
                         reads=yw_reads + [("aw", st, 1)])
                evac(kb_[st][:, j0:j0 + wd_], ps[bank][:, 0:wd_], [("ps", bank)], [("ak", st)])
                j0 += wd_
                yield
            ntile = d * (nqt + 1)
            for vt in range(ntile):
                r, m_ = vt // (nqt + 1), vt % (nqt + 1)
                ws = 1024 + r + d * (128 * m_ - 64)
                bank = nbank(6, 8)
                mm_group(bank, ps[bank][:, 0:128],
                         [(yw[:, k, ws:ws + 127 * d + 1:d], w[:, 2, k, :]) for k in range(KC)],
                         reads=yw_reads + [("aw", st, 2)])
                vc = vcol0[gi] + vt
                outv = vb_[st][:, vt, :, 0:64]
                inv = ps[bank][:, 0:128].rearrange("p (e n) -> p e n", e=2)
                if vt % 2 == 0:
                    P.add("act", lambda e, outv=outv, inv=inv, vc=vc: e.activation(
                        out=outv, in_=inv, func=AF.Copy, scale=validA[:, vc:vc + 1]),
                        reads=[("ps", bank), "validA"], writes=[("av", st)])
                else:
                    P.add("dve", lambda e, outv=outv, inv=inv, vc=vc: e.tensor_scalar(
                        out=outv, in0=inv, scalar1=validA[:, vc:vc + 1], scalar2=None, op0=ALU.mult),
                        reads=[("ps", bank), "validA"], writes=[("av", st)])
                yield
            for e2 in range(2):
                P.add("dve", lambda e, e2=e2, ntile=ntile, gi=gi: e.tensor_copy(
                    out=vb_[st][:, 0:ntile, e2, 64], in_=validA[:, vcol0[gi]:vcol0[gi] + ntile]),
                    reads=["validA"], writes=[("av1", st, e2)])

        pt_i = [0]
        ob_i = [0]

        def pull(gen, n=1):
            if gen is None:
                return
            for _ in range(n):
                try:
                    next(gen)
                except StopIteration:
                    return

        def attn(si, gen=None):
            c, gi = steps[si]
            st = si % 2
            d = A_GROUPS[gi][1]
            nqt = 16 // d
            tiles = [(r, j) for r in range(d) for j in range(nqt)]
            seq = [(e2, b4, u) for e2 in range(2) for b4 in range(4) for u in range(4)]
            info = {}

            def emit_qk(ix):
                e2, b4, u = seq[ix]
                pb = e2 * 64
                slope = alibi_slope(gi, 2 * c + e2) * d
                r, j = tiles[b4 * 4 + u]
                qc0 = r + d * 128 * j
                bk, half = pt_i[0] % 4, 0
                i = pt_i[0] % 4
                i5 = pt_i[0] % 5
                pt_i[0] += 1
                info[ix] = i5
                sview = ps[bk][:, half * 256:(half + 1) * 256]
                fns = []
                for piece in range(2):
                    kc0 = r + 128 * d * (j + piece)
                    fns.append(lambda e, piece=piece, kc0=kc0: e.matmul(
                        ps[bk][:, half * 256 + piece * 128:half * 256 + (piece + 1) * 128],
                        kb_[st][pb:pb + 64, kc0:kc0 + 127 * d + 1:d], qs_[st][pb:pb + 64, qc0:qc0 + 127 * d + 1:d],
                        start=True, stop=True))
                P.add("pe", fns, reads=[("aq", st), ("ak", st)], writes=[("psh", bk, half)])
                P.add("dve", lambda e: e.scalar_tensor_tensor(
                    out=tmp[i][:], in0=dist[:], scalar=float(slope), in1=sview, op0=ALU.mult, op1=ALU.add),
                    reads=[("psh", bk, half), "dist"], writes=[("atmp", i)])
                P.add("act", lambda e: e.activation(out=ptl[i5][:], in_=tmp[i][:], func=AF.Exp),
                      reads=[("atmp", i)], writes=[("apt", i5)])

            def emit_pv(ix):
                e2, b4, u = seq[ix]
                i5 = info[ix]
                r, j = tiles[b4 * 4 + u]
                if u == 0:
                    ob_i[0] += 1
                ob = 4 + ob_i[0] % 2
                vt0 = r * (nqt + 1) + j
                fns = []
                for piece in range(2):
                    fns.append(lambda e, piece=piece: e.matmul(
                        ps[ob][0:65, u * 128:(u + 1) * 128], vb_[st][:, vt0 + piece, e2, :],
                        ptl[i5][:, piece * 128:(piece + 1) * 128], start=(piece == 0), stop=(piece == 1)))
                P.add("pe", fns, reads=[("apt", i5), ("av", st), ("av1", st, e2)], writes=[("ps", ob)])
                if u == 3:
                    if d == 1:
                        av = acc[0:65, e2, b4 * 512:(b4 + 1) * 512]
                        pv_ = ps[ob][0:65, :]
                    elif d == 4:
                        av = acc[0:65, e2, b4:T:4]
                        pv_ = ps[ob][0:65, :]
                    else:
                        av = acc[0:65, e2, :].rearrange("p (i dd) -> p dd i", dd=16)[:, b4 * 4:(b4 + 1) * 4, :]
                        pv_ = ps[ob][0:65, :].rearrange("p (u i) -> p u i", u=4)
                    if gi == 0:
                        P.add("dve", lambda e: e.tensor_copy(out=av, in_=pv_),
                              reads=[("ps", ob)], writes=[("acc", e2)])
                    else:
                        P.add("dve", lambda e: e.tensor_tensor(out=av, in0=av, in1=pv_, op=ALU.add),
                              reads=[("ps", ob)], writes=[("acc", e2)])

            LA = 3
            for ix in range(len(seq) + LA):
                if ix < len(seq):
                    emit_qk(ix)
                if ix >= LA:
                    emit_pv(ix - LA)
                pull(gen, 1)
                if ix < 16:
                    run_pending(1)
            pull(gen, 1000)
            if gi == 2:
                for e2 in range(2):
                    for blk in range(NB):
                        pending.append(norm_thunks(c, e2, blk))

        def norm_thunks(c, e2, blk):
            pb = e2 * 64
            bs = slice(blk * BW, (blk + 1) * BW)
            zi = (e2 * NB + blk) % 2

            def ph_a():
                P.add("dve", lambda e: e.reciprocal(out=rzr[zi][64:65, :], in_=acc[64:65, e2, bs]),
                      reads=[("acc", e2)], writes=[("rzr", zi)])
                bcast64_a(rzr[zi][64:65, :], ("rzr", zi))

            def ph_b():
                zb = nbank(6, 8)
                bcast64_b(zb)
                P.add("dve", lambda e: e.tensor_tensor(
                    out=aoT[pb:pb + 64, c, bs], in0=acc[0:64, e2, bs], in1=ps[zb][0:64, :], op=ALU.mult),
                    reads=[("ps", zb), ("acc", e2)], writes=[("ao", blk, c, e2)])
            return [ph_a, ph_b]

        pending = []

        def run_pending(n):
            for _ in range(n):
                if not pending:
                    return
                th = pending[0]
                th.pop(0)()
                if not th:
                    pending.pop(0)

        nsteps = len(steps)
        if KSTOP == 'a1':
            nsteps = 0
        elif KSTOP in ('a2', 'a3'):
            nsteps = 1
        if nsteps:
            pull(proj(0), 1000)
        for si in range(nsteps):
            gen = proj(si + 1) if si + 1 < nsteps else None
            if KSTOP != 'a2':
                attn(si, gen)
            else:
                pull(gen, 1000)
        run_pending(1000)
        P.barrier()
        if KSTOP == 'adbg':
            P.add("pool", lambda e: e.dma_start(out=out3[:, :, :], in_=aoT[:]), writes=["dbg"], dma=True)
            return
        scr, _ = epi_scratch(R_BIG)
        out_proj_epilogue(wo_sb, layer, 1, next_pre, final, scr)

    mixers[0] = mixer_a

    from_x = True
    nl = len(layers)
    for li, layer in enumerate(layers):
        kind = layer % 3
        last = (li == nl - 1)
        if li == 0:
            P.barrier()
            hbl = [sb.at(R_BIG + i * 16384, [128, KC, BW], F32) for i in range(2)]
            sqb = [sb.at(R_BIG + 32768 + i * 8192, [128, KC, BW], BF16) for i in range(2)]
            first_kind = 2 if skip_mixer else 0
            for blk in range(NB):
                prologue_only(blk, hbl[blk % 2], sqb[blk % 2], layer, first_kind, xT3, True)
        if not skip_mixer:
            mixers[kind](layer, (layer, 2) if not skip_ffn else (None if last else (layers[li + 1], 0)),
                         final=(last and skip_ffn))
        if not skip_ffn:
            nxt = None if last else (layers[li + 1], 2 if skip_mixer else 0)
            ffn(layer, nxt, final=last)

    P.barrier()

    with nc.Block() as block:
        P.replay(block)
    nc.used_inputs = used_inputs
    return nc


mixers = {}


def _prep_shared(inp):
    f = np.float32
    sh = {}
    g = np.zeros((128, 128), f)
    for kind, name in enumerate(["norm_mix_pre", "norm_mix_post", "norm_ffn_pre", "norm_ffn_post"]):
        a = np.asarray(inp[name], f)
        for l in range(DEPTH):
            g[:, (kind * 4 + l) * 8:(kind * 4 + l) * 8 + 8] = a[l].reshape(8, 128).T
    sh["gains"] = g
    wg = np.asarray(inp["ffn_wg"], f); wu = np.asarray(inp["ffn_wu"], f); wd = np.asarray(inp["ffn_wd"], f)
    sh["wg"] = np.ascontiguousarray(wg.reshape(DEPTH, 8, 128, JC, 128).transpose(0, 3, 2, 1, 4)).reshape(DEPTH * JC * 128, 1024)
    sh["wu"] = np.ascontiguousarray(wu.reshape(DEPTH, 8, 128, JC, 128).transpose(0, 3, 2, 1, 4)).reshape(DEPTH * JC * 128, 1024)
    sh["wd"] = np.ascontiguousarray(wd.reshape(DEPTH, 2, 11, 128, 8, 128).transpose(0, 1, 4, 3, 2, 5)).reshape(DEPTH * 2 * 8 * 128, 11 * 128)
    aw = np.asarray(inp["a_wqkv"], f)
    sh["a_wqkv"] = np.ascontiguousarray(aw.reshape(2, 8, 128, 3, 3, 8, 128).transpose(0, 3, 4, 5, 2, 1, 6)).reshape(2 * 3 * 3 * 8 * 128, 1024)
    sh["a_wo"] = np.ascontiguousarray(np.asarray(inp["a_wo"], f).reshape(2 * D, D))
    sh["b_wqkv"] = np.ascontiguousarray(np.asarray(inp["b_wqkv"], f)[0])
    sh["b_wo"] = np.ascontiguousarray(np.asarray(inp["b_wo"], f)[0])
    sh["c_win"] = np.ascontiguousarray(np.asarray(inp["c_win"], f)[0])
    sh["c_wuq"] = np.ascontiguousarray(np.asarray(inp["c_wuq"], f)[0])
    sh["c_wukv"] = np.ascontiguousarray(np.asarray(inp["c_wukv"], f)[0])
    sh["c_wo"] = np.ascontiguousarray(np.asarray(inp["c_wo"], f)[0])
    bg = np.zeros((128, 8), f)
    qn = np.asarray(inp["b_qnorm"], f)[0]; kn = np.asarray(inp["b_knorm"], f)[0]
    partner = np.arange(128); dd = partner % 64
    partner = np.where(dd < 32, partner + 32, partner - 32)
    bg[:, 0] = qn; bg[:, 1] = qn[partner]; bg[:, 2] = kn; bg[:, 3] = kn[partner]
    sh["bgains"] = bg
    cg = np.zeros((128, 8), f)
    cq = np.asarray(inp["c_qnorm"], f)[0]; ck = np.asarray(inp["c_kvnorm"], f)[0]
    for i in range(3):
        cg[:, i] = cq[i * 128:(i + 1) * 128]
    for i in range(2):
        cg[:, 3 + i] = ck[i * 128:(i + 1) * 128]
    sh["cgains"] = cg
    c = np.zeros((128, 1024), f)
    for mcol in range(128):
        c[partner[mcol], mcol] = 1.0
    p96 = np.arange(96)
    p96 = np.where(p96 < 64, p96, np.where(p96 < 80, p96 + 16, p96 - 16))
    for mcol in range(96):
        c[p96[mcol], 128 + mcol] = 1.0
    p32 = np.arange(32); p32 = np.where(p32 < 16, p32 + 16, p32 - 16)
    for mcol in range(32):
        c[p32[mcol], 256 + mcol] = 1.0
    pk = np.arange(128)[:, None]; qi = np.arange(128)[None, :]
    for piece, off in enumerate((-64, 64)):
        diff = pk + off - qi
        c[:, 384 + piece * 128:384 + (piece + 1) * 128] = np.where(np.abs(diff) <= 64, -np.abs(diff), -1e9)
    sh["consts"] = c
    return sh


def _prep_core(inp, c):
    f = np.float32
    b, g = c // 4, c % 4
    T0 = g * T
    per = {}
    per["xT"] = np.ascontiguousarray(np.asarray(inp["x"], f)[b, T0:T0 + T, :].T)
    s = (T0 + np.arange(T)).astype(f)
    d = np.arange(128); dd = d % 64; fi = dd % 32
    freq = np.power(f(10000.0), -(fi.astype(f)) * f(2.0) / f(64.0)).astype(f)
    row = np.floor(s / 64.0).astype(f); col = (s - row * 64).astype(f)
    pos = np.where((d < 64)[:, None], row[None, :], col[None, :]).astype(f)
    ang = (pos * freq[:, None]).astype(f)
    cb = np.cos(ang).astype(f); sn = np.sin(ang).astype(f)
    sb_ = np.where((dd < 32)[:, None], -sn, sn).astype(f)
    per["ropeB"] = np.ascontiguousarray(np.concatenate([cb, sb_], axis=1))
    fi16 = (np.arange(32) % 16).astype(f)
    fr = np.power(f(10000.0), -fi16 * f(2.0) / f(32.0)).astype(f)
    ang = (s[None, :] * fr[:, None]).astype(f)
    cc = np.cos(ang).astype(f); ss = np.sin(ang).astype(f)
    ss = np.where((np.arange(32) < 16)[:, None], -ss, ss).astype(f)
    tc_ = np.zeros((128, 4 * T), f)
    tc_[0:64, 0:T] = 1.0
    tc_[64:96, 0:T] = cc; tc_[64:96, T:2 * T] = ss
    tc_[0:32, 2 * T:3 * T] = cc; tc_[0:32, 3 * T:] = ss
    per["ropeC"] = tc_
    va = np.zeros((128, 72), f)
    col0 = 0
    for (window, dil) in A_GROUPS:
        nqt = 16 // dil
        for r in range(dil):
            for mm in range(nqt + 1):
                idx = 128 * mm - 64 + np.arange(128)
                tok = T0 + r + dil * idx
                va[:, col0 + r * (nqt + 1) + mm] = ((tok >= 0) & (tok < S)).astype(f)
        col0 += dil * (nqt + 1)
    per["validA"] = va
    return per


_CACHE = {}


def kernel(**inputs):
    key = "full"
    if key not in _CACHE:
        _CACHE[key] = build_program()
    nc = _CACHE[key]
    sh = _prep_shared(inputs)
    in_maps = []
    for c in range(NCORES):
        mp = dict(sh)
        mp.update(_prep_core(inputs, c))
        in_maps.append({k: mp[k] for k in nc.used_inputs})
    res = run_bass_kernel_spmd(nc, in_maps, core_ids=list(range(NCORES)))
    out = np.empty((2, S, D), np.float32)
    for c in range(NCORES):
        b, g = c // 4, c % 4
        out[b, g * T:(g + 1) * T, :] = res.results[c]["out"].T
    return out
```

```python
import contextlib
import os
KSTOP = os.environ.get('KSTOP', '')
import numpy as np
import concourse.bass as bass
import concourse.mybir as mybir
from concourse.bass_utils import run_bass_kernel_spmd

F32 = mybir.dt.float32
BF16 = mybir.dt.bfloat16
AF = mybir.ActivationFunctionType
ALU = mybir.AluOpType

NCORES = 8
D = 1024
KC = 8
T = 2048
S = 8192
NB = 4
BW = 512
DFF = 2816
JC = 22
EPS = 1e-6
DEPTH = 4
SB_BASE = 18560
SB_END = 229376
A_GROUPS = ((128, 1), (512, 4), (2048, 16))


class Prog:
    ENG = ("pe", "act", "dve", "pool", "sp")
    MAXC = 20000

    def __init__(self, nc):
        self.nc = nc
        self.q = {e: [] for e in self.ENG}
        self.tick = {}
        self.nsem = 0
        self.lastw = {}
        self.readers = {}
        self.seen = {e: {} for e in self.ENG}
        self.dma_slots = {}
        self.dma_rr = {}
        self.last_ticket = {}
        self.sems = {}

    def _newsem(self, name):
        self.nsem += 1
        h = self.nc.alloc_semaphore(f"{name}_{self.nsem}")
        self.sems[id(h)] = h
        return h

    def _compute_ticket(self, eng):
        st = self.tick.get(eng)
        if st is None or st[1] >= self.MAXC:
            st = [self._newsem("t" + eng), 0]
            self.tick[eng] = st
        st[1] += 1
        return (id(st[0]), st[1], eng), (st[0], 1)

    def _dma_ticket(self, eng, nslots=12):
        slots = self.dma_slots.setdefault(eng, [])
        if len(slots) < nslots:
            slots.append([self._newsem("d" + eng), 0])
            i = len(slots) - 1
        else:
            i = self.dma_rr.get(eng, 0) % nslots
            self.dma_rr[eng] = i + 1
        st = slots[i]
        prev = (id(st[0]), st[1], "dma") if st[1] > 0 else None
        st[1] += 16
        return (id(st[0]), st[1], "dma"), (st[0], 16), prev

    def add(self, eng, fns, reads=(), writes=(), dma=False, waits=(), cc=False):
        if callable(fns):
            fns = [fns]
        tks = set(waits)
        for k in reads:
            t = self.lastw.get(k)
            if t is not None:
                tks.add(t)
        for k in writes:
            t = self.lastw.get(k)
            if t is not None:
                tks.add(t)
            for t in self.readers.get(k, ()):
                tks.add(t)
        if dma:
            ticket, inc, prev = self._dma_ticket(eng)
            if prev is not None:
                tks.add(prev)
        elif cc:
            st = [self._newsem("cc"), 1]
            ticket, inc = (id(st[0]), st[1], "dma"), (st[0], None)
        else:
            ticket, inc = self._compute_ticket(eng)
        need = {}
        for (sid, val, prod) in tks:
            if prod == "pe" and eng == "pe":
                continue
            if self.seen[eng].get(sid, 0) >= val:
                continue
            if need.get(sid, 0) < val:
                need[sid] = val
        for sid, val in need.items():
            self.seen[eng][sid] = val
        if not (dma or cc):
            pass
        self.q[eng].append(([(self.sems[sid], val) for sid, val in need.items()], fns, inc))
        for k in reads:
            self.readers.setdefault(k, []).append(ticket)
        for k in writes:
            self.lastw[k] = ticket
            self.readers[k] = []
        self.last_ticket[eng if not (dma or cc) else ("dma", ticket[0])] = ticket
        return ticket

    def barrier(self):
        tks = list(self.last_ticket.values())
        for eng in self.ENG:
            need = {}
            for (sid, val, prod) in tks:
                if self.seen[eng].get(sid, 0) >= val:
                    continue
                if need.get(sid, 0) < val:
                    need[sid] = val
            for sid, val in need.items():
                self.seen[eng][sid] = val
            if need:
                self.q[eng].append(([(self.sems[sid], val) for sid, val in need.items()], [], None))
        self.lastw = {}
        self.readers = {}

    def replay(self, block):
        def run(eng):
            def body(e):
                for waits, fns, inc in self.q[eng]:
                    for sem, val in waits:
                        e.wait_ge(sem, val)
                    ins = None
                    for fn in fns:
                        ins = fn(e)
                    if inc is not None and ins is not None:
                        if inc[1] is None:
                            ins.then_inc(inc[0])
                        else:
                            ins.then_inc(inc[0], inc[1])
            return body
        block.tensor(run("pe"))
        block.scalar(run("act"))
        block.vector(run("dve"))
        block.gpsimd(run("pool"))
        block.sync(run("sp"))


class SB:
    def __init__(self, nc):
        self.nc = nc
        self.n = 0

    def at(self, off, shape, dtype):
        self.n += 1
        esz = 4 if dtype == F32 else 2
        nbytes = esz * int(np.prod(shape[1:]))
        assert off % 32 == 0, off
        assert SB_BASE <= off and off + nbytes <= SB_END, (off, nbytes, shape)
        return self.nc.alloc_sbuf_tensor_at(f"sb{self.n}", list(shape), dtype, offset=off)


def alibi_slope(g, h):
    return float(2.0 ** (-8.0 * (g * 16 + h + 1) / 48.0))


def build_program(layers=(0, 1, 2, 3), skip_mixer=False, skip_ffn=False):
    nc = bass.Bass("TRN2", target_bir_lowering=False)
    P = Prog(nc)
    sb = SB(nc)

    used_inputs = []

    class _Lazy:
        def __init__(self, name, shape):
            self.name, self.shape, self.ap_ = name, shape, None

        def get(self):
            if self.ap_ is None:
                self.ap_ = nc.dram_tensor(self.name, list(self.shape), F32, kind="ExternalInput").ap()
                used_inputs.append(self.name)
            return self.ap_

        def __getitem__(self, idx):
            return self.get()[idx]

        def rearrange(self, *a, **k):
            return self.get().rearrange(*a, **k)

    def din(name, shape, dt=F32):
        return _Lazy(name, shape)

    xT = din("xT", [D, T])
    gains_d = din("gains", [128, 128])
    consts_d = din("consts", [128, 1024])
    ropeB_d = din("ropeB", [128, 2 * T])
    ropeC_d = din("ropeC", [128, 4 * T])
    validA_d = din("validA", [128, 72])
    bgains_d = din("bgains", [128, 8])
    cgains_d = din("cgains", [128, 8])
    wg_d = din("wg", [DEPTH * JC * 128, 1024])
    wu_d = din("wu", [DEPTH * JC * 128, 1024])
    wd_d = din("wd", [DEPTH * 2 * 8 * 128, 11 * 128])
    awqkv_d = din("a_wqkv", [2 * 3 * 3 * 8 * 128, 1024])
    awo_d = din("a_wo", [2 * D, D])
    bwqkv_d = din("b_wqkv", [D, 1536])
    bwo_d = din("b_wo", [D, D])
    cwin_d = din("c_win", [D, 672])
    cwuq_d = din("c_wuq", [384, 1536])
    cwukv_d = din("c_wukv", [256, 2048])
    cwo_d = din("c_wo", [D, D])
    out_d = nc.dram_tensor("out", [D, T], F32, kind="ExternalOutput").ap()
    hbuf = nc.dram_tensor("hbuf", [D, T], F32).ap()
    payA4 = [nc.dram_tensor(f"payA{i}", [D, BW], BF16) for i in range(4)]
    gA4 = [nc.dram_tensor(f"gA{i}", [4 * D, BW], BF16) for i in range(4)]
    payBK = nc.dram_tensor("payBK", [256, T], BF16)
    gBK = nc.dram_tensor("gBK", [4 * 256, T], BF16)
    payBV = nc.dram_tensor("payBV", [256, T], BF16)
    gBV = nc.dram_tensor("gBV", [4 * 256, T], BF16)
    RG = [[0, 1, 2, 3], [4, 5, 6, 7]]
    sv = {}

    o = SB_BASE
    R_Y = o; o += 32768
    R_AO = o; o += 32768
    R_BIG = o; o += 114688
    R_W = o; o += 17408
    R_M = o
    assert SB_END - R_M >= 13000, SB_END - R_M

    yT = sb.at(R_Y, [128, KC, T], BF16)
    aoT = sb.at(R_AO, [128, KC, T], BF16)
    m = R_M
    gains = sb.at(m, [128, 128], F32); m += 512
    ones_bf = sb.at(m, [128, 128], BF16); m += 256
    ones_f = sb.at(m, [128, 64], F32); m += 256
    rstd = [sb.at(m + i * 2048, [128, BW], F32) for i in range(2)]; m += 4096
    sdt = [sb.at(m + i * 2048, [128, BW], F32) for i in range(2)]; m += 4096
    bgains = sb.at(m, [128, 8], F32); m += 32
    cgains = sb.at(m, [128, 8], F32); m += 32
    validA = sb.at(m, [128, 72], F32); m += 288
    rh_ = sb.at(m, [128, BW], BF16); m += 1024
    rl_ = sb.at(m, [128, BW], BF16); m += 1024
    assert m <= SB_END, m

    def bcast64_a(row_ap, row_key):
        P.add("dve", lambda e: e.tensor_copy(out=rh_[64:65, :], in_=row_ap), reads=[row_key], writes=["rh"])
        P.add("dve", lambda e: e.tensor_tensor(out=rl_[64:65, :], in0=row_ap, in1=rh_[64:65, :], op=ALU.subtract),
              reads=[row_key, "rh"], writes=["rl"])

    def bcast64_b(zb):
        mm_group(zb, ps[zb][0:64, :], [(ones_bf[64:65, 0:64], rh_[64:65, :]), (ones_bf[64:65, 0:64], rl_[64:65, :])],
                 reads=["rh", "rl", "ones_bf"])

    ps = [nc.alloc_psum_tensor(f"ps{i}", [128, BW], F32) for i in range(8)]
    bank_rr = [0]

    def nbank(lo=0, hi=8):
        b = lo + bank_rr[0] % (hi - lo)
        bank_rr[0] += 1
        return b

    def mm_group(bank, out_ap, pairs, reads, extra_writes=()):
        n = len(pairs)
        fns = []
        for i, (l, r) in enumerate(pairs):
            fns.append(lambda e, l=l, r=r, i=i: e.matmul(out_ap, l, r, start=(i == 0), stop=(i == n - 1)))
        return P.add("pe", fns, reads=reads, writes=[("ps", bank)] + list(extra_writes))

    def gcol(kind, layer, k):
        c = (kind * 4 + layer) * 8 + k
        return gains[:, c:c + 1]

    P.add("sp", lambda e: e.dma_start(out=gains[:], in_=gains_d[:, :]), writes=["gains"], dma=True)
    P.add("sp", lambda e: e.dma_start(out=bgains[:], in_=bgains_d[:, :]), writes=["bgains"], dma=True)
    P.add("sp", lambda e: e.dma_start(out=cgains[:], in_=cgains_d[:, :]), writes=["cgains"], dma=True)
    P.add("sp", lambda e: e.dma_start(out=validA[:], in_=validA_d[:, :]), writes=["validA"], dma=True)
    P.add("dve", lambda e: e.memset(ones_bf[:], 1.0), writes=["ones_bf"])
    P.add("dve", lambda e: e.memset(ones_f[:], 1.0), writes=["ones_f"])

    xT3 = xT.rearrange("(k p) t -> p k t", p=128)
    hb3 = hbuf.rearrange("(k p) t -> p k t", p=128)
    out3 = out_d.rearrange("(k p) t -> p k t", p=128)

    stat_rr = [0]

    def rms_rstd(sq_aps, sq_keys, nfeat, npart=128):
        slot = stat_rr[0] % 2
        stat_rr[0] += 1
        bank = nbank()
        mm_group(bank, ps[bank][0:npart, :], [(ones_bf[:, 0:npart], a) for a in sq_aps],
                 reads=list(sq_keys) + ["ones_bf"])
        P.add("act", lambda e: e.activation(out=sdt[slot][0:npart, :], in_=ps[bank][0:npart, :], func=AF.Ln,
                                            bias=EPS, scale=1.0 / nfeat),
              reads=[("ps", bank)], writes=[("sd", slot)])
        P.add("act", lambda e: e.activation(out=rstd[slot][0:npart, :], in_=sdt[slot][0:npart, :], func=AF.Exp, scale=-0.5),
              reads=[("sd", slot)], writes=[("rstd", slot)])
        return slot

    def prologue_only(blk, hblk, sqb, layer, kind_pre, hsrc3, first_store):
        bs = slice(blk * BW, (blk + 1) * BW)
        P.add("sp", lambda e: e.dma_start(out=hblk[:], in_=hsrc3[:, :, bs]), reads=[("hb", blk)],
              writes=[("hblk", id(hblk)), ("hblkA", id(hblk)), ("hblkB", id(hblk))], dma=True)
        if first_store:
            P.add("sp", lambda e: e.dma_start(out=hb3[:, :, bs], in_=hblk[:]), reads=[("hblk", id(hblk)), ("hblkA", id(hblk)), ("hblkB", id(hblk))],
                  writes=[("hb", blk)], dma=True)
        pre_norm(blk, hblk, sqb, layer, kind_pre)

    def pre_norm(blk, hblk, sqb, layer, kind_pre):
        bs = slice(blk * BW, (blk + 1) * BW)
        P.add("act", lambda e: e.activation(out=sqb[:], in_=hblk[:], func=AF.Square),
              reads=[("hblk", id(hblk)), ("hblkA", id(hblk)), ("hblkB", id(hblk))], writes=[("sq", id(sqb))])
        slot = rms_rstd([sqb[:, k, :] for k in range(KC)], [("sq", id(sqb))], D)
        for k in range(KC):
            P.add("dve", lambda e, k=k: e.scalar_tensor_tensor(
                out=yT[:, k, bs], in0=hblk[:, k, :], scalar=gcol(kind_pre, layer, k), in1=rstd[slot][:],
                op0=ALU.mult, op1=ALU.mult),
                reads=[("hblk", id(hblk)), ("hblkA", id(hblk)), ("hblkB", id(hblk)), ("rstd", slot), "gains"], writes=[("yT", blk, k)])
        if kind_pre == 0 and layer % 3 == 0:
            P.add("sp", lambda e: e.dma_start(out=payA4[blk].ap().rearrange("(k p) t -> p k t", p=128), in_=yT[:, :, bs]),
                  reads=[("yT", blk, k) for k in range(KC)], writes=[("payA", blk)], dma=True)
            P.add("pool", lambda e: e.collective_compute("AllGather", ALU.bypass, replica_groups=RG,
                                                         ins=[payA4[blk].ap().opt()], outs=[gA4[blk].ap().opt()]),
                  reads=[("payA", blk)], writes=[("gA", blk)], cc=True)

    def epilogue(blk, osb, sqo, hblk, sqb, layer, kind_post, next_pre, final, split=False):
        bs = slice(blk * BW, (blk + 1) * BW)
        P.add("sp", lambda e: e.dma_start(out=hblk[:], in_=hb3[:, :, bs]), reads=[("hb", blk)],
              writes=[("hblk", id(hblk)), ("hblkA", id(hblk)), ("hblkB", id(hblk))], dma=True)
        slot = rms_rstd([sqo[:, k, :] for k in range(KC)], [("sqo", id(sqo))], D)
        for k in range(KC):
            P.add("dve", lambda e, k=k: e.scalar_tensor_tensor(
                out=osb[:, k, :], in0=osb[:, k, :], scalar=gcol(kind_post, layer, k), in1=rstd[slot][:],
                op0=ALU.mult, op1=ALU.mult),
                reads=[("rstd", slot), "gains"], writes=[("osb", id(osb))])
        P.add("pool", lambda e: e.tensor_tensor(out=hblk[:, 0:2, :], in0=hblk[:, 0:2, :], in1=osb[:, 0:2, :], op=ALU.add),
              reads=[("osb", id(osb)), ("hblk", id(hblk))], writes=[("hblkA", id(hblk))])
        P.add("dve", lambda e: e.tensor_tensor(out=hblk[:, 2:8, :], in0=hblk[:, 2:8, :], in1=osb[:, 2:8, :], op=ALU.add),
              reads=[("osb", id(osb)), ("hblk", id(hblk))], writes=[("hblkB", id(hblk))])
        dst = out3 if final else hb3
        P.add("sp", lambda e: e.dma_start(out=dst[:, :, bs], in_=hblk[:]), reads=[("hblk", id(hblk)), ("hblkA", id(hblk)), ("hblkB", id(hblk))],
              writes=[("hb", blk)] if not final else [("outd", blk)], dma=True)
        if next_pre is not None and not split:
            pre_norm(blk, hblk, sqb, next_pre[0], next_pre[1])

    def out_proj_epilogue(w_sb, layer, kind_post, next_pre, final, scr):
        osbs, sqos, hblks, sqbs = scr

        def wo_blk(blk):
            bs = slice(blk * BW, (blk + 1) * BW)
            osb, sqo = osbs[blk % 2], sqos[blk % 2]
            for n in range(KC):
                bank = nbank()
                mm_group(bank, ps[bank][:], [(w_sb[:, k, n * 128:(n + 1) * 128], aoT[:, k, bs]) for k in range(KC)],
                         reads=[("ao", blk, k) for k in range(KC)] + ["wo"])
                P.add("act", lambda e, n=n, bank=bank, osb=osb: e.activation(out=osb[:, n, :], in_=ps[bank][:], func=AF.Copy),
                      reads=[("ps", bank)], writes=[("osb", id(osb))])
                P.add("act", lambda e, n=n, bank=bank, sqo=sqo: e.activation(out=sqo[:, n, :], in_=ps[bank][:], func=AF.Square),
                      reads=[("ps", bank)], writes=[("sqo", id(sqo))])

        def post(blk):
            epilogue(blk, osbs[blk % 2], sqos[blk % 2], hblks[blk % 2], sqbs[blk % 2], layer, kind_post, next_pre, final, split=True)

        def pre(blk):
            if next_pre is not None:
                pre_norm(blk, hblks[blk % 2], sqbs[blk % 2], next_pre[0], next_pre[1])

        wo_blk(0)
        wo_blk(1)
        post(0)
        for blk in range(1, NB):
            if blk + 1 < NB:
                wo_blk(blk + 1)
            post(blk)
            pre(blk - 1)
        pre(NB - 1)

    def epi_scratch(base):
        o = base
        osbs = [sb.at(o + i * 16384, [128, KC, BW], F32) for i in range(2)]; o += 32768
        hblks = [sb.at(o + i * 16384, [128, KC, BW], F32) for i in range(2)]; o += 32768
        sqos = [sb.at(o + i * 8192, [128, KC, BW], BF16) for i in range(2)]; o += 16384
        sqbs = [sb.at(o + i * 8192, [128, KC, BW], BF16) for i in range(2)]; o += 16384
        return (osbs, sqos, hblks, sqbs), o

    def ffn(layer, next_pre, final):
        P.barrier()
        o = R_BIG
        act = sb.at(o, [128, 11, T], BF16); o += 45056
        oall = sb.at(o, [128, KC, T], F32); o += 65536
        assert o <= R_BIG + 114688
        sg = [sb.at(R_AO + i * 2048, [128, BW], F32) for i in range(2)]
        wgu = [sb.at(R_W + i * 4096, [128, 2, KC, 128], BF16) for i in range(3)]
        wds = [sb.at(R_W + 12288 + i * 2816, [128, 11, 128], BF16) for i in range(1)]
        wd2 = sb.at(R_AO + 4096, [128, 11, 128], BF16)
        wdl = [wds[0], wd2]
        for grp in range(2):
            for jj in range(11):
                j = grp * 11 + jj
                wsl = wgu[j % 3]
                r0 = (layer * JC + j) * 128
                P.add("pool", lambda e, wsl=wsl, r0=r0: e.dma_start(
                    out=wsl[:, 0, :, :], in_=wg_d[r0:r0 + 128, :].rearrange("p (k n) -> p k n", k=KC)),
                    writes=[("wgu", j % 3, 0)], dma=True)
                P.add("pool", lambda e, wsl=wsl, r0=r0: e.dma_start(
                    out=wsl[:, 1, :, :], in_=wu_d[r0:r0 + 128, :].rearrange("p (k n) -> p k n", k=KC)),
                    writes=[("wgu", j % 3, 1)], dma=True)
                for blk in range(NB):
                    bs = slice(blk * BW, (blk + 1) * BW)
                    bg = nbank()
                    mm_group(bg, ps[bg][:], [(wsl[:, 0, k, :], yT[:, k, bs]) for k in range(KC)],
                             reads=[("yT", blk, k) for k in range(KC)] + [("wgu", j % 3, 0)])
                    bu = nbank()
                    mm_group(bu, ps[bu][:], [(wsl[:, 1, k, :], yT[:, k, bs]) for k in range(KC)],
                             reads=[("yT", blk, k) for k in range(KC)] + [("wgu", j % 3, 1)])
                    s = sg[(j * NB + blk) % 2]
                    P.add("act", lambda e, s=s, bg=bg: e.activation(out=s[:], in_=ps[bg][:], func=AF.Silu),
                          reads=[("ps", bg)], writes=[("sg", id(s))])
                    P.add("dve", lambda e, s=s, bu=bu, jj=jj, bs=bs: e.tensor_tensor(
                        out=act[:, jj, bs], in0=ps[bu][:], in1=s[:], op=ALU.mult),
                        reads=[("ps", bu), ("sg", id(s))], writes=[("act", jj, bs.start)])
            for n in range(KC):
                wsl = wdl[n % 2]
                r0 = ((layer * 2 + grp) * 8 + n) * 128
                P.add("pool", lambda e, wsl=wsl, r0=r0: e.dma_start(
                    out=wsl[:], in_=wd_d[r0:r0 + 128, :].rearrange("p (j n) -> p j n", j=11)),
                    writes=[("wd", n % 2)], dma=True)
                for blk in range(NB):
                    bs = slice(blk * BW, (blk + 1) * BW)
                    bank = nbank()
                    mm_group(bank, ps[bank][:], [(wsl[:, jj, :], act[:, jj, bs]) for jj in range(11)],
                             reads=[("act", jj, bs.start) for jj in range(11)] + [("wd", n % 2)])
                    if grp == 0:
                        P.add("act", lambda e, n=n, bs=bs, bank=bank: e.activation(
                            out=oall[:, n, bs], in_=ps[bank][:], func=AF.Copy),
                            reads=[("ps", bank)], writes=[("oall", n, bs.start)])
                    else:
                        P.add("dve", lambda e, n=n, bs=bs, bank=bank: e.tensor_tensor(
                            out=oall[:, n, bs], in0=oall[:, n, bs], in1=ps[bank][:], op=ALU.add),
                            reads=[("ps", bank)], writes=[("oall", n, bs.start)])
        P.barrier()
        o = R_BIG
        hblks = [sb.at(o + i * 16384, [128, KC, BW], F32) for i in range(2)]; o += 32768
        hblks.append(sb.at(R_W, [128, KC, BW], F32))
        sqos = [sb.at(R_AO + 8192 + i * 8192, [128, KC, BW], BF16) for i in range(2)]
        sqbs = [sb.at(o + i * 8192, [128, KC, BW], BF16) for i in range(1)]; o += 8192
        sqbs.append(sb.at(R_AO + 24576, [128, KC, BW], BF16))
        assert o <= R_BIG + 45056
        def fpost(blk):
            bs = slice(blk * BW, (blk + 1) * BW)
            sqo = sqos[blk % 2]
            P.add("act", lambda e: e.activation(out=sqo[:], in_=oall[:, :, bs], func=AF.Square),
                  writes=[("sqo", id(sqo))])
            epilogue(blk, _View(oall, bs), sqo, hblks[blk % 3], sqbs[blk % 2], layer, 3, next_pre, final, split=True)

        def fpre(blk):
            if next_pre is not None:
                pre_norm(blk, hblks[blk % 3], sqbs[blk % 2], next_pre[0], next_pre[1])

        fpost(0)
        for blk in range(1, NB):
            fpost(blk)
            fpre(blk - 1)
        fpre(NB - 1)

    class _View:
        def __init__(self, t, bs):
            self.t, self.bs = t, bs

        def __getitem__(self, idx):
            if idx == slice(None):
                return self.t[:, :, self.bs]
            a, b, c = idx
            assert c == slice(None)
            return self.t[a, b, self.bs]

    pt_rr = [0]

    def attn_core(nkt, qk_pair, pv_ops, scale, pT, kv_reads, q_reads, acc_banks, npart_s=128, hook=None, zsum=None):
        sbanks = [None] * nkt
        pts = [None] * nkt

        def qk(t):
            bk = nbank(0, 4)
            sbanks[t] = bk
            l, r = qk_pair(t)
            mm_group(bk, ps[bk][0:npart_s, :], [(l, r)], reads=list(kv_reads) + list(q_reads))
            i = pt_rr[0] % len(pT)
            pt_rr[0] += 1
            pts[t] = i
            P.add("act", lambda e: e.activation(out=pT[i][0:npart_s, :], in_=ps[bk][0:npart_s, :], func=AF.Exp,
                                                scale=scale),
                  reads=[("ps", bk)], writes=[("pT", i)])

        def pv(t):
            i = pts[t]
            fns = []
            for (bank, out_ap, lhs_fn) in pv_ops:
                fns.append(lambda e, out_ap=out_ap, lhs_fn=lhs_fn: e.matmul(
                    out_ap, lhs_fn(t), pT[i][0:npart_s, :], start=(t == 0), stop=(t == nkt - 1)))
            P.add("pe", fns, reads=[("pT", i)] + list(kv_reads), writes=[("ps", b) for b in acc_banks])
            if zsum is not None:
                zb_, abuf, zbuf = zsum
                if zq and zq[0][0] <= t - 2:
                    zemit()
                if t % 2 == 1:
                    a = abuf[(t // 2) % 2]
                    i0 = pts[t - 1]
                    P.add("dve", lambda e: e.tensor_tensor(out=a[:], in0=pT[i0][:], in1=pT[i][:], op=ALU.add),
                          reads=[("pT", i0), ("pT", i)], writes=[("zab", id(a))])
                if t % 4 == 3:
                    z = zbuf[(t // 4) % 2]
                    P.add("dve", lambda e: e.tensor_tensor(out=z[:], in0=abuf[0][:], in1=abuf[1][:], op=ALU.add),
                          reads=[("zab", id(abuf[0])), ("zab", id(abuf[1]))], writes=[("zzb", id(z))])
                    zq.append((t, z, t // 4))

        zq = []

        def zemit():
            zb_, abuf, zbuf = zsum
            t_, z, g_ = zq.pop(0)
            ng = nkt // 4
            P.add("pe", lambda e: e.matmul(ps[zb_][:], ones_bf[:], z[:], start=(g_ == 0), stop=(g_ == ng - 1)),
                  reads=[("zzb", id(z)), "ones_bf"], writes=[("ps", zb_)])

        LA = 3
        for t in range(nkt + LA):
            if t < nkt:
                qk(t)
            if t >= LA:
                pv(t - LA)
            if hook is not None:
                hook(t)
        while zq:
            zemit()

    def mixer_b(layer, next_pre, final):
        P.barrier()
        o = R_BIG
        qT = sb.at(o, [128, 8, T], BF16); o += 32768
        kTo = sb.at(o, [128, 2, T], BF16); o += 8192
        Vo = sb.at(o, [128, 2, T], BF16); o += 8192
        o2 = o
        wqkv = sb.at(o, [128, KC, 1536], BF16); o += 24576
        qf = [sb.at(o + i * 2048, [128, BW], F32) for i in range(2)]; o += 4096
        sqh = [sb.at(o + i * 1024, [128, BW], BF16) for i in range(2)]; o += 2048
        ta = [sb.at(o + i * 2048, [128, BW], F32) for i in range(2)]; o += 4096
        tb = [sb.at(o + i * 2048, [128, BW], F32) for i in range(2)]; o += 4096
        rc = [sb.at(o + i * 2048, [128, BW], F32) for i in range(2)]; o += 4096
        rs = [sb.at(o + i * 2048, [128, BW], F32) for i in range(2)]; o += 4096
        permB = sb.at(o, [128, 128], F32); o += 512
        assert o <= R_BIG + 114688, o - R_BIG
        wo_sb = sb.at(R_W, [128, KC, D], BF16)
        P.add("pool", lambda e: e.dma_start(out=wqkv[:], in_=bwqkv_d.rearrange("(k p) n -> p k n", p=128)),
              writes=["wqkv"], dma=True)
        P.add("pool", lambda e: e.dma_start(out=wo_sb[:], in_=bwo_d.rearrange("(k p) n -> p k n", p=128)),
              writes=["wo"], dma=True)
        P.add("sp", lambda e: e.dma_start(out=permB[:], in_=consts_d[:, 0:128]), writes=["permB"], dma=True)
        cnt = 0
        for blk in range(NB):
            bs = slice(blk * BW, (blk + 1) * BW)
            s1 = blk % 2
            P.add("sp", lambda e, s1=s1, bs=bs: e.dma_start(out=rc[s1][:], in_=ropeB_d[:, bs]),
                  writes=[("rc", s1)], dma=True)
            P.add("sp", lambda e, s1=s1, blk=blk: e.dma_start(out=rs[s1][:], in_=ropeB_d[:, T + blk * BW:T + (blk + 1) * BW]),
                  writes=[("rs", s1)], dma=True)
            for c in range(10):
                s2 = cnt % 2
                cnt += 1
                bank = nbank()
                mm_group(bank, ps[bank][:], [(wqkv[:, k, c * 128:(c + 1) * 128], yT[:, k, bs]) for k in range(KC)],
                         reads=[("yT", blk, k) for k in range(KC)] + ["wqkv"])
                P.add("act", lambda e, s2=s2, bank=bank: e.activation(out=qf[s2][:], in_=ps[bank][:], func=AF.Copy),
                      reads=[("ps", bank)], writes=[("qf", s2)])
                P.add("act", lambda e, s2=s2, bank=bank: e.activation(out=sqh[s2][:], in_=ps[bank][:], func=AF.Square),
                      reads=[("ps", bank)], writes=[("sqh", s2)])
                slot = rms_rstd([sqh[s2][:]], [("sqh", s2)], 128)
                b2 = nbank()
                mm_group(b2, ps[b2][:], [(permB[:], qf[s2][:])], reads=[("qf", s2), "permB"])
                gi = 0 if c < 8 else 2
                P.add("dve", lambda e, s2=s2, s1=s1, gi=gi: e.scalar_tensor_tensor(
                    out=ta[s2][:], in0=qf[s2][:], scalar=bgains[:, gi:gi + 1], in1=rc[s1][:], op0=ALU.mult, op1=ALU.mult),
                    reads=[("qf", s2), ("rc", s1), "bgains"], writes=[("ta", s2)])
                P.add("dve", lambda e, s2=s2, s1=s1, gi=gi, b2=b2: e.scalar_tensor_tensor(
                    out=tb[s2][:], in0=ps[b2][:], scalar=bgains[:, gi + 1:gi + 2], in1=rs[s1][:], op0=ALU.mult, op1=ALU.mult),
                    reads=[("ps", b2), ("rs", s1), "bgains"], writes=[("tb", s2)])
                P.add("pool", lambda e, s2=s2: e.tensor_tensor(out=ta[s2][:], in0=ta[s2][:], in1=tb[s2][:], op=ALU.add),
                      reads=[("tb", s2)], writes=[("ta", s2)])
                dest = qT[:, c, bs] if c < 8 else kTo[:, c - 8, bs]
                dkey = ("qT", c, blk) if c < 8 else ("kTo", c - 8, blk)
                P.add("dve", lambda e, s2=s2, slot=slot, dest=dest: e.tensor_tensor(
                    out=dest, in0=ta[s2][:], in1=rstd[slot][:], op=ALU.mult),
                    reads=[("ta", s2), ("rstd", slot)], writes=[dkey])
            for tt in range(4):
                ti = blk * 4 + tt
                bank = nbank()
                mm_group(bank, ps[bank][:, 0:256],
                         [(yT[:, k, ti * 128:(ti + 1) * 128], wqkv[:, k, 1280:1536]) for k in range(KC)],
                         reads=[("yT", blk, k) for k in range(KC)] + ["wqkv"])
                P.add("act", lambda e, ti=ti, bank=bank: e.activation(
                    out=Vo[:, :, ti * 128:(ti + 1) * 128], in_=ps[bank][:, 0:256].rearrange("p (k n) -> p k n", k=2),
                    func=AF.Copy), reads=[("ps", bank)], writes=[("Vo", ti)])
        if KSTOP == 'b1':
            P.barrier()
            scr, _ = epi_scratch(R_BIG)
            out_proj_epilogue(wo_sb, layer, 1, next_pre, final, scr)
            return
        P.add("sp", lambda e: e.dma_start(out=payBK.ap().rearrange("(k d) t -> d k t", k=2), in_=kTo[:]),
              reads=[("kTo", kv, blk) for kv in range(2) for blk in range(NB)], writes=["payBK"], dma=True)
        P.add("sp", lambda e: e.dma_start(out=payBV.ap().rearrange("(k p) f -> p k f", k=2), in_=Vo[:]),
              reads=[("Vo", ti) for ti in range(16)], writes=["payBV"], dma=True)
        P.add("pool", lambda e: e.collective_compute("AllGather", ALU.bypass, replica_groups=RG,
                                                     ins=[payBK.ap().opt()], outs=[gBK.ap().opt()]),
              reads=["payBK"], writes=["gBK"], cc=True)
        P.add("pool", lambda e: e.collective_compute("AllGather", ALU.bypass, replica_groups=RG,
                                                     ins=[payBV.ap().opt()], outs=[gBV.ap().opt()]),
              reads=["payBV"], writes=["gBV"], cc=True)
        o = o2
        Kk = sb.at(o, [128, 4, T], BF16); o += 16384
        Vk = sb.at(o, [128, 4, T], BF16); o += 16384
        pT = [sb.at(o + i * 1024, [128, BW], BF16) for i in range(6)]; o += 6144
        rz = [sb.at(o + i * 2048, [128, BW], F32) for i in range(2)]; o += 4096
        zab = [sb.at(o + i * 1024, [128, BW], BF16) for i in range(2)]; o += 2048
        zzb = [sb.at(o + i * 1024, [128, BW], BF16) for i in range(2)]; o += 2048
        assert o <= R_BIG + 114688
        P.barrier()
        it = 0
        for kv in range(2):
            P.add("sp", lambda e, kv=kv: e.dma_start(
                out=Kk[:], in_=gBK.ap().rearrange("(r k d) t -> d k r t", k=2, d=128)[:, kv]),
                reads=["gBK"], writes=["Kk"], dma=True)
            P.add("sp", lambda e, kv=kv: e.dma_start(
                out=Vk[:], in_=gBV.ap().rearrange("(r k p) f -> p k r f", k=2, p=128)[:, kv]),
                reads=["gBV"], writes=["Vk"], dma=True)
            for h in range(kv * 4, kv * 4 + 4):
                if KSTOP == 'b2' or (KSTOP == 'b3' and h > 0):
                    break
                for qb in range(NB if KSTOP != 'b3' else 1):
                    qs = slice(qb * BW, (qb + 1) * BW)
                    ob = 4 + (it % 2) * 2
                    zb = ob + 1
                    it += 1
                    attn_core(
                        64,
                        lambda t, h=h, qs=qs: (Kk[:, t // 16, (t % 16) * 128:(t % 16 + 1) * 128], qT[:, h, qs]),
                        [(ob, ps[ob][:], lambda t: Vk[:, t // 16, (t % 16) * 128:(t % 16 + 1) * 128])],
                        float(128 ** -0.5), pT, ["Kk", "Vk"], [("qT", h, qb)], [ob], zsum=(zb, zab, zzb))
                    rzi = rz[it % 2]
                    P.add("dve", lambda e, rzi=rzi, zb=zb: e.tensor_copy(out=rzi[:], in_=ps[zb][:]),
                          reads=[("ps", zb)], writes=[("rz", id(rzi))])
                    P.add("dve", lambda e, rzi=rzi: e.reciprocal(out=rzi[:], in_=rzi[:]),
                          reads=[("rz", id(rzi))], writes=[("rz", id(rzi))])
                    P.add("dve", lambda e, rzi=rzi, ob=ob, h=h, qs=qs: e.tensor_tensor(
                        out=aoT[:, h, qs], in0=ps[ob][:], in1=rzi[:], op=ALU.mult),
                        reads=[("ps", ob), ("rz", id(rzi))], writes=[("ao", qb, h)])
        P.barrier()
        scr, _ = epi_scratch(R_BIG)
        out_proj_epilogue(wo_sb, layer, 1, next_pre, final, scr)

    mixers[1] = mixer_b

    def mixer_c(layer, next_pre, final):
        P.barrier()
        o = R_BIG
        cqn = sb.at(o, [128, 3, T], BF16); o += 12288
        o_keep = o
        ckvo = sb.at(o, [128, 2, T], BF16); o += 8192
        kro = sb.at(o, [128, T], BF16); o += 4096
        win = sb.at(o, [128, KC, 672], BF16); o += 10752
        cf = [sb.at(o + i * 6144, [128, 3, BW], F32) for i in range(2)]; o += 12288
        sq = [sb.at(o + i * 3072, [128, 3, BW], BF16) for i in range(2)]; o += 6144
        ta = [sb.at(o + i * 2048, [128, BW], F32) for i in range(2)]; o += 4096
        tb = [sb.at(o + i * 2048, [128, BW], F32) for i in range(2)]; o += 4096
        rc = [sb.at(o + i * 2048, [128, BW], F32) for i in range(2)]; o += 4096
        rs = [sb.at(o + i * 2048, [128, BW], F32) for i in range(2)]; o += 4096
        perm32 = sb.at(o, [128, 32], F32); o += 128
        perm96 = sb.at(o, [128, 96], F32); o += 384
        assert o <= R_BIG + 114688
        wuq = sb.at(R_W, [128, 3, 1536], BF16)
        wukv = sb.at(R_W + 9216, [128, 2, 2048], BF16)
        P.add("pool", lambda e: e.dma_start(out=win[:], in_=cwin_d.rearrange("(k p) n -> p k n", p=128)),
              writes=["win"], dma=True)
        P.add("pool", lambda e: e.dma_start(out=wuq[:], in_=cwuq_d.rearrange("(k p) n -> p k n", p=128)),
              writes=["wuq"], dma=True)
        P.add("pool", lambda e: e.dma_start(out=wukv[:], in_=cwukv_d.rearrange("(k p) n -> p k n", p=128)),
              writes=["wukv"], dma=True)
        P.add("sp", lambda e: e.dma_start(out=perm32[:], in_=consts_d[:, 256:288]), writes=["perm32"], dma=True)
        P.add("sp", lambda e: e.dma_start(out=perm96[:], in_=consts_d[:, 128:224]), writes=["perm96"], dma=True)
        cnt = 0
        for blk in range(NB):
            bs = slice(blk * BW, (blk + 1) * BW)
            s1 = blk % 2
            P.add("sp", lambda e, s1=s1, blk=blk: e.dma_start(
                out=rc[s1][0:32, :], in_=ropeC_d[0:32, 2 * T + blk * BW:2 * T + (blk + 1) * BW]),
                writes=[("rc", s1)], dma=True)
            P.add("sp", lambda e, s1=s1, blk=blk: e.dma_start(
                out=rs[s1][0:32, :], in_=ropeC_d[0:32, 3 * T + blk * BW:3 * T + (blk + 1) * BW]),
                writes=[("rs", s1)], dma=True)
            for (nch, c0, g0, dst, dname) in ((3, 0, 0, cqn, "cqn"), (2, 384, 3, ckvo, "ckvo")):
                s2 = cnt % 2
                cnt += 1
                for i in range(nch):
                    bank = nbank()
                    mm_group(bank, ps[bank][:], [(win[:, k, c0 + i * 128:c0 + (i + 1) * 128], yT[:, k, bs]) for k in range(KC)],
                             reads=[("yT", blk, k) for k in range(KC)] + ["win"])
                    P.add("act", lambda e, s2=s2, i=i, bank=bank: e.activation(out=cf[s2][:, i, :], in_=ps[bank][:], func=AF.Copy),
                          reads=[("ps", bank)], writes=[("cf", s2, i)])
                    P.add("act", lambda e, s2=s2, i=i, bank=bank: e.activation(out=sq[s2][:, i, :], in_=ps[bank][:], func=AF.Square),
                          reads=[("ps", bank)], writes=[("csq", s2, i)])
                slot = rms_rstd([sq[s2][:, i, :] for i in range(nch)], [("csq", s2, i) for i in range(nch)], nch * 128)
                for i in range(nch):
                    P.add("dve", lambda e, s2=s2, i=i, slot=slot, dst=dst, g0=g0, bs=bs: e.scalar_tensor_tensor(
                        out=dst[:, i, bs], in0=cf[s2][:, i, :], scalar=cgains[:, g0 + i:g0 + i + 1], in1=rstd[slot][:],
                        op0=ALU.mult, op1=ALU.mult),
                        reads=[("cf", s2, i), ("rstd", slot), "cgains"], writes=[(dname, i, blk)])
            bank = nbank()
            mm_group(bank, ps[bank][0:32, :], [(win[:, k, 640:672], yT[:, k, bs]) for k in range(KC)],
                     reads=[("yT", blk, k) for k in range(KC)] + ["win"])
            P.add("act", lambda e, s1=s1, bank=bank: e.activation(out=ta[s1][0:32, :], in_=ps[bank][0:32, :], func=AF.Copy),
                  reads=[("ps", bank)], writes=[("ta", s1)])
            b2 = nbank()
            mm_group(b2, ps[b2][0:32, :], [(perm32[0:32, :], ta[s1][0:32, :])], reads=[("ta", s1), "perm32"])
            P.add("dve", lambda e, s1=s1, b2=b2: e.tensor_tensor(out=tb[s1][0:32, :], in0=ps[b2][0:32, :], in1=rs[s1][0:32, :], op=ALU.mult),
                  reads=[("ps", b2), ("rs", s1)], writes=[("tb", s1)])
            P.add("dve", lambda e, s1=s1: e.tensor_tensor(out=ta[s1][0:32, :], in0=ta[s1][0:32, :], in1=rc[s1][0:32, :], op=ALU.mult),
                  reads=[("rc", s1)], writes=[("ta", s1)])
            P.add("pool", lambda e, s1=s1, bs=bs: e.tensor_tensor(out=kro[0:32, bs], in0=ta[s1][0:32, :], in1=tb[s1][0:32, :], op=ALU.add),
                  reads=[("ta", s1), ("tb", s1)], writes=[("kro", blk)])
        if KSTOP == 'c1':
            P.barrier()
            wo_sb = sb.at(R_W, [128, KC, D], BF16)
            P.add("pool", lambda e: e.dma_start(out=wo_sb[:], in_=cwo_d.rearrange("(k p) n -> p k n", p=128)),
                  writes=["wo"], dma=True)
            scr, _ = epi_scratch(R_BIG)
            out_proj_epilogue(wo_sb, layer, 1, next_pre, final, scr)
            return
        P.add("sp", lambda e: e.dma_start(out=payBK.ap().rearrange("(i p) t -> p i t", p=128), in_=ckvo[:]),
              reads=[("ckvo", i, blk) for i in range(2) for blk in range(NB)], writes=["payBK"], dma=True)
        P.add("sp", lambda e: e.dma_start(out=payBV.ap().rearrange("(i p) t -> p i t", p=128)[:, 0, :], in_=kro[:]),
              reads=[("kro", blk) for blk in range(NB)], writes=["payBV"], dma=True)
        P.add("pool", lambda e: e.collective_compute("AllGather", ALU.bypass, replica_groups=RG,
                                                     ins=[payBK.ap().opt()], outs=[gBK.ap().opt()]),
              reads=["payBK"], writes=["gBK"], cc=True)
        P.add("pool", lambda e: e.collective_compute("AllGather", ALU.bypass, replica_groups=RG,
                                                     ins=[payBV.ap().opt()], outs=[gBV.ap().opt()]),
              reads=["payBV"], writes=["gBV"], cc=True)
        P.barrier()
        ckvA = sb.at(R_Y, [128, 2, S], BF16)
        o = o_keep
        Kh = [sb.at(o + i * 16384, [128, S], BF16) for i in range(2)]; o += 32768
        Vh = [sb.at(o + i * 8320, [128, 64, 65], BF16) for i in range(2)]; o += 16640
        qh = [sb.at(o + i * 4096, [128, T], BF16) for i in range(2)]; o += 8192
        pT = [sb.at(o + i * 1024, [128, BW], BF16) for i in range(6)]; o += 6144
        tq = sb.at(o, [128, 2, T], F32); o += 16384
        qf = [sb.at(o + i * 2048, [128, BW], F32) for i in range(2)]; o += 4096
        tqa = [sb.at(o + i * 2048, [128, BW], F32) for i in range(2)]; o += 4096
        tqb = [sb.at(o + i * 2048, [128, BW], F32) for i in range(2)]; o += 4096
        rzr = [sb.at(o + i * 2048, [128, BW], F32) for i in range(2)]; o += 4096
        zs = [sb.at(o, [128, BW], F32) for i in range(2)]; o += 2048
        perm96b = sb.at(o, [128, 96], F32); o += 384
        assert o <= R_BIG + 114688, o - R_BIG
        P.add("sp", lambda e: e.dma_start(out=perm96b[:], in_=consts_d[:, 128:224]), writes=["perm96"], dma=True)
        P.add("sp", lambda e: e.dma_start(out=tq[:], in_=ropeC_d[:, 0:2 * T].rearrange("p (a t) -> p a t", a=2)),
              writes=["tq"], dma=True)
        for r in range(4):
            P.add("sp", lambda e, r=r: e.dma_start(
                out=ckvA[:, :, r * T:(r + 1) * T], in_=gBK.ap()[r * 256:(r + 1) * 256, :].rearrange("(i p) t -> p i t", p=128)),
                writes=[("ckvA", r)], dma=True)
            for b in range(2):
                P.add("sp", lambda e, r=r, b=b: e.dma_start(
                    out=Kh[b][64:96, r * T:(r + 1) * T], in_=gBV.ap()[r * 256:r * 256 + 32, :]),
                    writes=[("Khr", b, r)], dma=True)
        for b in range(2):
            P.add("pool", lambda e, b=b: e.memset(Vh[b][:], 1.0), writes=[("Vh1", b), ("Vh", b)])
        ckv_reads = [("ckvA", r) for r in range(4)]
        prep_cnt = [0]

        def prep(h):
            b = h % 2
            for blk in range(NB):
                bs = slice(blk * BW, (blk + 1) * BW)
                s2 = prep_cnt[0] % 2
                prep_cnt[0] += 1
                bank = nbank(6, 8)
                mm_group(bank, ps[bank][0:96, :], [(wuq[:, i, h * 96:(h + 1) * 96], cqn[:, i, bs]) for i in range(3)],
                         reads=["wuq"])
                P.add("dve", lambda e, s2=s2, bank=bank: e.tensor_copy(out=qf[s2][0:96, :], in_=ps[bank][0:96, :]),
                      reads=[("ps", bank)], writes=[("qf", s2)])
                b2 = nbank(6, 8)
                mm_group(b2, ps[b2][0:96, :], [(perm96b[0:96, :], qf[s2][0:96, :])], reads=[("qf", s2), "perm96"])
                P.add("dve", lambda e, s2=s2, bs=bs: e.tensor_tensor(out=tqa[s2][0:96, :], in0=qf[s2][0:96, :], in1=tq[0:96, 0, bs], op=ALU.mult),
                      reads=[("qf", s2), "tq"], writes=[("tqa", s2)])
                P.add("dve", lambda e, s2=s2, bs=bs, b2=b2: e.tensor_tensor(out=tqb[s2][0:96, :], in0=ps[b2][0:96, :], in1=tq[0:96, 1, bs], op=ALU.mult),
                      reads=[("ps", b2), "tq"], writes=[("tqb", s2)])
                P.add("pool", lambda e, s2=s2, bs=bs, b=b: e.tensor_tensor(out=qh[b][0:96, bs], in0=tqa[s2][0:96, :], in1=tqb[s2][0:96, :], op=ALU.add),
                      reads=[("tqa", s2), ("tqb", s2)], writes=[("qh", b, blk)])
                yield
            for kb in range(16):
                ks = slice(kb * BW, (kb + 1) * BW)
                bank = nbank(6, 8)
                mm_group(bank, ps[bank][0:64, :], [(wukv[:, i, h * 128:h * 128 + 64], ckvA[:, i, ks]) for i in range(2)],
                         reads=["wukv"] + ckv_reads)
                P.add("dve", lambda e, b=b, ks=ks, bank=bank: e.tensor_copy(out=Kh[b][0:64, ks], in_=ps[bank][0:64, :]),
                      reads=[("ps", bank)], writes=[("Kh", b)])
                yield
            for g8 in range(8):
                bank = nbank(6, 8)
                fns = []
                for u in range(8):
                    tl = g8 * 8 + u
                    for i in range(2):
                        fns.append(lambda e, u=u, tl=tl, i=i, bank=bank: e.matmul(
                            ps[bank][:, u * 64:(u + 1) * 64], ckvA[:, i, tl * 128:(tl + 1) * 128],
                            wukv[:, i, h * 128 + 64:h * 128 + 128], start=(i == 0), stop=(i == 1)))
                P.add("pe", fns, reads=["wukv"] + ckv_reads, writes=[("ps", bank)])
                P.add("dve", lambda e, b=b, g8=g8, bank=bank: e.tensor_copy(
                    out=Vh[b][:, g8 * 8:(g8 + 1) * 8, 0:64], in_=ps[bank][:].rearrange("p (u n) -> p u n", u=8)),
                    reads=[("ps", bank)], writes=[("Vh", b)])
                yield

        it = 0

        def cpull(gen, n=1):
            if gen is None:
                return
            for _ in range(n):
                try:
                    next(gen)
                except StopIteration:
                    return

        cpend = []

        def cnorm(zi, ob, h, qs, qb):
            pb = (h % 2) * 64

            def ph_a():
                P.add("dve", lambda e: e.tensor_copy(out=rzr[zi][64:65, :], in_=ps[ob][64:65, :]),
                      reads=[("ps", ob)], writes=[("rzr", zi)])
                P.add("dve", lambda e: e.reciprocal(out=rzr[zi][64:65, :], in_=rzr[zi][64:65, :]),
                      reads=[("rzr", zi)], writes=[("rzr", zi)])
                bcast64_a(rzr[zi][64:65, :], ("rzr", zi))

            def ph_b():
                zb = nbank(6, 8)
                bcast64_b(zb)
                P.add("dve", lambda e: e.tensor_copy(out=zs[zi][0:64, :], in_=ps[zb][0:64, :]),
                      reads=[("ps", zb)], writes=[("zs", 0)])
                P.add("dve", lambda e: e.tensor_tensor(
                    out=aoT[pb:pb + 64, h // 2, qs], in0=ps[ob][0:64, :], in1=zs[zi][0:64, :], op=ALU.mult),
                    reads=[("ps", ob), ("zs", 0)], writes=[("ao", qb, h // 2, h % 2)])
            return [ph_a, ph_b]

        def crun():
            if not cpend:
                return
            th = cpend[0]
            th.pop(0)()
            if not th:
                cpend.pop(0)

        def chook(t, cgen):
            if t % 8 == 4:
                cpull(cgen, 1)
            if t in (6, 14):
                crun()

        if KSTOP != 'c2':
            cpull(prep(0), 1000)
        for h in range(16):
            if KSTOP in ('c2', 'c3'):
                break
            b = h % 2
            cgen = prep(h + 1) if h + 1 < 16 else None
            for qb in range(NB):
                qs = slice(qb * BW, (qb + 1) * BW)
                ob = 4 + (it % 2)
                it += 1
                attn_core(
                    64,
                    lambda t, b=b, qs=qs: (Kh[b][0:96, t * 128:(t + 1) * 128], qh[b][0:96, qs]),
                    [(ob, ps[ob][0:65, :], lambda t, b=b: Vh[b][:, t, :])],
                    float(96 ** -0.5), pT, [("Kh", b), ("Vh", b), ("Vh1", b)] + [("Khr", b, r) for r in range(4)],
                    [("qh", b, qb)], [ob], hook=(lambda t, cgen=cgen: chook(t, cgen)))
                cpend.append(cnorm(it % 2, ob, h, qs, qb))
                if qb == NB - 1:
                    cpull(cgen, 1000)
        while cpend:
            crun()
        P.barrier()
        wo_sb = sb.at(R_W, [128, KC, D], BF16)
        P.add("pool", lambda e: e.dma_start(out=wo_sb[:], in_=cwo_d.rearrange("(k p) n -> p k n", p=128)),
              writes=["wo"], dma=True)
        scr, _ = epi_scratch(R_BIG)
        out_proj_epilogue(wo_sb, layer, 1, next_pre, final, scr)

    mixers[2] = mixer_c

    def mixer_a(layer, next_pre, final):
        la = layer // 3
        P.barrier()
        P.barrier()
        o = R_BIG
        yw = sb.at(o, [128, KC, 4096], BF16); o += 65536
        qs_ = [sb.at(o + i * 4096, [128, T], BF16) for i in range(2)]; o += 8192
        kb_ = [sb.at(o + i * 8192, [128, 4096], BF16) for i in range(2)]; o += 16384
        vb_ = [sb.at(o + i * 8320, [128, 32, 2, 65], BF16) for i in range(2)]; o += 16640
        tmp = [sb.at(o + i * 1024, [128, 256], F32) for i in range(4)]; o += 4096
        ptl = [sb.at(o + i * 512, [128, 256], BF16) for i in range(5)]; o += 2560
        dist = sb.at(o, [128, 256], F32); o += 1024
        assert o <= R_BIG + 114688, o - R_BIG
        o = R_Y
        acc = sb.at(o, [128, 2, T], F32); o += 16384
        wsl = [sb.at(o + i * 6144, [128, 3, KC, 128], BF16) for i in range(2)]; o += 12288
        rzr = [sb.at(o + i * 2048, [128, BW], F32) for i in range(2)]; o += 4096
        assert o <= R_Y + 32768
        wo_sb = sb.at(R_W, [128, KC, D], BF16)
        P.add("pool", lambda e: e.dma_start(out=wo_sb[:], in_=awo_d[la * D:(la + 1) * D, :].rearrange("(k p) n -> p k n", p=128)),
              writes=["wo"], dma=True)
        P.add("sp", lambda e: e.dma_start(out=dist[:], in_=consts_d[:, 384:640]), writes=["dist"], dma=True)

        def wl(part, blk, o0):
            def f(e):
                if not sv:
                    pid = nc.partition_id()
                    g = pid % 4
                    sv[0] = ((g + 3) % 4) * D
                    sv[1] = g * D
                    sv[2] = ((g + 1) % 4) * D
                return e.dma_start(out=yw[:, :, o0:o0 + BW],
                                   in_=gA4[blk].ap()[bass.ds(sv[part], D), :].rearrange("(k p) t -> p k t", p=128))
            return f

        wl_list = [(1, b_, 1024 + b_ * BW) for b_ in range(4)] + [(0, 2, 0), (0, 3, BW), (2, 0, 3072), (2, 1, 3072 + BW)]
        for (part, b_, o0) in wl_list:
            P.add("pool", wl(part, b_, o0), writes=[("yw", part, b_)], dma=True)
        yw_reads = [("yw", part, b_) for (part, b_, o0) in wl_list]
        steps = [(c, gi) for c in range(8) for gi in range(3)]
        vcol0 = [0, 17, 37]
        evac_rr = [0]

        def evac(out_ap, in_ap, reads, writes, scale=None):
            use_act = (evac_rr[0] % 2 == 0)
            evac_rr[0] += 1
            if use_act:
                if scale is None:
                    P.add("act", lambda e: e.activation(out=out_ap, in_=in_ap, func=AF.Copy), reads=reads, writes=writes)
                else:
                    P.add("act", lambda e: e.activation(out=out_ap, in_=in_ap, func=AF.Copy, scale=scale), reads=reads, writes=writes)
            else:
                if scale is None:
                    P.add("dve", lambda e: e.tensor_copy(out=out_ap, in_=in_ap), reads=reads, writes=writes)
                else:
                    P.add("dve", lambda e: e.tensor_scalar(out=out_ap, in0=in_ap, scalar1=scale, scalar2=None, op0=ALU.mult),
                          reads=reads, writes=writes)

        def proj(si):
            c, gi = steps[si]
            st = si % 2
            d = A_GROUPS[gi][1]
            nqt = 16 // d
            w = wsl[st]
            for s3 in range(3):
                r0 = ((((la * 3 + s3) * 3 + gi) * 8) + c) * 128
                P.add("pool", lambda e, s3=s3, r0=r0: e.dma_start(
                    out=w[:, s3, :, :], in_=awqkv_d[r0:r0 + 128, :].rearrange("p (k n) -> p k n", k=KC)),
                    writes=[("aw", st, s3)], dma=True)
            for blk in range(NB):
                bank = nbank(6, 8)
                mm_group(bank, ps[bank][:], [(w[:, 0, k, :], yw[:, k, 1024 + blk * BW:1024 + (blk + 1) * BW]) for k in range(KC)],
                         reads=yw_reads + [("aw", st, 0)])
                evac(qs_[st][:, blk * BW:(blk + 1) * BW], ps[bank][:], [("ps", bank)], [("aq", st)], scale=0.125)
                yield
            w0 = 1024 - 64 * d
            Lk = 2048 + 128 * d
            j0 = 0
            while j0 < Lk:
                wd_ = min(BW, Lk - j0)
                bank = nbank(6, 8)
                mm_group(bank, ps[bank][:, 0:wd_], [(w[:, 1, k, :], yw[:, k, w0 + j0:w0 + j0 + wd_]) for k in range(KC)],
                         reads=yw_reads + [("aw", st, 1)])
                evac(kb_[st][:, j0:j0 + wd_], ps[bank][:, 0:wd_], [("ps", bank)], [("ak", st)])
                j0 += wd_
                yield
            ntile = d * (nqt + 1)
            for vt in range(ntile):
                r, m_ = vt // (nqt + 1), vt % (nqt + 1)
                ws = 1024 + r + d * (128 * m_ - 64)
                bank = nbank(6, 8)
                mm_group(bank, ps[bank][:, 0:128],
                         [(yw[:, k, ws:ws + 127 * d + 1:d], w[:, 2, k, :]) for k in range(KC)],
                         reads=yw_reads + [("aw", st, 2)])
                vc = vcol0[gi] + vt
                outv = vb_[st][:, vt, :, 0:64]
                inv = ps[bank][:, 0:128].rearrange("p (e n) -> p e n", e=2)
                if vt % 2 == 0:
                    P.add("act", lambda e, outv=outv, inv=inv, vc=vc: e.activation(
                        out=outv, in_=inv, func=AF.Copy, scale=validA[:, vc:vc + 1]),
                        reads=[("ps", bank), "validA"], writes=[("av", st)])
                else:
                    P.add("dve", lambda e, outv=outv, inv=inv, vc=vc: e.tensor_scalar(
                        out=outv, in0=inv, scalar1=validA[:, vc:vc + 1], scalar2=None, op0=ALU.mult),
                        reads=[("ps", bank), "validA"], writes=[("av", st)])
                yield
            for e2 in range(2):
                P.add("dve", lambda e, e2=e2, ntile=ntile, gi=gi: e.tensor_copy(
                    out=vb_[st][:, 0:ntile, e2, 64], in_=validA[:, vcol0[gi]:vcol0[gi] + ntile]),
                    reads=["validA"], writes=[("av1", st, e2)])

        pt_i = [0]
        ob_i = [0]

        def pull(gen, n=1):
            if gen is None:
                return
            for _ in range(n):
                try:
                    next(gen)
                except StopIteration:
                    return

        def attn(si, gen=None):
            c, gi = steps[si]
            st = si % 2
            d = A_GROUPS[gi][1]
            nqt = 16 // d
            tiles = [(r, j) for r in range(d) for j in range(nqt)]
            seq = [(e2, b4, u) for e2 in range(2) for b4 in range(4) for u in range(4)]
            info = {}

            def emit_qk(ix):
                e2, b4, u = seq[ix]
                pb = e2 * 64
                slope = alibi_slope(gi, 2 * c + e2) * d
                r, j = tiles[b4 * 4 + u]
                qc0 = r + d * 128 * j
                bk, half = pt_i[0] % 4, 0
                i = pt_i[0] % 4
                i5 = pt_i[0] % 5
                pt_i[0] += 1
                info[ix] = i5
                sview = ps[bk][:, half * 256:(half + 1) * 256]
                fns = []
                for piece in range(2):
                    kc0 = r + 128 * d * (j + piece)
                    fns.append(lambda e, piece=piece, kc0=kc0: e.matmul(
                        ps[bk][:, half * 256 + piece * 128:half * 256 + (piece + 1) * 128],
                        kb_[st][pb:pb + 64, kc0:kc0 + 127 * d + 1:d], qs_[st][pb:pb + 64, qc0:qc0 + 127 * d + 1:d],
                        start=True, stop=True))
                P.add("pe", fns, reads=[("aq", st), ("ak", st)], writes=[("psh", bk, half)])
                P.add("dve", lambda e: e.scalar_tensor_tensor(
                    out=tmp[i][:], in0=dist[:], scalar=float(slope), in1=sview, op0=ALU.mult, op1=ALU.add),
                    reads=[("psh", bk, half), "dist"], writes=[("atmp", i)])
                P.add("act", lambda e: e.activation(out=ptl[i5][:], in_=tmp[i][:], func=AF.Exp),
                      reads=[("atmp", i)], writes=[("apt", i5)])

            def emit_pv(ix):
                e2, b4, u = seq[ix]
                i5 = info[ix]
                r, j = tiles[b4 * 4 + u]
                if u == 0:
                    ob_i[0] += 1
                ob = 4 + ob_i[0] % 2
                vt0 = r * (nqt + 1) + j
                fns = []
                for piece in range(2):
                    fns.append(lambda e, piece=piece: e.matmul(
                        ps[ob][0:65, u * 128:(u + 1) * 128], vb_[st][:, vt0 + piece, e2, :],
                        ptl[i5][:, piece * 128:(piece + 1) * 128], start=(piece == 0), stop=(piece == 1)))
                P.add("pe", fns, reads=[("apt", i5), ("av", st), ("av1", st, e2)], writes=[("ps", ob)])
                if u == 3:
                    if d == 1:
                        av = acc[0:65, e2, b4 * 512:(b4 + 1) * 512]
                        pv_ = ps[ob][0:65, :]
                    elif d == 4:
                        av = acc[0:65, e2, b4:T:4]
                        pv_ = ps[ob][0:65, :]
                    else:
                        av = acc[0:65, e2, :].rearrange("p (i dd) -> p dd i", dd=16)[:, b4 * 4:(b4 + 1) * 4, :]
                        pv_ = ps[ob][0:65, :].rearrange("p (u i) -> p u i", u=4)
                    if gi == 0:
                        P.add("dve", lambda e: e.tensor_copy(out=av, in_=pv_),
                              reads=[("ps", ob)], writes=[("acc", e2)])
                    else:
                        P.add("dve", lambda e: e.tensor_tensor(out=av, in0=av, in1=pv_, op=ALU.add),
                              reads=[("ps", ob)], writes=[("acc", e2)])

            LA = 3
            for ix in range(len(seq) + LA):
                if ix < len(seq):
                    emit_qk(ix)
                if ix >= LA:
                    emit_pv(ix - LA)
                pull(gen, 1)
                if ix < 16:
                    run_pending(1)
            pull(gen, 1000)
            if gi == 2:
                for e2 in range(2):
                    for blk in range(NB):
                        pending.append(norm_thunks(c, e2, blk))

        def norm_thunks(c, e2, blk):
            pb = e2 * 64
            bs = slice(blk * BW, (blk + 1) * BW)
            zi = (e2 * NB + blk) % 2

            def ph_a():
                P.add("dve", lambda e: e.reciprocal(out=rzr[zi][64:65, :], in_=acc[64:65, e2, bs]),
                      reads=[("acc", e2)], writes=[("rzr", zi)])
                bcast64_a(rzr[zi][64:65, :], ("rzr", zi))

            def ph_b():
                zb = nbank(6, 8)
                bcast64_b(zb)
                P.add("dve", lambda e: e.tensor_tensor(
                    out=aoT[pb:pb + 64, c, bs], in0=acc[0:64, e2, bs], in1=ps[zb][0:64, :], op=ALU.mult),
                    reads=[("ps", zb), ("acc", e2)], writes=[("ao", blk, c, e2)])
            return [ph_a, ph_b]

        pending = []

        def run_pending(n):
            for _ in range(n):
                if not pending:
                    return
                th = pending[0]
                th.pop(0)()
                if not th:
                    pending.pop(0)

        nsteps = len(steps)
        if KSTOP == 'a1':
            nsteps = 0
        elif KSTOP in ('a2', 'a3'):
            nsteps = 1
        if nsteps:
            pull(proj(0), 1000)
        for si in range(nsteps):
            gen = proj(si + 1) if si + 1 < nsteps else None
            if KSTOP != 'a2':
                attn(si, gen)
            else:
                pull(gen, 1000)
        run_pending(1000)
        P.barrier()
        if KSTOP == 'adbg':
            P.add("pool", lambda e: e.dma_start(out=out3[:, :, :], in_=aoT[:]), writes=["dbg"], dma=True)
            return
        scr, _ = epi_scratch(R_BIG)
        out_proj_epilogue(wo_sb, layer, 1, next_pre, final, scr)

    mixers[0] = mixer_a

    from_x = True
    nl = len(layers)
    for li, layer in enumerate(layers):
        kind = layer % 3
        last = (li == nl - 1)
        if li == 0:
            P.barrier()
            hbl = [sb.at(R_BIG + i * 16384, [128, KC, BW], F32) for i in range(2)]
            sqb = [sb.at(R_BIG + 32768 + i * 8192, [128, KC, BW], BF16) for i in range(2)]
            first_kind = 2 if skip_mixer else 0
            for blk in range(NB):
                prologue_only(blk, hbl[blk % 2], sqb[blk % 2], layer, first_kind, xT3, True)
        if not skip_mixer:
            mixers[kind](layer, (layer, 2) if not skip_ffn else (None if last else (layers[li + 1], 0)),
                         final=(last and skip_ffn))
        if not skip_ffn:
            nxt = None if last else (layers[li + 1], 2 if skip_mixer else 0)
            ffn(layer, nxt, final=last)

    P.barrier()

    with nc.Block() as block:
        P.replay(block)
    nc.used_inputs = used_inputs
    return nc


mixers = {}


def _prep_shared(inp):
    f = np.float32
    sh = {}
    g = np.zeros((128, 128), f)
    for kind, name in enumerate(["norm_mix_pre", "norm_mix_post", "norm_ffn_pre", "norm_ffn_post"]):
        a = np.asarray(inp[name], f)
        for l in range(DEPTH):
            g[:, (kind * 4 + l) * 8:(kind * 4 + l) * 8 + 8] = a[l].reshape(8, 128).T
    sh["gains"] = g
    wg = np.asarray(inp["ffn_wg"], f); wu = np.asarray(inp["ffn_wu"], f); wd = np.asarray(inp["ffn_wd"], f)
    sh["wg"] = np.ascontiguousarray(wg.reshape(DEPTH, 8, 128, JC, 128).transpose(0, 3, 2, 1, 4)).reshape(DEPTH * JC * 128, 1024)
    sh["wu"] = np.ascontiguousarray(wu.reshape(DEPTH, 8, 128, JC, 128).transpose(0, 3, 2, 1, 4)).reshape(DEPTH * JC * 128, 1024)
    sh["wd"] = np.ascontiguousarray(wd.reshape(DEPTH, 2, 11, 128, 8, 128).transpose(0, 1, 4, 3, 2, 5)).reshape(DEPTH * 2 * 8 * 128, 11 * 128)
    aw = np.asarray(inp["a_wqkv"], f)
    sh["a_wqkv"] = np.ascontiguousarray(aw.reshape(2, 8, 128, 3, 3, 8, 128).transpose(0, 3, 4, 5, 2, 1, 6)).reshape(2 * 3 * 3 * 8 * 128, 1024)
    sh["a_wo"] = np.ascontiguousarray(np.asarray(inp["a_wo"], f).reshape(2 * D, D))
    sh["b_wqkv"] = np.ascontiguousarray(np.asarray(inp["b_wqkv"], f)[0])
    sh["b_wo"] = np.ascontiguousarray(np.asarray(inp["b_wo"], f)[0])
    sh["c_win"] = np.ascontiguousarray(np.asarray(inp["c_win"], f)[0])
    sh["c_wuq"] = np.ascontiguousarray(np.asarray(inp["c_wuq"], f)[0])
    sh["c_wukv"] = np.ascontiguousarray(np.asarray(inp["c_wukv"], f)[0])
    sh["c_wo"] = np.ascontiguousarray(np.asarray(inp["c_wo"], f)[0])
    bg = np.zeros((128, 8), f)
    qn = np.asarray(inp["b_qnorm"], f)[0]; kn = np.asarray(inp["b_knorm"], f)[0]
    partner = np.arange(128); dd = partner % 64
    partner = np.where(dd < 32, partner + 32, partner - 32)
    bg[:, 0] = qn; bg[:, 1] = qn[partner]; bg[:, 2] = kn; bg[:, 3] = kn[partner]
    sh["bgains"] = bg
    cg = np.zeros((128, 8), f)
    cq = np.asarray(inp["c_qnorm"], f)[0]; ck = np.asarray(inp["c_kvnorm"], f)[0]
    for i in range(3):
        cg[:, i] = cq[i * 128:(i + 1) * 128]
    for i in range(2):
        cg[:, 3 + i] = ck[i * 128:(i + 1) * 128]
    sh["cgains"] = cg
    c = np.zeros((128, 1024), f)
    for mcol in range(128):
        c[partner[mcol], mcol] = 1.0
    p96 = np.arange(96)
    p96 = np.where(p96 < 64, p96, np.where(p96 < 80, p96 + 16, p96 - 16))
    for mcol in range(96):
        c[p96[mcol], 128 + mcol] = 1.0
    p32 = np.arange(32); p32 = np.where(p32 < 16, p32 + 16, p32 - 16)
    for mcol in range(32):
        c[p32[mcol], 256 + mcol] = 1.0
    pk = np.arange(128)[:, None]; qi = np.arange(128)[None, :]
    for piece, off in enumerate((-64, 64)):
        diff = pk + off - qi
        c[:, 384 + piece * 128:384 + (piece + 1) * 128] = np.where(np.abs(diff) <= 64, -np.abs(diff), -1e9)
    sh["consts"] = c
    return sh


def _prep_core(inp, c):
    f = np.float32
    b, g = c // 4, c % 4
    T0 = g * T
    per = {}
    per["xT"] = np.ascontiguousarray(np.asarray(inp["x"], f)[b, T0:T0 + T, :].T)
    s = (T0 + np.arange(T)).astype(f)
    d = np.arange(128); dd = d % 64; fi = dd % 32
    freq = np.power(f(10000.0), -(fi.astype(f)) * f(2.0) / f(64.0)).astype(f)
    row = np.floor(s / 64.0).astype(f); col = (s - row * 64).astype(f)
    pos = np.where((d < 64)[:, None], row[None, :], col[None, :]).astype(f)
    ang = (pos * freq[:, None]).astype(f)
    cb = np.cos(ang).astype(f); sn = np.sin(ang).astype(f)
    sb_ = np.where((dd < 32)[:, None], -sn, sn).astype(f)
    per["ropeB"] = np.ascontiguousarray(np.concatenate([cb, sb_], axis=1))
    fi16 = (np.arange(32) % 16).astype(f)
    fr = np.power(f(10000.0), -fi16 * f(2.0) / f(32.0)).astype(f)
    ang = (s[None, :] * fr[:, None]).astype(f)
    cc = np.cos(ang).astype(f); ss = np.sin(ang).astype(f)
    ss = np.where((np.arange(32) < 16)[:, None], -ss, ss).astype(f)
    tc_ = np.zeros((128, 4 * T), f)
    tc_[0:64, 0:T] = 1.0
    tc_[64:96, 0:T] = cc; tc_[64:96, T:2 * T] = ss
    tc_[0:32, 2 * T:3 * T] = cc; tc_[0:32, 3 * T:] = ss
    per["ropeC"] = tc_
    va = np.zeros((128, 72), f)
    col0 = 0
    for (window, dil) in A_GROUPS:
        nqt = 16 // dil
        for r in range(dil):
            for mm in range(nqt + 1):
                idx = 128 * mm - 64 + np.arange(128)
                tok = T0 + r + dil * idx
                va[:, col0 + r * (nqt + 1) + mm] = ((tok >= 0) & (tok < S)).astype(f)
        col0 += dil * (nqt + 1)
    per["validA"] = va
    return per


_CACHE = {}


def kernel(**inputs):
    key = "full"
    if key not in _CACHE:
        _CACHE[key] = build_program()
    nc = _CACHE[key]
    sh = _prep_shared(inputs)
    in_maps = []
    for c in range(NCORES):
        mp = dict(sh)
        mp.update(_prep_core(inputs, c))
        in_maps.append({k: mp[k] for k in nc.used_inputs})
    res = run_bass_kernel_spmd(nc, in_maps, core_ids=list(range(NCORES)))
    out = np.empty((2, S, D), np.float32)
    for c in range(NCORES):
        b, g = c // 4, c % 4
        out[b, g * T:(g + 1) * T, :] = res.results[c]["out"].T
    return out
```

```python
import contextlib
import os
KSTOP = os.environ.get('KSTOP', '')
import numpy as np
import concourse.bass as bass
import concourse.mybir as mybir
from concourse.bass_utils import run_bass_kernel_spmd

F32 = mybir.dt.float32
BF16 = mybir.dt.bfloat16
AF = mybir.ActivationFunctionType
ALU = mybir.AluOpType

NCORES = 8
D = 1024
KC = 8
T = 2048
S = 8192
NB = 4
BW = 512
DFF = 2816
JC = 22
EPS = 1e-6
DEPTH = 4
SB_BASE = 18560
SB_END = 229376
A_GROUPS = ((128, 1), (512, 4), (2048, 16))


class Prog:
    ENG = ("pe", "act", "dve", "pool", "sp")
    MAXC = 20000

    def __init__(self, nc):
        self.nc = nc
        self.q = {e: [] for e in self.ENG}
        self.tick = {}
        self.nsem = 0
        self.lastw = {}
        self.readers = {}
        self.seen = {e: {} for e in self.ENG}
        self.dma_slots = {}
        self.dma_rr = {}
        self.last_ticket = {}
        self.sems = {}

    def _newsem(self, name):
        self.nsem += 1
        h = self.nc.alloc_semaphore(f"{name}_{self.nsem}")
        self.sems[id(h)] = h
        return h

    def _compute_ticket(self, eng):
        st = self.tick.get(eng)
        if st is None or st[1] >= self.MAXC:
            st = [self._newsem("t" + eng), 0]
            self.tick[eng] = st
        st[1] += 1
        return (id(st[0]), st[1], eng), (st[0], 1)

    def _dma_ticket(self, eng, nslots=12):
        slots = self.dma_slots.setdefault(eng, [])
        if len(slots) < nslots:
            slots.append([self._newsem("d" + eng), 0])
            i = len(slots) - 1
        else:
            i = self.dma_rr.get(eng, 0) % nslots
            self.dma_rr[eng] = i + 1
        st = slots[i]
        prev = (id(st[0]), st[1], "dma") if st[1] > 0 else None
        st[1] += 16
        return (id(st[0]), st[1], "dma"), (st[0], 16), prev

    def add(self, eng, fns, reads=(), writes=(), dma=False, waits=(), cc=False):
        if callable(fns):
            fns = [fns]
        tks = set(waits)
        for k in reads:
            t = self.lastw.get(k)
            if t is not None:
                tks.add(t)
        for k in writes:
            t = self.lastw.get(k)
            if t is not None:
                tks.add(t)
            for t in self.readers.get(k, ()):
                tks.add(t)
        if dma:
            ticket, inc, prev = self._dma_ticket(eng)
            if prev is not None:
                tks.add(prev)
        elif cc:
            st = [self._newsem("cc"), 1]
            ticket, inc = (id(st[0]), st[1], "dma"), (st[0], None)
        else:
            ticket, inc = self._compute_ticket(eng)
        need = {}
        for (sid, val, prod) in tks:
            if prod == "pe" and eng == "pe":
                continue
            if self.seen[eng].get(sid, 0) >= val:
                continue
            if need.get(sid, 0) < val:
                need[sid] = val
        for sid, val in need.items():
            self.seen[eng][sid] = val
        if not (dma or cc):
            pass
        self.q[eng].append(([(self.sems[sid], val) for sid, val in need.items()], fns, inc))
        for k in reads:
            self.readers.setdefault(k, []).append(ticket)
        for k in writes:
            self.lastw[k] = ticket
            self.readers[k] = []
        self.last_ticket[eng if not (dma or cc) else ("dma", ticket[0])] = ticket
        return ticket

    def barrier(self):
        tks = list(self.last_ticket.values())
        for eng in self.ENG:
            need = {}
            for (sid, val, prod) in tks:
                if self.seen[eng].get(sid, 0) >= val:
                    continue
                if need.get(sid, 0) < val:
                    need[sid] = val
            for sid, val in need.items():
                self.seen[eng][sid] = val
            if need:
                self.q[eng].append(([(self.sems[sid], val) for sid, val in need.items()], [], None))
        self.lastw = {}
        self.readers = {}

    def replay(self, block):
        def run(eng):
            def body(e):
                for waits, fns, inc in self.q[eng]:
                    for sem, val in waits:
                        e.wait_ge(sem, val)
                    ins = None
                    for fn in fns:
                        ins = fn(e)
                    if inc is not None and ins is not None:
                        if inc[1] is None:
                            ins.then_inc(inc[0])
                        else:
                            ins.then_inc(inc[0], inc[1])
            return body
        block.tensor(run("pe"))
        block.scalar(run("act"))
        block.vector(run("dve"))
        block.gpsimd(run("pool"))
        block.sync(run("sp"))


class SB:
    def __init__(self, nc):
        self.nc = nc
        self.n = 0

    def at(self, off, shape, dtype):
        self.n += 1
        esz = 4 if dtype == F32 else 2
        nbytes = esz * int(np.prod(shape[1:]))
        assert off % 32 == 0, off
        assert SB_BASE <= off and off + nbytes <= SB_END, (off, nbytes, shape)
        return self.nc.alloc_sbuf_tensor_at(f"sb{self.n}", list(shape), dtype, offset=off)


def alibi_slope(g, h):
    return float(2.0 ** (-8.0 * (g * 16 + h + 1) / 48.0))


def build_program(layers=(0, 1, 2, 3), skip_mixer=False, skip_ffn=False):
    nc = bass.Bass("TRN2", target_bir_lowering=False)
    P = Prog(nc)
    sb = SB(nc)

    used_inputs = []

    class _Lazy:
        def __init__(self, name, shape):
            self.name, self.shape, self.ap_ = name, shape, None

        def get(self):
            if self.ap_ is None:
                self.ap_ = nc.dram_tensor(self.name, list(self.shape), F32, kind="ExternalInput").ap()
                used_inputs.append(self.name)
            return self.ap_

        def __getitem__(self, idx):
            return self.get()[idx]

        def rearrange(self, *a, **k):
            return self.get().rearrange(*a, **k)

    def din(name, shape, dt=F32):
        return _Lazy(name, shape)

    xT = din("xT", [D, T])
    gains_d = din("gains", [128, 128])
    consts_d = din("consts", [128, 1024])
    ropeB_d = din("ropeB", [128, 2 * T])
    ropeC_d = din("ropeC", [128, 4 * T])
    validA_d = din("validA", [128, 72])
    bgains_d = din("bgains", [128, 8])
    cgains_d = din("cgains", [128, 8])
    wg_d = din("wg", [DEPTH * JC * 128, 1024])
    wu_d = din("wu", [DEPTH * JC * 128, 1024])
    wd_d = din("wd", [DEPTH * 2 * 8 * 128, 11 * 128])
    awqkv_d = din("a_wqkv", [2 * 3 * 3 * 8 * 128, 1024])
    awo_d = din("a_wo", [2 * D, D])
    bwqkv_d = din("b_wqkv", [D, 1536])
    bwo_d = din("b_wo", [D, D])
    cwin_d = din("c_win", [D, 672])
    cwuq_d = din("c_wuq", [384, 1536])
    cwukv_d = din("c_wukv", [256, 2048])
    cwo_d = din("c_wo", [D, D])
    out_d = nc.dram_tensor("out", [D, T], F32, kind="ExternalOutput").ap()
    hbuf = nc.dram_tensor("hbuf", [D, T], F32).ap()
    payA4 = [nc.dram_tensor(f"payA{i}", [D, BW], BF16) for i in range(4)]
    gA4 = [nc.dram_tensor(f"gA{i}", [4 * D, BW], BF16) for i in range(4)]
    payBK = nc.dram_tensor("payBK", [256, T], BF16)
    gBK = nc.dram_tensor("gBK", [4 * 256, T], BF16)
    payBV = nc.dram_tensor("payBV", [256, T], BF16)
    gBV = nc.dram_tensor("gBV", [4 * 256, T], BF16)
    RG = [[0, 1, 2, 3], [4, 5, 6, 7]]
    sv = {}

    o = SB_BASE
    R_Y = o; o += 32768
    R_AO = o; o += 32768
    R_BIG = o; o += 114688
    R_W = o; o += 17408
    R_M = o
    assert SB_END - R_M >= 13000, SB_END - R_M

    yT = sb.at(R_Y, [128, KC, T], BF16)
    aoT = sb.at(R_AO, [128, KC, T], BF16)
    m = R_M
    gains = sb.at(m, [128, 128], F32); m += 512
    ones_bf = sb.at(m, [128, 128], BF16); m += 256
    ones_f = sb.at(m, [128, 64], F32); m += 256
    rstd = [sb.at(m + i * 2048, [128, BW], F32) for i in range(2)]; m += 4096
    sdt = [sb.at(m + i * 2048, [128, BW], F32) for i in range(2)]; m += 4096
    bgains = sb.at(m, [128, 8], F32); m += 32
    cgains = sb.at(m, [128, 8], F32); m += 32
    validA = sb.at(m, [128, 72], F32); m += 288
    rh_ = sb.at(m, [128, BW], BF16); m += 1024
    rl_ = sb.at(m, [128, BW], BF16); m += 1024
    assert m <= SB_END, m

    def bcast64_a(row_ap, row_key):
        P.add("dve", lambda e: e.tensor_copy(out=rh_[64:65, :], in_=row_ap), reads=[row_key], writes=["rh"])
        P.add("dve", lambda e: e.tensor_tensor(out=rl_[64:65, :], in0=row_ap, in1=rh_[64:65, :], op=ALU.subtract),
              reads=[row_key, "rh"], writes=["rl"])

    def bcast64_b(zb):
        mm_group(zb, ps[zb][0:64, :], [(ones_bf[64:65, 0:64], rh_[64:65, :]), (ones_bf[64:65, 0:64], rl_[64:65, :])],
                 reads=["rh", "rl", "ones_bf"])

    ps = [nc.alloc_psum_tensor(f"ps{i}", [128, BW], F32) for i in range(8)]
    bank_rr = [0]

    def nbank(lo=0, hi=8):
        b = lo + bank_rr[0] % (hi - lo)
        bank_rr[0] += 1
        return b

    def mm_group(bank, out_ap, pairs, reads, extra_writes=()):
        n = len(pairs)
        fns = []
        for i, (l, r) in enumerate(pairs):
            fns.append(lambda e, l=l, r=r, i=i: e.matmul(out_ap, l, r, start=(i == 0), stop=(i == n - 1)))
        return P.add("pe", fns, reads=reads, writes=[("ps", bank)] + list(extra_writes))

    def gcol(kind, layer, k):
        c = (kind * 4 + layer) * 8 + k
        return gains[:, c:c + 1]

    P.add("sp", lambda e: e.dma_start(out=gains[:], in_=gains_d[:, :]), writes=["gains"], dma=True)
    P.add("sp", lambda e: e.dma_start(out=bgains[:], in_=bgains_d[:, :]), writes=["bgains"], dma=True)
    P.add("sp", lambda e: e.dma_start(out=cgains[:], in_=cgains_d[:, :]), writes=["cgains"], dma=True)
    P.add("sp", lambda e: e.dma_start(out=validA[:], in_=validA_d[:, :]), writes=["validA"], dma=True)
    P.add("dve", lambda e: e.memset(ones_bf[:], 1.0), writes=["ones_bf"])
    P.add("dve", lambda e: e.memset(ones_f[:], 1.0), writes=["ones_f"])

    xT3 = xT.rearrange("(k p) t -> p k t", p=128)
    hb3 = hbuf.rearrange("(k p) t -> p k t", p=128)
    out3 = out_d.rearrange("(k p) t -> p k t", p=128)

    stat_rr = [0]

    def rms_rstd(sq_aps, sq_keys, nfeat, npart=128):
        slot = stat_rr[0] % 2
        stat_rr[0] += 1
        bank = nbank()
        mm_group(bank, ps[bank][0:npart, :], [(ones_bf[:, 0:npart], a) for a in sq_aps],
                 reads=list(sq_keys) + ["ones_bf"])
        P.add("act", lambda e: e.activation(out=sdt[slot][0:npart, :], in_=ps[bank][0:npart, :], func=AF.Ln,
                                            bias=EPS, scale=1.0 / nfeat),
              reads=[("ps", bank)], writes=[("sd", slot)])
        P.add("act", lambda e: e.activation(out=rstd[slot][0:npart, :], in_=sdt[slot][0:npart, :], func=AF.Exp, scale=-0.5),
              reads=[("sd", slot)], writes=[("rstd", slot)])
        return slot

    def prologue_only(blk, hblk, sqb, layer, kind_pre, hsrc3, first_store):
        bs = slice(blk * BW, (blk + 1) * BW)
        P.add("sp", lambda e: e.dma_start(out=hblk[:], in_=hsrc3[:, :, bs]), reads=[("hb", blk)],
              writes=[("hblk", id(hblk)), ("hblkA", id(hblk)), ("hblkB", id(hblk))], dma=True)
        if first_store:
            P.add("sp", lambda e: e.dma_start(out=hb3[:, :, bs], in_=hblk[:]), reads=[("hblk", id(hblk)), ("hblkA", id(hblk)), ("hblkB", id(hblk))],
                  writes=[("hb", blk)], dma=True)
        pre_norm(blk, hblk, sqb, layer, kind_pre)

    def pre_norm(blk, hblk, sqb, layer, kind_pre):
        bs = slice(blk * BW, (blk + 1) * BW)
        P.add("act", lambda e: e.activation(out=sqb[:], in_=hblk[:], func=AF.Square),
              reads=[("hblk", id(hblk)), ("hblkA", id(hblk)), ("hblkB", id(hblk))], writes=[("sq", id(sqb))])
        slot = rms_rstd([sqb[:, k, :] for k in range(KC)], [("sq", id(sqb))], D)
        for k in range(KC):
            P.add("dve", lambda e, k=k: e.scalar_tensor_tensor(
                out=yT[:, k, bs], in0=hblk[:, k, :], scalar=gcol(kind_pre, layer, k), in1=rstd[slot][:],
                op0=ALU.mult, op1=ALU.mult),
                reads=[("hblk", id(hblk)), ("hblkA", id(hblk)), ("hblkB", id(hblk)), ("rstd", slot), "gains"], writes=[("yT", blk, k)])
        if kind_pre == 0 and layer % 3 == 0:
            P.add("sp", lambda e: e.dma_start(out=payA4[blk].ap().rearrange("(k p) t -> p k t", p=128), in_=yT[:, :, bs]),
                  reads=[("yT", blk, k) for k in range(KC)], writes=[("payA", blk)], dma=True)
            P.add("pool", lambda e: e.collective_compute("AllGather", ALU.bypass, replica_groups=RG,
                                                         ins=[payA4[blk].ap().opt()], outs=[gA4[blk].ap().opt()]),
                  reads=[("payA", blk)], writes=[("gA", blk)], cc=True)

    def epilogue(blk, osb, sqo, hblk, sqb, layer, kind_post, next_pre, final, split=False):
        bs = slice(blk * BW, (blk + 1) * BW)
        P.add("sp", lambda e: e.dma_start(out=hblk[:], in_=hb3[:, :, bs]), reads=[("hb", blk)],
              writes=[("hblk", id(hblk)), ("hblkA", id(hblk)), ("hblkB", id(hblk))], dma=True)
        slot = rms_rstd([sqo[:, k, :] for k in range(KC)], [("sqo", id(sqo))], D)
        for k in range(KC):
            P.add("dve", lambda e, k=k: e.scalar_tensor_tensor(
                out=osb[:, k, :], in0=osb[:, k, :], scalar=gcol(kind_post, layer, k), in1=rstd[slot][:],
                op0=ALU.mult, op1=ALU.mult),
                reads=[("rstd", slot), "gains"], writes=[("osb", id(osb))])
        P.add("pool", lambda e: e.tensor_tensor(out=hblk[:, 0:2, :], in0=hblk[:, 0:2, :], in1=osb[:, 0:2, :], op=ALU.add),
              reads=[("osb", id(osb)), ("hblk", id(hblk))], writes=[("hblkA", id(hblk))])
        P.add("dve", lambda e: e.tensor_tensor(out=hblk[:, 2:8, :], in0=hblk[:, 2:8, :], in1=osb[:, 2:8, :], op=ALU.add),
              reads=[("osb", id(osb)), ("hblk", id(hblk))], writes=[("hblkB", id(hblk))])
        dst = out3 if final else hb3
        P.add("sp", lambda e: e.dma_start(out=dst[:, :, bs], in_=hblk[:]), reads=[("hblk", id(hblk)), ("hblkA", id(hblk)), ("hblkB", id(hblk))],
              writes=[("hb", blk)] if not final else [("outd", blk)], dma=True)
        if next_pre is not None and not split:
            pre_norm(blk, hblk, sqb, next_pre[0], next_pre[1])

    def out_proj_epilogue(w_sb, layer, kind_post, next_pre, final, scr):
        osbs, sqos, hblks, sqbs = scr

        def wo_blk(blk):
            bs = slice(blk * BW, (blk + 1) * BW)
            osb, sqo = osbs[blk % 2], sqos[blk % 2]
            for n in range(KC):
                bank = nbank()
                mm_group(bank, ps[bank][:], [(w_sb[:, k, n * 128:(n + 1) * 128], aoT[:, k, bs]) for k in range(KC)],
                         reads=[("ao", blk, k) for k in range(KC)] + ["wo"])
                P.add("act", lambda e, n=n, bank=bank, osb=osb: e.activation(out=osb[:, n, :], in_=ps[bank][:], func=AF.Copy),
                      reads=[("ps", bank)], writes=[("osb", id(osb))])
                P.add("act", lambda e, n=n, bank=bank, sqo=sqo: e.activation(out=sqo[:, n, :], in_=ps[bank][:], func=AF.Square),
                      reads=[("ps", bank)], writes=[("sqo", id(sqo))])

        def post(blk):
            epilogue(blk, osbs[blk % 2], sqos[blk % 2], hblks[blk % 2], sqbs[blk % 2], layer, kind_post, next_pre, final, split=True)

        def pre(blk):
            if next_pre is not None:
                pre_norm(blk, hblks[blk % 2], sqbs[blk % 2], next_pre[0], next_pre[1])

        wo_blk(0)
        wo_blk(1)
        post(0)
        for blk in range(1, NB):
            if blk + 1 < NB:
                wo_blk(blk + 1)
            post(blk)
            pre(blk - 1)
        pre(NB - 1)

    def epi_scratch(base):
        o = base
        osbs = [sb.at(o + i * 16384, [128, KC, BW], F32) for i in range(2)]; o += 32768
        hblks = [sb.at(o + i * 16384, [128, KC, BW], F32) for i in range(2)]; o += 32768
        sqos = [sb.at(o + i * 8192, [128, KC, BW], BF16) for i in range(2)]; o += 16384
        sqbs = [sb.at(o + i * 8192, [128, KC, BW], BF16) for i in range(2)]; o += 16384
        return (osbs, sqos, hblks, sqbs), o

    def ffn(layer, next_pre, final):
        P.barrier()
        o = R_BIG
        act = sb.at(o, [128, 11, T], BF16); o += 45056
        oall = sb.at(o, [128, KC, T], F32); o += 65536
        assert o <= R_BIG + 114688
        sg = [sb.at(R_AO + i * 2048, [128, BW], F32) for i in range(2)]
        wgu = [sb.at(R_W + i * 4096, [128, 2, KC, 128], BF16) for i in range(3)]
        wds = [sb.at(R_W + 12288 + i * 2816, [128, 11, 128], BF16) for i in range(1)]
        wd2 = sb.at(R_AO + 4096, [128, 11, 128], BF16)
        wdl = [wds[0], wd2]
        for grp in range(2):
            for jj in range(11):
                j = grp * 11 + jj
                wsl = wgu[j % 3]
                r0 = (layer * JC + j) * 128
                P.add("pool", lambda e, wsl=wsl, r0=r0: e.dma_start(
                    out=wsl[:, 0, :, :], in_=wg_d[r0:r0 + 128, :].rearrange("p (k n) -> p k n", k=KC)),
                    writes=[("wgu", j % 3, 0)], dma=True)
                P.add("pool", lambda e, wsl=wsl, r0=r0: e.dma_start(
                    out=wsl[:, 1, :, :], in_=wu_d[r0:r0 + 128, :].rearrange("p (k n) -> p k n", k=KC)),
                    writes=[("wgu", j % 3, 1)], dma=True)
                for blk in range(NB):
                    bs = slice(blk * BW, (blk + 1) * BW)
                    bg = nbank()
                    mm_group(bg, ps[bg][:], [(wsl[:, 0, k, :], yT[:, k, bs]) for k in range(KC)],
                             reads=[("yT", blk, k) for k in range(KC)] + [("wgu", j % 3, 0)])
                    bu = nbank()
                    mm_group(bu, ps[bu][:], [(wsl[:, 1, k, :], yT[:, k, bs]) for k in range(KC)],
                             reads=[("yT", blk, k) for k in range(KC)] + [("wgu", j % 3, 1)])
                    s = sg[(j * NB + blk) % 2]
                    P.add("act", lambda e, s=s, bg=bg: e.activation(out=s[:], in_=ps[bg][:], func=AF.Silu),
                          reads=[("ps", bg)], writes=[("sg", id(s))])
                    P.add("dve", lambda e, s=s, bu=bu, jj=jj, bs=bs: e.tensor_tensor(
                        out=act[:, jj, bs], in0=ps[bu][:], in1=s[:], op=ALU.mult),
                        reads=[("ps", bu), ("sg", id(s))], writes=[("act", jj, bs.start)])
            for n in range(KC):
                wsl = wdl[n % 2]
                r0 = ((layer * 2 + grp) * 8 + n) * 128
                P.add("pool", lambda e, wsl=wsl, r0=r0: e.dma_start(
                    out=wsl[:], in_=wd_d[r0:r0 + 128, :].rearrange("p (j n) -> p j n", j=11)),
                    writes=[("wd", n % 2)], dma=True)
                for blk in range(NB):
                    bs = slice(blk * BW, (blk + 1) * BW)
                    bank = nbank()
                    mm_group(bank, ps[bank][:], [(wsl[:, jj, :], act[:, jj, bs]) for jj in range(11)],
                             reads=[("act", jj, bs.start) for jj in range(11)] + [("wd", n % 2)])
                    if grp == 0:
                        P.add("act", lambda e, n=n, bs=bs, bank=bank: e.activation(
                            out=oall[:, n, bs], in_=ps[bank][:], func=AF.Copy),
                            reads=[("ps", bank)], writes=[("oall", n, bs.start)])
                    else:
                        P.add("dve", lambda e, n=n, bs=bs, bank=bank: e.tensor_tensor(
                            out=oall[:, n, bs], in0=oall[:, n, bs], in1=ps[bank][:], op=ALU.add),
                            reads=[("ps", bank)], writes=[("oall", n, bs.start)])
        P.barrier()
        o = R_BIG
        hblks = [sb.at(o + i * 16384, [128, KC, BW], F32) for i in range(2)]; o += 32768
        hblks.append(sb.at(R_W, [128, KC, BW], F32))
        sqos = [sb.at(R_AO + 8192 + i * 8192, [128, KC, BW], BF16) for i in range(2)]
        sqbs = [sb.at(o + i * 8192, [128, KC, BW], BF16) for i in range(1)]; o += 8192
        sqbs.append(sb.at(R_AO + 24576, [128, KC, BW], BF16))
        assert o <= R_BIG + 45056
        def fpost(blk):
            bs = slice(blk * BW, (blk + 1) * BW)
            sqo = sqos[blk % 2]
            P.add("act", lambda e: e.activation(out=sqo[:], in_=oall[:, :, bs], func=AF.Square),
                  writes=[("sqo", id(sqo))])
            epilogue(blk, _View(oall, bs), sqo, hblks[blk % 3], sqbs[blk % 2], layer, 3, next_pre, final, split=True)

        def fpre(blk):
            if next_pre is not None:
                pre_norm(blk, hblks[blk % 3], sqbs[blk % 2], next_pre[0], next_pre[1])

        fpost(0)
        for blk in range(1, NB):
            fpost(blk)
            fpre(blk - 1)
        fpre(NB - 1)

    class _View:
        def __init__(self, t, bs):
            self.t, self.bs = t, bs

        def __getitem__(self, idx):
            if idx == slice(None):
                return self.t[:, :, self.bs]
            a, b, c = idx
            assert c == slice(None)
            return self.t[a, b, self.bs]

    pt_rr = [0]

    def attn_core(nkt, qk_pair, pv_ops, scale, pT, kv_reads, q_reads, acc_banks, npart_s=128, hook=None, zsum=None):
        sbanks = [None] * nkt
        pts = [None] * nkt

        def qk(t):
            bk = nbank(0, 4)
            sbanks[t] = bk
            l, r = qk_pair(t)
            mm_group(bk, ps[bk][0:npart_s, :], [(l, r)], reads=list(kv_reads) + list(q_reads))
            i = pt_rr[0] % len(pT)
            pt_rr[0] += 1
            pts[t] = i
            P.add("act", lambda e: e.activation(out=pT[i][0:npart_s, :], in_=ps[bk][0:npart_s, :], func=AF.Exp,
                                                scale=scale),
                  reads=[("ps", bk)], writes=[("pT", i)])

        def pv(t):
            i = pts[t]
            fns = []
            for (bank, out_ap, lhs_fn) in pv_ops:
                fns.append(lambda e, out_ap=out_ap, lhs_fn=lhs_fn: e.matmul(
                    out_ap, lhs_fn(t), pT[i][0:npart_s, :], start=(t == 0), stop=(t == nkt - 1)))
            P.add("pe", fns, reads=[("pT", i)] + list(kv_reads), writes=[("ps", b) for b in acc_banks])
            if zsum is not None:
                zb_, abuf, zbuf = zsum
                if zq and zq[0][0] <= t - 2:
                    zemit()
                if t % 2 == 1:
                    a = abuf[(t // 2) % 2]
                    i0 = pts[t - 1]
                    P.add("dve", lambda e: e.tensor_tensor(out=a[:], in0=pT[i0][:], in1=pT[i][:], op=ALU.add),
                          reads=[("pT", i0), ("pT", i)], writes=[("zab", id(a))])
                if t % 4 == 3:
                    z = zbuf[(t // 4) % 2]
                    P.add("dve", lambda e: e.tensor_tensor(out=z[:], in0=abuf[0][:], in1=abuf[1][:], op=ALU.add),
                          reads=[("zab", id(abuf[0])), ("zab", id(abuf[1]))], writes=[("zzb", id(z))])
                    zq.append((t, z, t // 4))

        zq = []

        def zemit():
            zb_, abuf, zbuf = zsum
            t_, z, g_ = zq.pop(0)
            ng = nkt // 4
            P.add("pe", lambda e: e.matmul(ps[zb_][:], ones_bf[:], z[:], start=(g_ == 0), stop=(g_ == ng - 1)),
                  reads=[("zzb", id(z)), "ones_bf"], writes=[("ps", zb_)])

        LA = 3
        for t in range(nkt + LA):
            if t < nkt:
                qk(t)
            if t >= LA:
                pv(t - LA)
            if hook is not None:
                hook(t)
        while zq:
            zemit()

    def mixer_b(layer, next_pre, final):
        P.barrier()
        o = R_BIG
        qT = sb.at(o, [128, 8, T], BF16); o += 32768
        kTo = sb.at(o, [128, 2, T], BF16); o += 8192
        Vo = sb.at(o, [128, 2, T], BF16); o += 8192
        o2 = o
        wqkv = sb.at(o, [128, KC, 1536], BF16); o += 24576
        qf = [sb.at(o + i * 2048, [128, BW], F32) for i in range(2)]; o += 4096
        sqh = [sb.at(o + i * 1024, [128, BW], BF16) for i in range(2)]; o += 2048
        ta = [sb.at(o + i * 2048, [128, BW], F32) for i in range(2)]; o += 4096
        tb = [sb.at(o + i * 2048, [128, BW], F32) for i in range(2)]; o += 4096
        rc = [sb.at(o + i * 2048, [128, BW], F32) for i in range(2)]; o += 4096
        rs = [sb.at(o + i * 2048, [128, BW], F32) for i in range(2)]; o += 4096
        permB = sb.at(o, [128, 128], F32); o += 512
        assert o <= R_BIG + 114688, o - R_BIG
        wo_sb = sb.at(R_W, [128, KC, D], BF16)
        P.add("pool", lambda e: e.dma_start(out=wqkv[:], in_=bwqkv_d.rearrange("(k p) n -> p k n", p=128)),
              writes=["wqkv"], dma=True)
        P.add("pool", lambda e: e.dma_start(out=wo_sb[:], in_=bwo_d.rearrange("(k p) n -> p k n", p=128)),
              writes=["wo"], dma=True)
        P.add("sp", lambda e: e.dma_start(out=permB[:], in_=consts_d[:, 0:128]), writes=["permB"], dma=True)
        cnt = 0
        for blk in range(NB):
            bs = slice(blk * BW, (blk + 1) * BW)
            s1 = blk % 2
            P.add("sp", lambda e, s1=s1, bs=bs: e.dma_start(out=rc[s1][:], in_=ropeB_d[:, bs]),
                  writes=[("rc", s1)], dma=True)
            P.add("sp", lambda e, s1=s1, blk=blk: e.dma_start(out=rs[s1][:], in_=ropeB_d[:, T + blk * BW:T + (blk + 1) * BW]),
                  writes=[("rs", s1)], dma=True)
            for c in range(10):
                s2 = cnt % 2
                cnt += 1
                bank = nbank()
                mm_group(bank, ps[bank][:], [(wqkv[:, k, c * 128:(c + 1) * 128], yT[:, k, bs]) for k in range(KC)],
                         reads=[("yT", blk, k) for k in range(KC)] + ["wqkv"])
                P.add("act", lambda e, s2=s2, bank=bank: e.activation(out=qf[s2][:], in_=ps[bank][:], func=AF.Copy),
                      reads=[("ps", bank)], writes=[("qf", s2)])
                P.add("act", lambda e, s2=s2, bank=bank: e.activation(out=sqh[s2][:], in_=ps[bank][:], func=AF.Square),
                      reads=[("ps", bank)], writes=[("sqh", s2)])
                slot = rms_rstd([sqh[s2][:]], [("sqh", s2)], 128)
                b2 = nbank()
                mm_group(b2, ps[b2][:], [(permB[:], qf[s2][:])], reads=[("qf", s2), "permB"])
                gi = 0 if c < 8 else 2
                P.add("dve", lambda e, s2=s2, s1=s1, gi=gi: e.scalar_tensor_tensor(
                    out=ta[s2][:], in0=qf[s2][:], scalar=bgains[:, gi:gi + 1], in1=rc[s1][:], op0=ALU.mult, op1=ALU.mult),
                    reads=[("qf", s2), ("rc", s1), "bgains"], writes=[("ta", s2)])
                P.add("dve", lambda e, s2=s2, s1=s1, gi=gi, b2=b2: e.scalar_tensor_tensor(
                    out=tb[s2][:], in0=ps[b2][:], scalar=bgains[:, gi + 1:gi + 2], in1=rs[s1][:], op0=ALU.mult, op1=ALU.mult),
                    reads=[("ps", b2), ("rs", s1), "bgains"], writes=[("tb", s2)])
                P.add("pool", lambda e, s2=s2: e.tensor_tensor(out=ta[s2][:], in0=ta[s2][:], in1=tb[s2][:], op=ALU.add),
                      reads=[("tb", s2)], writes=[("ta", s2)])
                dest = qT[:, c, bs] if c < 8 else kTo[:, c - 8, bs]
                dkey = ("qT", c, blk) if c < 8 else ("kTo", c - 8, blk)
                P.add("dve", lambda e, s2=s2, slot=slot, dest=dest: e.tensor_tensor(
                    out=dest, in0=ta[s2][:], in1=rstd[slot][:], op=ALU.mult),
                    reads=[("ta", s2), ("rstd", slot)], writes=[dkey])
            for tt in range(4):
                ti = blk * 4 + tt
                bank = nbank()
                mm_group(bank, ps[bank][:, 0:256],
                         [(yT[:, k, ti * 128:(ti + 1) * 128], wqkv[:, k, 1280:1536]) for k in range(KC)],
                         reads=[("yT", blk, k) for k in range(KC)] + ["wqkv"])
                P.add("act", lambda e, ti=ti, bank=bank: e.activation(
                    out=Vo[:, :, ti * 128:(ti + 1) * 128], in_=ps[bank][:, 0:256].rearrange("p (k n) -> p k n", k=2),
                    func=AF.Copy), reads=[("ps", bank)], writes=[("Vo", ti)])
        if KSTOP == 'b1':
            P.barrier()
            scr, _ = epi_scratch(R_BIG)
            out_proj_epilogue(wo_sb, layer, 1, next_pre, final, scr)
            return
        P.add("sp", lambda e: e.dma_start(out=payBK.ap().rearrange("(k d) t -> d k t", k=2), in_=kTo[:]),
              reads=[("kTo", kv, blk) for kv in range(2) for blk in range(NB)], writes=["payBK"], dma=True)
        P.add("sp", lambda e: e.dma_start(out=payBV.ap().rearrange("(k p) f -> p k f", k=2), in_=Vo[:]),
              reads=[("Vo", ti) for ti in range(16)], writes=["payBV"], dma=True)
        P.add("pool", lambda e: e.collective_compute("AllGather", ALU.bypass, replica_groups=RG,
                                                     ins=[payBK.ap().opt()], outs=[gBK.ap().opt()]),
              reads=["payBK"], writes=["gBK"], cc=True)
        P.add("pool", lambda e: e.collective_compute("AllGather", ALU.bypass, replica_groups=RG,
                                                     ins=[payBV.ap().opt()], outs=[gBV.ap().opt()]),
              reads=["payBV"], writes=["gBV"], cc=True)
        o = o2
        Kk = sb.at(o, [128, 4, T], BF16); o += 16384
        Vk = sb.at(o, [128, 4, T], BF16); o += 16384
        pT = [sb.at(o + i * 1024, [128, BW], BF16) for i in range(6)]; o += 6144
        rz = [sb.at(o + i * 2048, [128, BW], F32) for i in range(2)]; o += 4096
        zab = [sb.at(o + i * 1024, [128, BW], BF16) for i in range(2)]; o += 2048
        zzb = [sb.at(o + i * 1024, [128, BW], BF16) for i in range(2)]; o += 2048
        assert o <= R_BIG + 114688
        P.barrier()
        it = 0
        for kv in range(2):
            P.add("sp", lambda e, kv=kv: e.dma_start(
                out=Kk[:], in_=gBK.ap().rearrange("(r k d) t -> d k r t", k=2, d=128)[:, kv]),
                reads=["gBK"], writes=["Kk"], dma=True)
            P.add("sp", lambda e, kv=kv: e.dma_start(
                out=Vk[:], in_=gBV.ap().rearrange("(r k p) f -> p k r f", k=2, p=128)[:, kv]),
                reads=["gBV"], writes=["Vk"], dma=True)
            for h in range(kv * 4, kv * 4 + 4):
                if KSTOP == 'b2' or (KSTOP == 'b3' and h > 0):
                    break
                for qb in range(NB if KSTOP != 'b3' else 1):
                    qs = slice(qb * BW, (qb + 1) * BW)
                    ob = 4 + (it % 2) * 2
                    zb = ob + 1
                    it += 1
                    attn_core(
                        64,
                        lambda t, h=h, qs=qs: (Kk[:, t // 16, (t % 16) * 128:(t % 16 + 1) * 128], qT[:, h, qs]),
                        [(ob, ps[ob][:], lambda t: Vk[:, t // 16, (t % 16) * 128:(t % 16 + 1) * 128])],
                        float(128 ** -0.5), pT, ["Kk", "Vk"], [("qT", h, qb)], [ob], zsum=(zb, zab, zzb))
                    rzi = rz[it % 2]
                    P.add("dve", lambda e, rzi=rzi, zb=zb: e.tensor_copy(out=rzi[:], in_=ps[zb][:]),
                          reads=[("ps", zb)], writes=[("rz", id(rzi))])
                    P.add("dve", lambda e, rzi=rzi: e.reciprocal(out=rzi[:], in_=rzi[:]),
                          reads=[("rz", id(rzi))], writes=[("rz", id(rzi))])
                    P.add("dve", lambda e, rzi=rzi, ob=ob, h=h, qs=qs: e.tensor_tensor(
                        out=aoT[:, h, qs], in0=ps[ob][:], in1=rzi[:], op=ALU.mult),
                        reads=[("ps", ob), ("rz", id(rzi))], writes=[("ao", qb, h)])
        P.barrier()
        scr, _ = epi_scratch(R_BIG)
        out_proj_epilogue(wo_sb, layer, 1, next_pre, final, scr)

    mixers[1] = mixer_b

    def mixer_c(layer, next_pre, final):
        P.barrier()
        o = R_BIG
        cqn = sb.at(o, [128, 3, T], BF16); o += 12288
        o_keep = o
        ckvo = sb.at(o, [128, 2, T], BF16); o += 8192
        kro = sb.at(o, [128, T], BF16); o += 4096
        win = sb.at(o, [128, KC, 672], BF16); o += 10752
        cf = [sb.at(o + i * 6144, [128, 3, BW], F32) for i in range(2)]; o += 12288
        sq = [sb.at(o + i * 3072, [128, 3, BW], BF16) for i in range(2)]; o += 6144
        ta = [sb.at(o + i * 2048, [128, BW], F32) for i in range(2)]; o += 4096
        tb = [sb.at(o + i * 2048, [128, BW], F32) for i in range(2)]; o += 4096
        rc = [sb.at(o + i * 2048, [128, BW], F32) for i in range(2)]; o += 4096
        rs = [sb.at(o + i * 2048, [128, BW], F32) for i in range(2)]; o += 4096
        perm32 = sb.at(o, [128, 32], F32); o += 128
        perm96 = sb.at(o, [128, 96], F32); o += 384
        assert o <= R_BIG + 114688
        wuq = sb.at(R_W, [128, 3, 1536], BF16)
        wukv = sb.at(R_W + 9216, [128, 2, 2048], BF16)
        P.add("pool", lambda e: e.dma_start(out=win[:], in_=cwin_d.rearrange("(k p) n -> p k n", p=128)),
              writes=["win"], dma=True)
        P.add("pool", lambda e: e.dma_start(out=wuq[:], in_=cwuq_d.rearrange("(k p) n -> p k n", p=128)),
              writes=["wuq"], dma=True)
        P.add("pool", lambda e: e.dma_start(out=wukv[:], in_=cwukv_d.rearrange("(k p) n -> p k n", p=128)),
              writes=["wukv"], dma=True)
        P.add("sp", lambda e: e.dma_start(out=perm32[:], in_=consts_d[:, 256:288]), writes=["perm32"], dma=True)
        P.add("sp", lambda e: e.dma_start(out=perm96[:], in_=consts_d[:, 128:224]), writes=["perm96"], dma=True)
        cnt = 0
        for blk in range(NB):
            bs = slice(blk * BW, (blk + 1) * BW)
            s1 = blk % 2
            P.add("sp", lambda e, s1=s1, blk=blk: e.dma_start(
                out=rc[s1][0:32, :], in_=ropeC_d[0:32, 2 * T + blk * BW:2 * T + (blk + 1) * BW]),
                writes=[("rc", s1)], dma=True)
            P.add("sp", lambda e, s1=s1, blk=blk: e.dma_start(
                out=rs[s1][0:32, :], in_=ropeC_d[0:32, 3 * T + blk * BW:3 * T + (blk + 1) * BW]),
                writes=[("rs", s1)], dma=True)
            for (nch, c0, g0, dst, dname) in ((3, 0, 0, cqn, "cqn"), (2, 384, 3, ckvo, "ckvo")):
                s2 = cnt % 2
                cnt += 1
                for i in range(nch):
                    bank = nbank()
                    mm_group(bank, ps[bank][:], [(win[:, k, c0 + i * 128:c0 + (i + 1) * 128], yT[:, k, bs]) for k in range(KC)],
                             reads=[("yT", blk, k) for k in range(KC)] + ["win"])
                    P.add("act", lambda e, s2=s2, i=i, bank=bank: e.activation(out=cf[s2][:, i, :], in_=ps[bank][:], func=AF.Copy),
                          reads=[("ps", bank)], writes=[("cf", s2, i)])
                    P.add("act", lambda e, s2=s2, i=i, bank=bank: e.activation(out=sq[s2][:, i, :], in_=ps[bank][:], func=AF.Square),
                          reads=[("ps", bank)], writes=[("csq", s2, i)])
                slot = rms_rstd([sq[s2][:, i, :] for i in range(nch)], [("csq", s2, i) for i in range(nch)], nch * 128)
                for i in range(nch):
                    P.add("dve", lambda e, s2=s2, i=i, slot=slot, dst=dst, g0=g0, bs=bs: e.scalar_tensor_tensor(
                        out=dst[:, i, bs], in0=cf[s2][:, i, :], scalar=cgains[:, g0 + i:g0 + i + 1], in1=rstd[slot][:],
                        op0=ALU.mult, op1=ALU.mult),
                        reads=[("cf", s2, i), ("rstd", slot), "cgains"], writes=[(dname, i, blk)])
            bank = nbank()
            mm_group(bank, ps[bank][0:32, :], [(win[:, k, 640:672], yT[:, k, bs]) for k in range(KC)],
                     reads=[("yT", blk, k) for k in range(KC)] + ["win"])
            P.add("act", lambda e, s1=s1, bank=bank: e.activation(out=ta[s1][0:32, :], in_=ps[bank][0:32, :], func=AF.Copy),
                  reads=[("ps", bank)], writes=[("ta", s1)])
            b2 = nbank()
            mm_group(b2, ps[b2][0:32, :], [(perm32[0:32, :], ta[s1][0:32, :])], reads=[("ta", s1), "perm32"])
            P.add("dve", lambda e, s1=s1, b2=b2: e.tensor_tensor(out=tb[s1][0:32, :], in0=ps[b2][0:32, :], in1=rs[s1][0:32, :], op=ALU.mult),
                  reads=[("ps", b2), ("rs", s1)], writes=[("tb", s1)])
            P.add("dve", lambda e, s1=s1: e.tensor_tensor(out=ta[s1][0:32, :], in0=ta[s1][0:32, :], in1=rc[s1][0:32, :], op=ALU.mult),
                  reads=[("rc", s1)], writes=[("ta", s1)])
            P.add("pool", lambda e, s1=s1, bs=bs: e.tensor_tensor(out=kro[0:32, bs], in0=ta[s1][0:32, :], in1=tb[s1][0:32, :], op=ALU.add),
                  reads=[("ta", s1), ("tb", s1)], writes=[("kro", blk)])
        if KSTOP == 'c1':
            P.barrier()
            wo_sb = sb.at(R_W, [128, KC, D], BF16)
            P.add("pool", lambda e: e.dma_start(out=wo_sb[:], in_=cwo_d.rearrange("(k p) n -> p k n", p=128)),
                  writes=["wo"], dma=True)
            scr, _ = epi_scratch(R_BIG)
            out_proj_epilogue(wo_sb, layer, 1, next_pre, final, scr)
            return
        P.add("sp", lambda e: e.dma_start(out=payBK.ap().rearrange("(i p) t -> p i t", p=128), in_=ckvo[:]),
              reads=[("ckvo", i, blk) for i in range(2) for blk in range(NB)], writes=["payBK"], dma=True)
        P.add("sp", lambda e: e.dma_start(out=payBV.ap().rearrange("(i p) t -> p i t", p=128)[:, 0, :], in_=kro[:]),
              reads=[("kro", blk) for blk in range(NB)], writes=["payBV"], dma=True)
        P.add("pool", lambda e: e.collective_compute("AllGather", ALU.bypass, replica_groups=RG,
                                                     ins=[payBK.ap().opt()], outs=[gBK.ap().opt()]),
              reads=["payBK"], writes=["gBK"], cc=True)
        P.add("pool", lambda e: e.collective_compute("AllGather", ALU.bypass, replica_groups=RG,
                                                     ins=[payBV.ap().opt()], outs=[gBV.ap().opt()]),
              reads=["payBV"], writes=["gBV"], cc=True)
        P.barrier()
        ckvA = sb.at(R_Y, [128, 2, S], BF16)
        o = o_keep
        Kh = [sb.at(o + i * 16384, [128, S], BF16) for i in range(2)]; o += 32768
        Vh = [sb.at(o + i * 8320, [128, 64, 65], BF16) for i in range(2)]; o += 16640
        qh = [sb.at(o + i * 4096, [128, T], BF16) for i in range(2)]; o += 8192
        pT = [sb.at(o + i * 1024, [128, BW], BF16) for i in range(6)]; o += 6144
        tq = sb.at(o, [128, 2, T], F32); o += 16384
        qf = [sb.at(o + i * 2048, [128, BW], F32) for i in range(2)]; o += 4096
        tqa = [sb.at(o + i * 2048, [128, BW], F32) for i in range(2)]; o += 4096
        tqb = [sb.at(o + i * 2048, [128, BW], F32) for i in range(2)]; o += 4096
        rzr = [sb.at(o + i * 2048, [128, BW], F32) for i in range(2)]; o += 4096
        zs = [sb.at(o, [128, BW], F32) for i in range(2)]; o += 2048
        perm96b = sb.at(o, [128, 96], F32); o += 384
        assert o <= R_BIG + 114688, o - R_BIG
        P.add("sp", lambda e: e.dma_start(out=perm96b[:], in_=consts_d[:, 128:224]), writes=["perm96"], dma=True)
        P.add("sp", lambda e: e.dma_start(out=tq[:], in_=ropeC_d[:, 0:2 * T].rearrange("p (a t) -> p a t", a=2)),
              writes=["tq"], dma=True)
        for r in range(4):
            P.add("sp", lambda e, r=r: e.dma_start(
                out=ckvA[:, :, r * T:(r + 1) * T], in_=gBK.ap()[r * 256:(r + 1) * 256, :].rearrange("(i p) t -> p i t", p=128)),
                writes=[("ckvA", r)], dma=True)
            for b in range(2):
                P.add("sp", lambda e, r=r, b=b: e.dma_start(
                    out=Kh[b][64:96, r * T:(r + 1) * T], in_=gBV.ap()[r * 256:r * 256 + 32, :]),
                    writes=[("Khr", b, r)], dma=True)
        for b in range(2):
            P.add("pool", lambda e, b=b: e.memset(Vh[b][:], 1.0), writes=[("Vh1", b), ("Vh", b)])
        ckv_reads = [("ckvA", r) for r in range(4)]
        prep_cnt = [0]

        def prep(h):
            b = h % 2
            for blk in range(NB):
                bs = slice(blk * BW, (blk + 1) * BW)
                s2 = prep_cnt[0] % 2
                prep_cnt[0] += 1
                bank = nbank(6, 8)
                mm_group(bank, ps[bank][0:96, :], [(wuq[:, i, h * 96:(h + 1) * 96], cqn[:, i, bs]) for i in range(3)],
                         reads=["wuq"])
                P.add("dve", lambda e, s2=s2, bank=bank: e.tensor_copy(out=qf[s2][0:96, :], in_=ps[bank][0:96, :]),
                      reads=[("ps", bank)], writes=[("qf", s2)])
                b2 = nbank(6, 8)
                mm_group(b2, ps[b2][0:96, :], [(perm96b[0:96, :], qf[s2][0:96, :])], reads=[("qf", s2), "perm96"])
                P.add("dve", lambda e, s2=s2, bs=bs: e.tensor_tensor(out=tqa[s2][0:96, :], in0=qf[s2][0:96, :], in1=tq[0:96, 0, bs], op=ALU.mult),
                      reads=[("qf", s2), "tq"], writes=[("tqa", s2)])
                P.add("dve", lambda e, s2=s2, bs=bs, b2=b2: e.tensor_tensor(out=tqb[s2][0:96, :], in0=ps[b2][0:96, :], in1=tq[0:96, 1, bs], op=ALU.mult),
                      reads=[("ps", b2), "tq"], writes=[("tqb", s2)])
                P.add("pool", lambda e, s2=s2, bs=bs, b=b: e.tensor_tensor(out=qh[b][0:96, bs], in0=tqa[s2][0:96, :], in1=tqb[s2][0:96, :], op=ALU.add),
                      reads=[("tqa", s2), ("tqb", s2)], writes=[("qh", b, blk)])
                yield
            for kb in range(16):
                ks = slice(kb * BW, (kb + 1) * BW)
                bank = nbank(6, 8)
                mm_group(bank, ps[bank][0:64, :], [(wukv[:, i, h * 128:h * 128 + 64], ckvA[:, i, ks]) for i in range(2)],
                         reads=["wukv"] + ckv_reads)
                P.add("dve", lambda e, b=b, ks=ks, bank=bank: e.tensor_copy(out=Kh[b][0:64, ks], in_=ps[bank][0:64, :]),
                      reads=[("ps", bank)], writes=[("Kh", b)])
                yield
            for g8 in range(8):
                bank = nbank(6, 8)
                fns = []
                for u in range(8):
                    tl = g8 * 8 + u
                    for i in range(2):
                        fns.append(lambda e, u=u, tl=tl, i=i, bank=bank: e.matmul(
                            ps[bank][:, u * 64:(u + 1) * 64], ckvA[:, i, tl * 128:(tl + 1) * 128],
                            wukv[:, i, h * 128 + 64:h * 128 + 128], start=(i == 0), stop=(i == 1)))
                P.add("pe", fns, reads=["wukv"] + ckv_reads, writes=[("ps", bank)])
                P.add("dve", lambda e, b=b, g8=g8, bank=bank: e.tensor_copy(
                    out=Vh[b][:, g8 * 8:(g8 + 1) * 8, 0:64], in_=ps[bank][:].rearrange("p (u n) -> p u n", u=8)),
                    reads=[("ps", bank)], writes=[("Vh", b)])
                yield

        it = 0

        def cpull(gen, n=1):
            if gen is None:
                return
            for _ in range(n):
                try:
                    next(gen)
                except StopIteration:
                    return

        cpend = []

        def cnorm(zi, ob, h, qs, qb):
            pb = (h % 2) * 64

            def ph_a():
                P.add("dve", lambda e: e.tensor_copy(out=rzr[zi][64:65, :], in_=ps[ob][64:65, :]),
                      reads=[("ps", ob)], writes=[("rzr", zi)])
                P.add("dve", lambda e: e.reciprocal(out=rzr[zi][64:65, :], in_=rzr[zi][64:65, :]),
                      reads=[("rzr", zi)], writes=[("rzr", zi)])
                bcast64_a(rzr[zi][64:65, :], ("rzr", zi))

            def ph_b():
                zb = nbank(6, 8)
                bcast64_b(zb)
                P.add("dve", lambda e: e.tensor_copy(out=zs[zi][0:64, :], in_=ps[zb][0:64, :]),
                      reads=[("ps", zb)], writes=[("zs", 0)])
                P.add("dve", lambda e: e.tensor_tensor(
                    out=aoT[pb:pb + 64, h // 2, qs], in0=ps[ob][0:64, :], in1=zs[zi][0:64, :], op=ALU.mult),
                    reads=[("ps", ob), ("zs", 0)], writes=[("ao", qb, h // 2, h % 2)])
            return [ph_a, ph_b]

        def crun():
            if not cpend:
                return
            th = cpend[0]
            th.pop(0)()
            if not th:
                cpend.pop(0)

        def chook(t, cgen):
            if t % 8 == 4:
                cpull(cgen, 1)
            if t in (6, 14):
                crun()

        if KSTOP != 'c2':
            cpull(prep(0), 1000)
        for h in range(16):
            if KSTOP in ('c2', 'c3'):
                break
            b = h % 2
            cgen = prep(h + 1) if h + 1 < 16 else None
            for qb in range(NB):
                qs = slice(qb * BW, (qb + 1) * BW)
                ob = 4 + (it % 2)
                it += 1
                attn_core(
                    64,
                    lambda t, b=b, qs=qs: (Kh[b][0:96, t * 128:(t + 1) * 128], qh[b][0:96, qs]),
                    [(ob, ps[ob][0:65, :], lambda t, b=b: Vh[b][:, t, :])],
                    float(96 ** -0.5), pT, [("Kh", b), ("Vh", b), ("Vh1", b)] + [("Khr", b, r) for r in range(4)],
                    [("qh", b, qb)], [ob], hook=(lambda t, cgen=cgen: chook(t, cgen)))
                cpend.append(cnorm(it % 2, ob, h, qs, qb))
                if qb == NB - 1:
                    cpull(cgen, 1000)
        while cpend:
            crun()
        P.barrier()
        wo_sb = sb.at(R_W, [128, KC, D], BF16)
        P.add("pool", lambda e: e.dma_start(out=wo_sb[:], in_=cwo_d.rearrange("(k p) n -> p k n", p=128)),
              writes=["wo"], dma=True)
        scr, _ = epi_scratch(R_BIG)
        out_proj_epilogue(wo_sb, layer, 1, next_pre, final, scr)

    mixers[2] = mixer_c

    def mixer_a(layer, next_pre, final):
        la = layer // 3
        P.barrier()
        P.barrier()
        o = R_BIG
        yw = sb.at(o, [128, KC, 4096], BF16); o += 65536
        qs_ = [sb.at(o + i * 4096, [128, T], BF16) for i in range(2)]; o += 8192
        kb_ = [sb.at(o + i * 8192, [128, 4096], BF16) for i in range(2)]; o += 16384
        vb_ = [sb.at(o + i * 8320, [128, 32, 2, 65], BF16) for i in range(2)]; o += 16640
        tmp = [sb.at(o + i * 1024, [128, 256], F32) for i in range(4)]; o += 4096
        ptl = [sb.at(o + i * 512, [128, 256], BF16) for i in range(5)]; o += 2560
        dist = sb.at(o, [128, 256], F32); o += 1024
        assert o <= R_BIG + 114688, o - R_BIG
        o = R_Y
        acc = sb.at(o, [128, 2, T], F32); o += 16384
        wsl = [sb.at(o + i * 6144, [128, 3, KC, 128], BF16) for i in range(2)]; o += 12288
        rzr = [sb.at(o + i * 2048, [128, BW], F32) for i in range(2)]; o += 4096
        assert o <= R_Y + 32768
        wo_sb = sb.at(R_W, [128, KC, D], BF16)
        P.add("pool", lambda e: e.dma_start(out=wo_sb[:], in_=awo_d[la * D:(la + 1) * D, :].rearrange("(k p) n -> p k n", p=128)),
              writes=["wo"], dma=True)
        P.add("sp", lambda e: e.dma_start(out=dist[:], in_=consts_d[:, 384:640]), writes=["dist"], dma=True)

        def wl(part, blk, o0):
            def f(e):
                if not sv:
                    pid = nc.partition_id()
                    g = pid % 4
                    sv[0] = ((g + 3) % 4) * D
                    sv[1] = g * D
                    sv[2] = ((g + 1) % 4) * D
                return e.dma_start(out=yw[:, :, o0:o0 + BW],
                                   in_=gA4[blk].ap()[bass.ds(sv[part], D), :].rearrange("(k p) t -> p k t", p=128))
            return f

        wl_list = [(1, b_, 1024 + b_ * BW) for b_ in range(4)] + [(0, 2, 0), (0, 3, BW), (2, 0, 3072), (2, 1, 3072 + BW)]
        for (part, b_, o0) in wl_list:
            P.add("pool", wl(part, b_, o0), writes=[("yw", part, b_)], dma=True)
        yw_reads = [("yw", part, b_) for (part, b_, o0) in wl_list]
        steps = [(c, gi) for c in range(8) for gi in range(3)]
        vcol0 = [0, 17, 37]
        evac_rr = [0]

        def evac(out_ap, in_ap, reads, writes, scale=None):
            use_act = (evac_rr[0] % 2 == 0)
            evac_rr[0] += 1
            if use_act:
                if scale is None:
                    P.add("act", lambda e: e.activation(out=out_ap, in_=in_ap, func=AF.Copy), reads=reads, writes=writes)
                else:
                    P.add("act", lambda e: e.activation(out=out_ap, in_=in_ap, func=AF.Copy, scale=scale), reads=reads, writes=writes)
            else:
                if scale is None:
                    P.add("dve", lambda e: e.tensor_copy(out=out_ap, in_=in_ap), reads=reads, writes=writes)
                else:
                    P.add("dve", lambda e: e.tensor_scalar(out=out_ap, in0=in_ap, scalar1=scale, scalar2=None, op0=ALU.mult),
                          reads=reads, writes=writes)

        def proj(si):
            c, gi = steps[si]
            st = si % 2
            d = A_GROUPS[gi][1]
            nqt = 16 // d
            w = wsl[st]
            for s3 in range(3):
                r0 = ((((la * 3 + s3) * 3 + gi) * 8) + c) * 128
                P.add("pool", lambda e, s3=s3, r0=r0: e.dma_start(
                    out=w[:, s3, :, :], in_=awqkv_d[r0:r0 + 128, :].rearrange("p (k n) -> p k n", k=KC)),
                    writes=[("aw", st, s3)], dma=True)
            for blk in range(NB):
                bank = nbank(6, 8)
                mm_group(bank, ps[bank][:], [(w[:, 0, k, :], yw[:, k, 1024 + blk * BW:1024 + (blk + 1) * BW]) for k in range(KC)],
                         reads=yw_reads + [("aw", st, 0)])
                evac(qs_[st][:, blk * BW:(blk + 1) * BW], ps[bank][:], [("ps", bank)], [("aq", st)], scale=0.125)
                yield
            w0 = 1024 - 64 * d
            Lk = 2048 + 128 * d
            j0 = 0
            while j0 < Lk:
                wd_ = min(BW, Lk - j0)
                bank = nbank(6, 8)
                mm_group(bank, ps[bank][:, 0:wd_], [(w[:, 1, k, :], yw[:, k, w0 + j0:w0 + j0 + wd_]) for k in range(KC)],
                         reads=yw_reads + [("aw", st, 1)])
                evac(kb_[st][:, j0:j0 + wd_], ps[bank][:, 0:wd_], [("ps", bank)], [("ak", st)])
                j0 += wd_
                yield
            ntile = d * (nqt + 1)
            for vt in range(ntile):
                r, m_ = vt // (nqt + 1), vt % (nqt + 1)
                ws = 1024 + r + d * (128 * m_ - 64)
                bank = nbank(6, 8)
                mm_group(bank, ps[bank][:, 0:128],
                         [(yw[:, k, ws:ws + 127 * d + 1:d], w[:, 2, k, :]) for k in range(KC)],
                         reads=yw_reads + [("aw", st, 2)])
                vc = vcol0[gi] + vt
                outv = vb_[st][:, vt, :, 0:64]
                inv = ps[bank][:, 0:128].rearrange("p (e n) -> p e n", e=2)
                if vt % 2 == 0:
                    P.add("act", lambda e, outv=outv, inv=inv, vc=vc: e.activation(
                        out=outv, in_=inv, func=AF.Copy, scale=validA[:, vc:vc + 1]),
                        reads=[("ps", bank), "validA"], writes=[("av", st)])
                else:
                    P.add("dve", lambda e, outv=outv, inv=inv, vc=vc: e.tensor_scalar(
                        out=outv, in0=inv, scalar1=validA[:, vc:vc + 1], scalar2=None, op0=ALU.mult),
                        reads=[("ps", bank), "validA"], writes=[("av", st)])
                yield
            for e2 in range(2):
                P.add("dve", lambda e, e2=e2, ntile=ntile, gi=gi: e.tensor_copy(
                    out=vb_[st][:, 0:ntile, e2, 64], in_=validA[:, vcol0[gi]:vcol0[gi] + ntile]),
                    reads=["validA"], writes=[("av1", st, e2)])

        pt_i = [0]
        ob_i = [0]

        def pull(gen, n=1):
            if gen is None:
                return
            for _ in range(n):
                try:
                    next(gen)
                except StopIteration:
                    return

        def attn(si, gen=None):
            c, gi = steps[si]
            st = si % 2
            d = A_GROUPS[gi][1]
            nqt = 16 // d
            tiles = [(r, j) for r in range(d) for j in range(nqt)]
            seq = [(e2, b4, u) for e2 in range(2) for b4 in range(4) for u in range(4)]
            info = {}

            def emit_qk(ix):
                e2, b4, u = seq[ix]
                pb = e2 * 64
                slope = alibi_slope(gi, 2 * c + e2) * d
                r, j = tiles[b4 * 4 + u]
                qc0 = r + d * 128 * j
                bk, half = pt_i[0] % 4, 0
                i = pt_i[0] % 4
                i5 = pt_i[0] % 5
                pt_i[0] += 1
                info[ix] = i5
                sview = ps[bk][:, half * 256:(half + 1) * 256]
                fns = []
                for piece in range(2):
                    kc0 = r + 128 * d * (j + piece)
                    fns.append(lambda e, piece=piece, kc0=kc0: e.matmul(
                        ps[bk][:, half * 256 + piece * 128:half * 256 + (piece + 1) * 128],
                        kb_[st][pb:pb + 64, kc0:kc0 + 127 * d + 1:d], qs_[st][pb:pb + 64, qc0:qc0 + 127 * d + 1:d],
                        start=True, stop=True))
                P.add("pe", fns, reads=[("aq", st), ("ak", st)], writes=[("psh", bk, half)])
                P.add("dve", lambda e: e.scalar_tensor_tensor(
                    out=tmp[i][:], in0=dist[:], scalar=float(slope), in1=sview, op0=ALU.mult, op1=ALU.add),
                    reads=[("psh", bk, half), "dist"], writes=[("atmp", i)])
                P.add("act", lambda e: e.activation(out=ptl[i5][:], in_=tmp[i][:], func=AF.Exp),
                      reads=[("atmp", i)], writes=[("apt", i5)])

            def emit_pv(ix):
                e2, b4, u = seq[ix]
                i5 = info[ix]
                r, j = tiles[b4 * 4 + u]
                if u == 0:
                    ob_i[0] += 1
                ob = 4 + ob_i[0] % 2
                vt0 = r * (nqt + 1) + j
                fns = []
                for piece in range(2):
                    fns.append(lambda e, piece=piece: e.matmul(
                        ps[ob][0:65, u * 128:(u + 1) * 128], vb_[st][:, vt0 + piece, e2, :],
                        ptl[i5][:, piece * 128:(piece + 1) * 128], start=(piece == 0), stop=(piece == 1)))
                P.add("pe", fns, reads=[("apt", i5), ("av", st), ("av1", st, e2)], writes=[("ps", ob)])
                if u == 3:
                    if d == 1:
                        av = acc[0:65, e2, b4 * 512:(b4 + 1) * 512]
                        pv_ = ps[ob][0:65, :]
                    elif d == 4:
                        av = acc[0:65, e2, b4:T:4]
                        pv_ = ps[ob][0:65, :]
                    else:
                        av = acc[0:65, e2, :].rearrange("p (i dd) -> p dd i", dd=16)[:, b4 * 4:(b4 + 1) * 4, :]
                        pv_ = ps[ob][0:65, :].rearrange("p (u i) -> p u i", u=4)
                    if gi == 0:
                        P.add("dve", lambda e: e.tensor_copy(out=av, in_=pv_),
                              reads=[("ps", ob)], writes=[("acc", e2)])
                    else:
                        P.add("dve", lambda e: e.tensor_tensor(out=av, in0=av, in1=pv_, op=ALU.add),
                              reads=[("ps", ob)], writes=[("acc", e2)])

            LA = 4
            for ix in range(len(seq) + LA):
                if ix < len(seq):
                    emit_qk(ix)
                if ix >= LA:
                    emit_pv(ix - LA)
                pull(gen, 1)
                if ix < 16:
                    run_pending(1)
            pull(gen, 1000)
            if gi == 2:
                for e2 in range(2):
                    for blk in range(NB):
                        pending.append(norm_thunks(c, e2, blk))

        def norm_thunks(c, e2, blk):
            pb = e2 * 64
            bs = slice(blk * BW, (blk + 1) * BW)
            zi = (e2 * NB + blk) % 2

            def ph_a():
                P.add("dve", lambda e: e.reciprocal(out=rzr[zi][64:65, :], in_=acc[64:65, e2, bs]),
                      reads=[("acc", e2)], writes=[("rzr", zi)])
                bcast64_a(rzr[zi][64:65, :], ("rzr", zi))

            def ph_b():
                zb = nbank(6, 8)
                bcast64_b(zb)
                P.add("dve", lambda e: e.tensor_tensor(
                    out=aoT[pb:pb + 64, c, bs], in0=acc[0:64, e2, bs], in1=ps[zb][0:64, :], op=ALU.mult),
                    reads=[("ps", zb), ("acc", e2)], writes=[("ao", blk, c, e2)])
            return [ph_a, ph_b]

        pending = []

        def run_pending(n):
            for _ in range(n):
                if not pending:
                    return
                th = pending[0]
                th.pop(0)()
                if not th:
                    pending.pop(0)

        nsteps = len(steps)
        if KSTOP == 'a1':
            nsteps = 0
        elif KSTOP in ('a2', 'a3'):
            nsteps = 1
        if nsteps:
            pull(proj(0), 1000)
        for si in range(nsteps):
            gen = proj(si + 1) if si + 1 < nsteps else None
            if KSTOP != 'a2':
                attn(si, gen)
            else:
                pull(gen, 1000)
        run_pending(1000)
        P.barrier()
        if KSTOP == 'adbg':
            P.add("pool", lambda e: e.dma_start(out=out3[:, :, :], in_=aoT[:]), writes=["dbg"], dma=True)
            return
        scr, _ = epi_scratch(R_BIG)
        out_proj_epilogue(wo_sb, layer, 1, next_pre, final, scr)

    mixers[0] = mixer_a

    from_x = True
    nl = len(layers)
    for li, layer in enumerate(layers):
        kind = layer % 3
        last = (li == nl - 1)
        if li == 0:
            P.barrier()
            hbl = [sb.at(R_BIG + i * 16384, [128, KC, BW], F32) for i in range(2)]
            sqb = [sb.at(R_BIG + 32768 + i * 8192, [128, KC, BW], BF16) for i in range(2)]
            first_kind = 2 if skip_mixer else 0
            for blk in range(NB):
                prologue_only(blk, hbl[blk % 2], sqb[blk % 2], layer, first_kind, xT3, True)
        if not skip_mixer:
            mixers[kind](layer, (layer, 2) if not skip_ffn else (None if last else (layers[li + 1], 0)),
                         final=(last and skip_ffn))
        if not skip_ffn:
            nxt = None if last else (layers[li + 1], 2 if skip_mixer else 0)
            ffn(layer, nxt, final=last)

    P.barrier()

    with nc.Block() as block:
        P.replay(block)
    nc.used_inputs = used_inputs
    return nc


mixers = {}


def _prep_shared(inp):
    f = np.float32
    sh = {}
    g = np.zeros((128, 128), f)
    for kind, name in enumerate(["norm_mix_pre", "norm_mix_post", "norm_ffn_pre", "norm_ffn_post"]):
        a = np.asarray(inp[name], f)
        for l in range(DEPTH):
            g[:, (kind * 4 + l) * 8:(kind * 4 + l) * 8 + 8] = a[l].reshape(8, 128).T
    sh["gains"] = g
    wg = np.asarray(inp["ffn_wg"], f); wu = np.asarray(inp["ffn_wu"], f); wd = np.asarray(inp["ffn_wd"], f)
    sh["wg"] = np.ascontiguousarray(wg.reshape(DEPTH, 8, 128, JC, 128).transpose(0, 3, 2, 1, 4)).reshape(DEPTH * JC * 128, 1024)
    sh["wu"] = np.ascontiguousarray(wu.reshape(DEPTH, 8, 128, JC, 128).transpose(0, 3, 2, 1, 4)).reshape(DEPTH * JC * 128, 1024)
    sh["wd"] = np.ascontiguousarray(wd.reshape(DEPTH, 2, 11, 128, 8, 128).transpose(0, 1, 4, 3, 2, 5)).reshape(DEPTH * 2 * 8 * 128, 11 * 128)
    aw = np.asarray(inp["a_wqkv"], f)
    sh["a_wqkv"] = np.ascontiguousarray(aw.reshape(2, 8, 128, 3, 3, 8, 128).transpose(0, 3, 4, 5, 2, 1, 6)).reshape(2 * 3 * 3 * 8 * 128, 1024)
    sh["a_wo"] = np.ascontiguousarray(np.asarray(inp["a_wo"], f).reshape(2 * D, D))
    sh["b_wqkv"] = np.ascontiguousarray(np.asarray(inp["b_wqkv"], f)[0])
    sh["b_wo"] = np.ascontiguousarray(np.asarray(inp["b_wo"], f)[0])
    sh["c_win"] = np.ascontiguousarray(np.asarray(inp["c_win"], f)[0])
    sh["c_wuq"] = np.ascontiguousarray(np.asarray(inp["c_wuq"], f)[0])
    sh["c_wukv"] = np.ascontiguousarray(np.asarray(inp["c_wukv"], f)[0])
    sh["c_wo"] = np.ascontiguousarray(np.asarray(inp["c_wo"], f)[0])
    bg = np.zeros((128, 8), f)
    qn = np.asarray(inp["b_qnorm"], f)[0]; kn = np.asarray(inp["b_knorm"], f)[0]
    partner = np.arange(128); dd = partner % 64
    partner = np.where(dd < 32, partner + 32, partner - 32)
    bg[:, 0] = qn; bg[:, 1] = qn[partner]; bg[:, 2] = kn; bg[:, 3] = kn[partner]
    sh["bgains"] = bg
    cg = np.zeros((128, 8), f)
    cq = np.asarray(inp["c_qnorm"], f)[0]; ck = np.asarray(inp["c_kvnorm"], f)[0]
    for i in range(3):
        cg[:, i] = cq[i * 128:(i + 1) * 128]
    for i in range(2):
        cg[:, 3 + i] = ck[i * 128:(i + 1) * 128]
    sh["cgains"] = cg
    c = np.zeros((128, 1024), f)
    for mcol in range(128):
        c[partner[mcol], mcol] = 1.0
    p96 = np.arange(96)
    p96 = np.where(p96 < 64, p96, np.where(p96 < 80, p96 + 16, p96 - 16))
    for mcol in range(96):
        c[p96[mcol], 128 + mcol] = 1.0
    p32 = np.arange(32); p32 = np.where(p32 < 16, p32 + 16, p32 - 16)
    for mcol in range(32):
        c[p32[mcol], 256 + mcol] = 1.0
    pk = np.arange(128)[:, None]; qi = np.arange(128)[None, :]
    for piece, off in enumerate((-64, 64)):
        diff = pk + off - qi
        c[:, 384 + piece * 128:384 + (piece + 1) * 128] = np.where(np.abs(diff) <= 64, -np.abs(diff), -1e9)
    sh["consts"] = c
    return sh


def _prep_core(inp, c):
    f = np.float32
    b, g = c // 4, c % 4
    T0 = g * T
    per = {}
    per["xT"] = np.ascontiguousarray(np.asarray(inp["x"], f)[b, T0:T0 + T, :].T)
    s = (T0 + np.arange(T)).astype(f)
    d = np.arange(128); dd = d % 64; fi = dd % 32
    freq = np.power(f(10000.0), -(fi.astype(f)) * f(2.0) / f(64.0)).astype(f)
    row = np.floor(s / 64.0).astype(f); col = (s - row * 64).astype(f)
    pos = np.where((d < 64)[:, None], row[None, :], col[None, :]).astype(f)
    ang = (pos * freq[:, None]).astype(f)
    cb = np.cos(ang).astype(f); sn = np.sin(ang).astype(f)
    sb_ = np.where((dd < 32)[:, None], -sn, sn).astype(f)
    per["ropeB"] = np.ascontiguousarray(np.concatenate([cb, sb_], axis=1))
    fi16 = (np.arange(32) % 16).astype(f)
    fr = np.power(f(10000.0), -fi16 * f(2.0) / f(32.0)).astype(f)
    ang = (s[None, :] * fr[:, None]).astype(f)
    cc = np.cos(ang).astype(f); ss = np.sin(ang).astype(f)
    ss = np.where((np.arange(32) < 16)[:, None], -ss, ss).astype(f)
    tc_ = np.zeros((128, 4 * T), f)
    tc_[0:64, 0:T] = 1.0
    tc_[64:96, 0:T] = cc; tc_[64:96, T:2 * T] = ss
    tc_[0:32, 2 * T:3 * T] = cc; tc_[0:32, 3 * T:] = ss
    per["ropeC"] = tc_
    va = np.zeros((128, 72), f)
    col0 = 0
    for (window, dil) in A_GROUPS:
        nqt = 16 // dil
        for r in range(dil):
            for mm in range(nqt + 1):
                idx = 128 * mm - 64 + np.arange(128)
                tok = T0 + r + dil * idx
                va[:, col0 + r * (nqt + 1) + mm] = ((tok >= 0) & (tok < S)).astype(f)
        col0 += dil * (nqt + 1)
    per["validA"] = va
    return per


_CACHE = {}


def kernel(**inputs):
    key = "full"
    if key not in _CACHE:
        _CACHE[key] = build_program()
    nc = _CACHE[key]
    sh = _prep_shared(inputs)
    in_maps = []
    for c in range(NCORES):
        mp = dict(sh)
        mp.update(_prep_core(inputs, c))
        in_maps.append({k: mp[k] for k in nc.used_inputs})
    res = run_bass_kernel_spmd(nc, in_maps, core_ids=list(range(NCORES)))
    out = np.empty((2, S, D), np.float32)
    for c in range(NCORES):
        b, g = c // 4, c % 4
        out[b, g * T:(g + 1) * T, :] = res.results[c]["out"].T
    return out
```
